# Optimizing a Trainium2 kernel written in Bass

```python
import math
import jax, jax.numpy as jnp
from jax import lax
import numpy as np

D_MODEL = 1024
BATCH = 2
SEQ = 8192
DEPTH = 4

POOL_WINDOWS = (2, 4, 8, 16)
POOL_GROUP = D_MODEL // 8
POOL_WIDTH = POOL_GROUP * len(POOL_WINDOWS)
SSM_D_INNER = D_MODEL
SSM_HEAD_DIM = 64
SSM_HEADS = SSM_D_INNER // SSM_HEAD_DIM
SSM_GROUPS = 2
SSM_STATE = 128
SSM_CONV = 4
SSM_CHUNK = 128
SSM_CONV_DIM = SSM_D_INNER + 2 * SSM_GROUPS * SSM_STATE
EVEN_IN = POOL_WIDTH + SSM_D_INNER + SSM_CONV_DIM + SSM_HEADS
EVEN_MIX = POOL_WIDTH + SSM_D_INNER
FOX_HEADS = 8
FOX_HEAD_DIM = 64
FOX_WIDTH = FOX_HEADS * FOX_HEAD_DIM
MLA_HEADS = 8
MLA_NOPE = 64
MLA_ROPE = 32
MLA_V = 64
MLA_Q_RANK = 512
MLA_KV_RANK = 256
ROPE_THETA = 10000.0
ODD_IN = 3 * FOX_WIDTH + FOX_HEADS + MLA_Q_RANK + MLA_KV_RANK + MLA_ROPE
ODD_MIX = FOX_WIDTH + MLA_HEADS * MLA_V
ATTN_BLOCK = 128
D_FF = -(-8 * D_MODEL // (3 * 256)) * 256
DEEPNORM_ALPHA = (2 * DEPTH) ** 0.25
DEEPNORM_BETA = (8 * DEPTH) ** -0.25
N_EVEN = (DEPTH + 1) // 2
N_ODD = DEPTH // 2
LN_EPS = 1e-5
RMS_EPS = 1e-6

kernel_name = "hybrid_pool_ssd_fox_mla_deepnorm"

F32 = jnp.float32


def layer_norm(x, g, b):
    xf = x.astype(F32)
    mu = jnp.mean(xf, axis=-1, keepdims=True)
    var = jnp.mean(jnp.square(xf - mu), axis=-1, keepdims=True)
    return ((xf - mu) * lax.rsqrt(var + LN_EPS) * g + b).astype(x.dtype)


def rms_norm(x, g):
    xf = x.astype(F32)
    return (xf * lax.rsqrt(jnp.mean(jnp.square(xf), axis=-1, keepdims=True) + RMS_EPS) * g).astype(x.dtype)


def rope(x, pos):
    half = x.shape[-1] // 2
    freqs = jnp.power(ROPE_THETA, -jnp.arange(half, dtype=F32) / half)
    ang = pos.astype(F32)[:, None] * freqs[None, :]
    cos, sin = jnp.cos(ang), jnp.sin(ang)
    x1 = x[..., :half].astype(F32)
    x2 = x[..., half:].astype(F32)
    return jnp.concatenate([x1 * cos - x2 * sin, x2 * cos + x1 * sin], axis=-1).astype(x.dtype)


def multiscale_pool(u, pool_w, pool_scale):
    b, s, _ = u.shape
    ng = len(POOL_WINDOWS)
    ug = u.reshape(b, s, ng, POOL_GROUP).astype(F32)
    cs = jnp.cumsum(ug, axis=1)
    pos = jnp.arange(s)
    means = []
    for g, w in enumerate(POOL_WINDOWS):
        c = cs[:, :, g]
        lagged = jnp.pad(c, ((0, 0), (w, 0), (0, 0)))[:, :s]
        count = jnp.minimum(pos + 1, w).astype(F32)[None, :, None]
        means.append((c - lagged) / count)
    diff = (jnp.stack(means, axis=2) - ug).astype(u.dtype)
    y = jnp.einsum('bsgc,gcd->bsgd', diff, pool_w).reshape(b, s, POOL_WIDTH)
    return y * pool_scale


def causal_dwconv(u, w, bias):
    c = u.shape[-1]
    out = lax.conv_general_dilated(
        u, w[:, None, :].astype(u.dtype), window_strides=(1,),
        padding=[(SSM_CONV - 1, 0)], dimension_numbers=('NWC', 'WIO', 'NWC'),
        feature_group_count=c)
    return out + bias


def mamba2_ssd(z, xbc, dt, conv_w, conv_b, dt_bias, a_log, d_skip, norm_w):
    b, s, _ = z.shape
    G, E, P, N, L = SSM_GROUPS, SSM_HEADS // SSM_GROUPS, SSM_HEAD_DIM, SSM_STATE, SSM_CHUNK
    nc = s // L
    xbc = jax.nn.silu(causal_dwconv(xbc, conv_w, conv_b))
    xs, bm, cm = jnp.split(xbc, [SSM_D_INNER, SSM_D_INNER + G * N], axis=-1)
    dt = jax.nn.softplus(dt.astype(F32) + dt_bias)
    a = -jnp.exp(a_log.astype(F32))
    xh = xs.reshape(b, nc, L, G, E, P).astype(F32)
    dtc = dt.reshape(b, nc, L, G, E)
    xdt = xh * dtc[..., None]
    da = dtc * a.reshape(G, E)
    bc = bm.reshape(b, nc, L, G, N).astype(F32)
    cc = cm.reshape(b, nc, L, G, N).astype(F32)
    acs = jnp.cumsum(da, axis=2).transpose(0, 1, 3, 4, 2)
    seg = acs[..., :, None] - acs[..., None, :]
    tri = jnp.tril(jnp.ones((L, L), dtype=bool))
    lmat = jnp.exp(jnp.where(tri, seg, -jnp.inf))
    cb = jnp.einsum('bclgn,bcsgn->bcgls', cc, bc)
    y_diag = jnp.einsum('bcgls,bcgels,bcsgep->bclgep', cb, lmat, xdt)
    decay_states = jnp.exp(acs[..., -1:] - acs)
    states = jnp.einsum('bclgn,bcgel,bclgep->bcgepn', bc, decay_states, xdt)
    chunk_decay = jnp.exp(acs[..., -1])

    def step(h, inp):
        st, dec = inp
        return h * dec[..., None, None] + st, h

    h0 = jnp.zeros((b, G, E, P, N), F32)
    _, prev = lax.scan(step, h0, (jnp.moveaxis(states, 1, 0), jnp.moveaxis(chunk_decay, 1, 0)))
    prev = jnp.moveaxis(prev, 0, 1)
    y_off = jnp.einsum('bclgn,bcgepn,bcgel->bclgep', cc, prev, jnp.exp(acs))
    y = y_diag + y_off + xh * d_skip.astype(F32).reshape(G, E)[:, :, None]
    y = y.reshape(b, s, SSM_D_INNER) * jax.nn.silu(z.astype(F32))
    return rms_norm(y, norm_w).astype(z.dtype)


def block_causal_attention(q, k, v, log_decay_cum):
    b, h, s, dk = q.shape
    nb = s // ATTN_BLOCK
    scale = dk ** -0.5
    qb = q.reshape(b, h, nb, ATTN_BLOCK, dk).transpose(2, 0, 1, 3, 4)
    fb = None if log_decay_cum is None else log_decay_cum.reshape(b, h, nb, ATTN_BLOCK).transpose(2, 0, 1, 3)
    kpos = jnp.arange(s)

    def one_block(args):
        qi, fi, i = args
        sc = jnp.einsum('bhqd,bhkd->bhqk', qi, k).astype(F32) * scale
        if fi is not None:
            sc = sc + fi[..., :, None] - log_decay_cum[:, :, None, :]
        qpos = i * ATTN_BLOCK + jnp.arange(ATTN_BLOCK)
        sc = jnp.where(kpos[None, :] <= qpos[:, None], sc, -jnp.inf)
        p = jax.nn.softmax(sc, axis=-1).astype(v.dtype)
        return jnp.einsum('bhqk,bhkd->bhqd', p, v)

    o = lax.map(one_block, (qb, fb, jnp.arange(nb)))
    return o.transpose(1, 0, 3, 2, 4).reshape(b, s, h * v.shape[-1])


def even_mixer(x, w_in, pool_w, pool_scale, conv_w, conv_b, dt_bias, a_log, d_skip, norm_w, w_out):
    proj = x @ w_in
    c1 = POOL_WIDTH
    c2 = c1 + SSM_D_INNER
    c3 = c2 + SSM_CONV_DIM
    u, z, xbc, dt = jnp.split(proj, [c1, c2, c3], axis=-1)
    y_pool = multiscale_pool(u, pool_w, pool_scale)
    y_ssm = mamba2_ssd(z, xbc, dt, conv_w, conv_b, dt_bias, a_log, d_skip, norm_w)
    return jnp.concatenate([y_pool, y_ssm], axis=-1) @ w_out


def odd_mixer(x, w_in, fgate_b, q_norm_w, w_uq, kv_norm_w, w_ukv, w_out):
    b, s, _ = x.shape
    proj = x @ w_in
    cuts = np.cumsum([FOX_WIDTH, FOX_WIDTH, FOX_WIDTH, FOX_HEADS, MLA_Q_RANK, MLA_KV_RANK]).tolist()
    qf, kf, vf, fl, cq, ckv, kr = jnp.split(proj, cuts, axis=-1)

    def heads(t, nh):
        return t.reshape(b, s, nh, -1).transpose(0, 2, 1, 3)

    log_f = jax.nn.log_sigmoid(fl.astype(F32) + fgate_b)
    fcum = jnp.cumsum(log_f, axis=1).transpose(0, 2, 1)
    o_fox = block_causal_attention(heads(qf, FOX_HEADS), heads(kf, FOX_HEADS), heads(vf, FOX_HEADS), fcum)

    pos = jnp.arange(s)
    q = heads(rms_norm(cq, q_norm_w) @ w_uq, MLA_HEADS)
    q = jnp.concatenate([q[..., :MLA_NOPE], rope(q[..., MLA_NOPE:], pos)], axis=-1)
    kv = heads(rms_norm(ckv, kv_norm_w) @ w_ukv, MLA_HEADS)
    k_rope = jnp.broadcast_to(rope(kr[:, None], pos), (b, MLA_HEADS, s, MLA_ROPE))
    k = jnp.concatenate([kv[..., :MLA_NOPE], k_rope], axis=-1)
    o_mla = block_causal_attention(q, k, kv[..., MLA_NOPE:], None)
    return jnp.concatenate([o_fox, o_mla], axis=-1) @ w_out


def swiglu(x, w_gate, w_up, w_down):
    return (jax.nn.silu(x @ w_gate) * (x @ w_up)) @ w_down


def setup_inputs(seed: int = 0) -> dict:
    key = jax.random.key(seed)
    ks = iter(jax.random.split(key, 40))

    def nrm(shape, std):
        return jax.random.normal(next(ks), shape, F32) * std

    def gain(shape):
        return 1.0 + 0.1 * jax.random.normal(next(ks), shape, F32)

    dt0 = jnp.exp(jax.random.uniform(next(ks), (N_EVEN, SSM_HEADS), F32, math.log(1e-3), math.log(1e-1)))
    return {
        "x": jax.random.normal(next(ks), (BATCH, SEQ, D_MODEL), F32),
        "even_w_in": nrm((N_EVEN, D_MODEL, EVEN_IN), D_MODEL ** -0.5),
        "pool_w": nrm((N_EVEN, len(POOL_WINDOWS), POOL_GROUP, POOL_GROUP), POOL_GROUP ** -0.5),
        "pool_scale": gain((N_EVEN, POOL_WIDTH)),
        "conv_w": nrm((N_EVEN, SSM_CONV, SSM_CONV_DIM), SSM_CONV ** -0.5),
        "conv_b": nrm((N_EVEN, SSM_CONV_DIM), 0.01),
        "dt_bias": dt0 + jnp.log(-jnp.expm1(-dt0)),
        "a_log": jnp.log(jax.random.uniform(next(ks), (N_EVEN, SSM_HEADS), F32, 1.0, 16.0)),
        "d_skip": gain((N_EVEN, SSM_HEADS)),
        "ssm_norm_w": gain((N_EVEN, SSM_D_INNER)),
        "even_w_out": nrm((N_EVEN, EVEN_MIX, D_MODEL), EVEN_MIX ** -0.5 * DEEPNORM_BETA),
        "odd_w_in": nrm((N_ODD, D_MODEL, ODD_IN), D_MODEL ** -0.5),
        "fgate_b": jax.random.uniform(next(ks), (N_ODD, FOX_HEADS), F32, 1.0, 5.0),
        "q_norm_w": gain((N_ODD, MLA_Q_RANK)),
        "w_uq": nrm((N_ODD, MLA_Q_RANK, MLA_HEADS * (MLA_NOPE + MLA_ROPE)), MLA_Q_RANK ** -0.5),
        "kv_norm_w": gain((N_ODD, MLA_KV_RANK)),
        "w_ukv": nrm((N_ODD, MLA_KV_RANK, MLA_HEADS * (MLA_NOPE + MLA_V)), MLA_KV_RANK ** -0.5),
        "odd_w_out": nrm((N_ODD, ODD_MIX, D_MODEL), ODD_MIX ** -0.5 * DEEPNORM_BETA),
        "ffn_w_gate": nrm((DEPTH, D_MODEL, D_FF), D_MODEL ** -0.5),
        "ffn_w_up": nrm((DEPTH, D_MODEL, D_FF), D_MODEL ** -0.5),
        "ffn_w_down": nrm((DEPTH, D_FF, D_MODEL), D_FF ** -0.5 * DEEPNORM_BETA),
        "ln_mix_g": gain((DEPTH, D_MODEL)),
        "ln_mix_b": nrm((DEPTH, D_MODEL), 0.02),
        "ln_ffn_g": gain((DEPTH, D_MODEL)),
        "ln_ffn_b": nrm((DEPTH, D_MODEL), 0.02),
    }


def reference(x, even_w_in, pool_w, pool_scale, conv_w, conv_b, dt_bias, a_log, d_skip, ssm_norm_w,
              even_w_out, odd_w_in, fgate_b, q_norm_w, w_uq, kv_norm_w, w_ukv, odd_w_out,
              ffn_w_gate, ffn_w_up, ffn_w_down, ln_mix_g, ln_mix_b, ln_ffn_g, ln_ffn_b):
    for l in range(DEPTH):
        i = l // 2
        if l % 2 == 0:
            h = even_mixer(x, even_w_in[i], pool_w[i], pool_scale[i], conv_w[i], conv_b[i], dt_bias[i],
                           a_log[i], d_skip[i], ssm_norm_w[i], even_w_out[i])
        else:
            h = odd_mixer(x, odd_w_in[i], fgate_b[i], q_norm_w[i], w_uq[i], kv_norm_w[i], w_ukv[i], odd_w_out[i])
        x = layer_norm(DEEPNORM_ALPHA * x + h, ln_mix_g[l], ln_mix_b[l])
        x = layer_norm(DEEPNORM_ALPHA * x + swiglu(x, ffn_w_gate[l], ffn_w_up[l], ffn_w_down[l]),
                       ln_ffn_g[l], ln_ffn_b[l])
    return x
```

```python
import contextlib
import numpy as np
import concourse.bass as bass
import concourse.mybir as mybir
from concourse.bass_utils import run_bass_kernel_spmd

F32 = mybir.dt.float32
BF16 = mybir.dt.bfloat16
AF = mybir.ActivationFunctionType
ALU = mybir.AluOpType
AX = mybir.AxisListType

D = 1024
DFF = 2816
NFF = DFF // 128
G = 512
ALPHA = 8.0 ** 0.25
LN_EPS = 1e-5
RMS_EPS = 1e-6

ENGS = ['pe', 'act', 'dve', 'pool', 'sp']
DRING = 16


class Buf:
    __slots__ = ('name', 'lw', 'rd')

    def __init__(self, name=''):
        self.name = name
        self.lw = None
        self.rd = {}


class Sched:
    def __init__(self, nc, ctx):
        self.nc = nc
        self.semh = {}
        for e in ENGS:
            self.semh[e] = ctx.enter_context(nc.semaphore('s_' + e))
        for e in ('sp', 'pool'):
            for i in range(DRING):
                self.semh[f'd_{e}{i}'] = ctx.enter_context(nc.semaphore(f'd_{e}{i}'))
        self.semh['cc'] = ctx.enter_context(nc.semaphore('s_cc'))
        self.cccnt = 0
        self.cnt = {e: 0 for e in ENGS}
        self.dcnt = {e: 0 for e in ENGS}
        self.seen = {e: {} for e in ENGS}
        self.q = {e: [] for e in ENGS}

    def _deps(self, r, w):
        deps = {}

        def add(tok):
            if tok is None:
                return
            k, v = tok
            if deps.get(k, 0) < v:
                deps[k] = v
        for b in r:
            add(b.lw)
        for b in w:
            add(b.lw)
            for k, v in b.rd.items():
                add((k, v))
        return deps

    def op(self, eng, fn, r=(), w=(), signal=True, dma=False, cc=False, cc_wait=True):
        deps = self._deps(r, w)
        waits = []
        seen = self.seen[eng]
        for k, v in deps.items():
            if eng == 'pe' and k == 'pe':
                continue
            if seen.get(k, 0) >= v:
                continue
            seen[k] = v
            waits.append((k, v))
        if cc:
            self.cccnt += 1
            tok = ('cc', self.cccnt)
            inc = 1
            sig = True
        elif dma:
            i = self.dcnt[eng]
            self.dcnt[eng] += 1
            slot, rnd = i % DRING, i // DRING
            key = f'd_{eng}{slot}'
            if rnd > 0 and seen.get(key, 0) < 16 * rnd:
                seen[key] = 16 * rnd
                waits.append((key, 16 * rnd))
            tok = (key, 16 * (rnd + 1))
            inc = 16
            sig = True
        else:
            if signal:
                self.cnt[eng] += 1
                tok = (eng, self.cnt[eng])
            else:
                tok = (eng, self.cnt[eng] + 1)
            inc = 1
            sig = signal
        self.q[eng].append((waits, fn, tok if sig else None, inc))
        if cc and cc_wait:
            seen['cc'] = tok[1]
            self.q[eng].append(([tok], None, None, 0))
        for b in w:
            b.lw = tok
            b.rd = {}
        for b in r:
            k, v = tok
            if b.rd.get(k, 0) < v:
                b.rd[k] = v
        return tok

    def wait_for(self, eng, bufs):
        deps = self._deps((), bufs)
        waits = []
        seen = self.seen[eng]
        for k, v in deps.items():
            if seen.get(k, 0) >= v:
                continue
            seen[k] = v
            waits.append((k, v))
        self.q[eng].append((waits, None, None, 0))

    def barrier(self):
        cur = {}
        for e in ENGS:
            if self.cnt[e] > 0:
                cur[e] = self.cnt[e]
        for e in ('sp', 'pool'):
            n = self.dcnt[e]
            for slot in range(DRING):
                rounds = (n - slot + DRING - 1) // DRING if n > slot else 0
                if rounds > 0:
                    cur[f'd_{e}{slot}'] = 16 * rounds
        if self.cccnt > 0:
            cur['cc'] = self.cccnt
        for e in ENGS:
            seen = self.seen[e]
            waits = []
            for k, v in cur.items():
                if k == e and e == 'pe':
                    continue
                if seen.get(k, 0) >= v:
                    continue
                seen[k] = v
                waits.append((k, v))
            self.q[e].append((waits, None, None, 0))

    def emit(self):
        semh = self.semh
        with self.nc.Block() as block:
            def mk(eng):
                items = self.q[eng]

                def body(e):
                    for waits, fn, tok, inc in items:
                        for k, v in waits:
                            e.wait_ge(semh[k], v)
                        if fn is None:
                            continue
                        ins = fn(e)
                        if tok is not None:
                            ins.then_inc(semh[tok[0]], inc)
                return body
            block.tensor(mk('pe'))
            block.scalar(mk('act'))
            block.vector(mk('dve'))
            block.gpsimd(mk('pool'))
            block.sync(mk('sp'))
        self.q = {e: [] for e in ENGS}


def panelize(W):
    K, N = W.shape
    nkc = -(-K // 128)
    nkg = -(-nkc // 8)
    ncp = -(-N // 512)
    Wp = np.zeros((nkg * 8 * 128, ncp * 512), np.float32)
    Wp[:K, :N] = W
    Wp = Wp.reshape(nkg, 8, 128, ncp, 512).transpose(3, 0, 2, 1, 4)
    return np.ascontiguousarray(Wp).reshape(ncp * nkg, 128, 8, 512)


def bcast_rows(v, n=128):
    return np.ascontiguousarray(np.broadcast_to(np.asarray(v, np.float32).reshape(1, -1), (n, v.size)))


class Prog:
    def __init__(self, name='k', nwb=8, ps_banks=(0, 1, 2, 3, 4, 5, 6, 7), fused=False):
        self.nc = bass.Bass("TRN2", target_bir_lowering=False)
        self.ps_banks = list(ps_banks)
        self.ctx = contextlib.ExitStack()
        self.pctx = None
        self.phase = 0
        self.fused = fused
        self.sfx = ''
        self.io = {}
        self.xmap = None
        self.hook = None
        self.w_eng = 'pool'
        self.S = Sched(self.nc, self.ctx)
        nc = self.nc
        self.psb = []
        for i in range(8):
            t = self.ctx.enter_context(nc.psum_tensor(f'ps{i}', [128, 512], F32))
            self.psb.append((t, Buf(f'ps{i}')))
        self.psi = 0
        self.ident = self.sb('ident_sb', [128, 128], F32)
        self.b_ident = Buf('ident')
        self.outs = []
        self.wp = []
        self.wpi = 0
        self.NWB = nwb
        if not fused:
            self.alloc_wp(nwb)

    def alloc_wp(self, n):
        self.NWB = n
        self.wp = []
        for i in range(n):
            t = self.sb(f'wp{i}', [128, 8, 512], BF16)
            self.wp.append((t, Buf(f'wp{i}')))
        self.wpi = 0

    def begin_phase(self, nwb, ps_banks=(0, 1, 2, 3, 4, 5, 6, 7)):
        self.phase += 1
        self.pctx = contextlib.ExitStack()
        self.ps_banks = list(ps_banks)
        self.psi = 0
        self.alloc_wp(nwb)

    def end_phase(self):
        self.S.barrier()
        self.S.emit()
        self.pctx.close()
        self.pctx = None

    def cc_async(self, kind, groups, src, dst, r=()):
        self.S.op('pool', lambda e: e.collective_compute(kind, ALU.bypass, replica_groups=groups, ins=[src], outs=[dst]),
                  r=list(r), cc=True, cc_wait=False)

    def collective(self, kind, groups, src, dst, rows):
        n = len(groups[0])
        nrow = src.shape[0]
        for k in range(nrow // rows):
            self.S.op('pool', lambda e, k=k: e.collective_compute(kind, ALU.bypass, replica_groups=groups,
                                                                  ins=[src[k * rows:(k + 1) * rows, :]],
                                                                  outs=[dst[k * n * rows:(k + 1) * n * rows, :]]), cc=True)
        self.S.barrier()

    def group_done(self, g):
        if self.hook is not None:
            self.hook(g)

    def xr(self, r0):
        return r0 if self.xmap is None else self.xmap(r0)

    def sb(self, name, shape, dt):
        ctx = self.pctx if self.pctx is not None else self.ctx
        return ctx.enter_context(self.nc.sbuf_tensor(f'sb{self.phase}_' + name, shape, dt))

    def dram_in(self, name, shape, dt=F32):
        if name in self.io:
            return self.io[name]
        return self.nc.dram_tensor(name + self.sfx, list(shape), dt, kind="ExternalInput").ap()

    def dram_out(self, name, shape, dt=F32):
        if name in self.io:
            return self.io[name]
        return self.nc.dram_tensor(name + self.sfx, list(shape), dt, kind="ExternalOutput").ap()

    def ps(self):
        t, b = self.psb[self.ps_banks[self.psi % len(self.ps_banks)]]
        self.psi += 1
        return t, b

    def load_w(self, wd, idx):
        t, b = self.wp[self.wpi % self.NWB]
        self.wpi += 1
        self.S.op(self.w_eng, lambda e: e.dma_start(out=t[:], in_=wd[idx]), w=[b], dma=True)
        return t, b

    def load(self, eng, dst_ap, src_ap, buf, r=()):
        return self.S.op(eng, lambda e: e.dma_start(out=dst_ap, in_=src_ap), r=list(r), w=[buf], dma=True)

    def gather(self, dst_ap, src_ap, idx_ap, buf, r=()):
        return self.S.op('pool', lambda e: e.indirect_dma_start(out=dst_ap, out_offset=None, in_=src_ap,
                                                                in_offset=bass.IndirectOffsetOnAxis(ap=idx_ap, axis=0)),
                         r=list(r), w=[buf], dma=True)

    def finish(self):
        self.S.wait_for('sp', self.outs)
        self.S.emit()
        return self.nc


def emit_ln(P, x, bx, g_t, b_t, bgb, tmp):
    S = P.S
    st, mv, sd, rstd, nmr, eps = tmp['st'], tmp['mv'], tmp['sd'], tmp['rstd'], tmp['nmr'], tmp['eps']
    bt = tmp['buf']
    S.op('dve', lambda e: e.bn_stats(out=st[:, 0, :], in_=x[:, 0:512]), r=[bx], w=[bt])
    S.op('dve', lambda e: e.bn_stats(out=st[:, 1, :], in_=x[:, 512:1024]), r=[bx], w=[bt])
    S.op('dve', lambda e: e.bn_aggr(out=mv[:, :], in_=st[:, :, :]), r=[bt], w=[bt])
    S.op('act', lambda e: e.activation(out=sd[:, :], in_=mv[:, 1:2], func=AF.Sqrt, bias=eps[:, 0:1], scale=1.0),
         r=[bt, tmp['beps']], w=[bt])
    S.op('dve', lambda e: e.reciprocal(out=rstd[:, :], in_=sd[:, :]), r=[bt], w=[bt])
    S.op('dve', lambda e: e.tensor_scalar(out=nmr[:, :], in0=mv[:, 0:1], scalar1=rstd[:, 0:1], scalar2=-1.0,
                                          op0=ALU.mult, op1=ALU.mult), r=[bt], w=[bt])
    S.op('act', lambda e: e.activation(out=x, in_=x, func=AF.Identity, bias=nmr[:, 0:1], scale=rstd[:, 0:1]),
         r=[bt, bx], w=[bx])
    S.op('dve', lambda e: e.tensor_tensor(out=x, in0=x, in1=g_t[:, :], op=ALU.mult), r=[bx, bgb], w=[bx])
    S.op('pool', lambda e: e.tensor_tensor(out=x, in0=x, in1=b_t[:, :], op=ALU.add), r=[bx, bgb], w=[bx])


def emit_transpose_group(P, xs, bxs, xT, bxT, ntile=4):
    S = P.S
    for c in range(8):
        pt, pb = P.ps()
        for t in range(ntile):
            S.op('pe', lambda e, t=t, c=c, pt=pt: e.transpose(pt[:, t * 128:(t + 1) * 128], xs[t][:, c * 128:(c + 1) * 128], P.ident[:, :]),
                 r=[bxs[t], P.b_ident], w=[pb], signal=(t == ntile - 1))
        eng = 'act' if c % 2 == 0 else 'dve'
        if eng == 'act':
            S.op('act', lambda e, c=c, pt=pt: e.activation(out=xT[:, c, 0:ntile * 128], in_=pt[:, 0:ntile * 128], func=AF.Copy),
                 r=[pb], w=[bxT[c]])
        else:
            S.op('dve', lambda e, c=c, pt=pt: e.tensor_copy(out=xT[:, c, 0:ntile * 128], in_=pt[:, 0:ntile * 128]),
                 r=[pb], w=[bxT[c]])


def emit_ffn(P, xT, bxT, hT, bhT, wg_d, wu_d, wd_d, x2, bx2, tmpsg, ntile=4, mid_hook=None):
    S = P.S
    n = ntile * 128
    ncp = -(-DFF // 512)
    for cp in range(ncp):
        wg, bwg = P.load_w(wg_d, cp)
        wu, bwu = P.load_w(wu_d, cp)
        if mid_hook is not None:
            mid_hook(cp)
        nch = min(4, NFF - cp * 4)
        for cc in range(nch):
            f = cp * 4 + cc
            pg, bg = P.ps()
            for k in range(8):
                S.op('pe', lambda e, k=k, cc=cc, pg=pg, wg=wg: e.matmul(pg[:, 0:n], lhsT=wg[:, k, cc * 128:(cc + 1) * 128], rhs=xT[:, k, 0:n],
                                                                       start=(k == 0), stop=(k == 7)),
                     r=[bwg, bxT[k]], w=[bg], signal=(k == 7))
            pu, bu = P.ps()
            for k in range(8):
                S.op('pe', lambda e, k=k, cc=cc, pu=pu, wu=wu: e.matmul(pu[:, 0:n], lhsT=wu[:, k, cc * 128:(cc + 1) * 128], rhs=xT[:, k, 0:n],
                                                                       start=(k == 0), stop=(k == 7)),
                     r=[bwu, bxT[k]], w=[bu], signal=(k == 7))
            sg, bsg = tmpsg[f % 2]
            S.op('act', lambda e, pg=pg, sg=sg: e.activation(out=sg[:, 0:n], in_=pg[:, 0:n], func=AF.Silu), r=[bg], w=[bsg])
            S.op('dve', lambda e, f=f, pu=pu, sg=sg: e.tensor_tensor(out=hT[:, f, 0:n], in0=sg[:, 0:n], in1=pu[:, 0:n], op=ALU.mult),
                 r=[bsg, bu], w=[bhT[f]])
    for hf in range(2):
        wds = [P.load_w(wd_d, hf * 3 + kg) for kg in range(3)]
        for t in range(ntile):
            po, bo = P.ps()
            for f in range(NFF):
                wt, bw = wds[f // 8]
                S.op('pe', lambda e, f=f, t=t, po=po, wt=wt: e.matmul(po[:, :], lhsT=hT[:, f, t * 128:(t + 1) * 128], rhs=wt[:, f % 8, :],
                                                                     start=(f == 0), stop=(f == NFF - 1)),
                     r=[bw, bhT[f]], w=[bo], signal=(f == NFF - 1))
            xs = x2[t][:, hf * 512:(hf + 1) * 512]
            S.op('dve', lambda e, xs=xs, po=po: e.scalar_tensor_tensor(out=xs, in0=xs, scalar=ALPHA, op0=ALU.mult, in1=po[:, :], op1=ALU.add),
                 r=[bo, bx2[t]], w=[bx2[t]])


class Common:
    def __init__(self, P, lng_names):
        nc = P.nc
        S = P.S
        self.P = P
        ident_d = P.dram_in('ident', [128, 128])
        P.load('sp', P.ident[:, :], ident_d, P.b_ident)
        self.eps = P.sb('eps', [128, 1], F32)
        self.beps = Buf('eps')
        S.op('dve', lambda e: e.memset(self.eps[:, :], LN_EPS), w=[self.beps])
        self.lnp = {}
        self.blnp = Buf('lnp')
        for nm in lng_names:
            d = P.dram_in(nm, [128, D])
            t = P.sb('t_' + nm, [128, D], F32)
            P.load('sp', t[:, :], d, self.blnp)
            self.lnp[nm] = t
        self.lntmp = {
            'st': P.sb('ln_st', [128, 2, 6], F32), 'mv': P.sb('ln_mv', [128, 2], F32),
            'sd': P.sb('ln_sd', [128, 1], F32), 'rstd': P.sb('ln_rstd', [128, 1], F32),
            'nmr': P.sb('ln_nmr', [128, 1], F32), 'eps': self.eps, 'beps': self.beps, 'buf': Buf('lntmp'),
        }
        self.xT = P.sb('xT', [128, 8, G], BF16)
        self.bxT = [Buf(f'xT{c}') for c in range(8)]
        self.hT = P.sb('hT', [128, NFF, G], BF16)
        self.bhT = [Buf(f'hT{f}') for f in range(NFF)]
        self.sg = [(P.sb(f'sg{i}', [128, G], F32), Buf(f'sg{i}')) for i in range(2)]
        self.r = P.sb('r', [128, 4, D], F32)
        self.br = [Buf(f'r{t}') for t in range(4)]


def emit_tail(P, C, wg_d, wu_d, wd_d, out_ap_tiles, mid_hook=None):
    S = P.S
    xs = [C.r[:, t, :] for t in range(4)]
    for t in range(4):
        emit_ln(P, xs[t], C.br[t], C.lnp['ln1g'], C.lnp['ln1b'], C.blnp, C.lntmp)
    emit_transpose_group(P, xs, C.br, C.xT, C.bxT)
    emit_ffn(P, C.xT, C.bxT, C.hT, C.bhT, wg_d, wu_d, wd_d, xs, C.br, C.sg, mid_hook=mid_hook)
    for t in range(4):
        emit_ln(P, xs[t], C.br[t], C.lnp['ln2g'], C.lnp['ln2b'], C.blnp, C.lntmp)
        ob = Buf('out')
        S.op('sp', lambda e, t=t: e.dma_start(out=out_ap_tiles[t], in_=xs[t]), r=[C.br[t]], w=[ob], dma=True)
        P.outs.append(ob)


def build_tail_test(ngroups):
    P = Prog()
    T = ngroups * G
    r_d = P.dram_in('r', [T, D])
    wg_d = P.dram_in('wg', [6, 128, 8, 512])
    wu_d = P.dram_in('wu', [6, 128, 8, 512])
    wd_d = P.dram_in('wd', [6, 128, 8, 512])
    C = Common(P, ['ln1g', 'ln1b', 'ln2g', 'ln2b'])
    out_d = P.dram_out('out', [T, D])
    for g in range(ngroups):
        for t in range(4):
            r0 = g * G + t * 128
            P.load('sp', C.r[:, t, :], r_d[r0:r0 + 128, :], C.br[t])
        emit_tail(P, C, wg_d, wu_d, wd_d, [out_d[g * G + t * 128: g * G + (t + 1) * 128, :] for t in range(4)])
    return P.finish()


def build_tail(kind, T):
    P = Prog()
    S = P.S
    ng = T // G
    x_d = P.dram_in('x', [T, D])
    if kind == 'even':
        yp_d = P.dram_in('ypool', [T, 512])
        yg_d = P.dram_in('yg', [T, D])
        ssq_d = P.dram_in('ssq', [T, 4])
        nw_d = P.dram_in('normw', [128, D])
        KC = 12
        npan = 4
    else:
        o_d = P.dram_in('o', [T, D])
        KC = 8
        npan = 2
    wo_d = P.dram_in('wo', [npan, 128, 8, 512])
    wg_d = P.dram_in('wg', [6, 128, 8, 512])
    wu_d = P.dram_in('wu', [6, 128, 8, 512])
    wd_d = P.dram_in('wd', [6, 128, 8, 512])
    C = Common(P, ['ln1g', 'ln1b', 'ln2g', 'ln2b'])
    out_d = P.dram_out('out', [T, D])
    xin = P.sb('xin', [128, 4, D], F32)
    bxin = [Buf(f'xin{t}') for t in range(4)]
    mix = P.sb('mix', [128, 4, KC * 128], F32)
    bmix = [Buf(f'mix{t}') for t in range(4)]
    mixT = P.sb('mixT', [128, KC, G], BF16)
    bmixT = [Buf(f'mixT{c}') for c in range(KC)]
    if kind == 'even':
        nw = P.sb('nw', [128, D], F32)
        bnw = Buf('nw')
        P.load('sp', nw[:, :], nw_d, bnw)
        ssq = P.sb('ssqt', [128, 4], F32)
        sm = P.sb('ssm', [128, 4], F32)
        bss = Buf('ssq')
        epsr = P.sb('epsr', [128, 1], F32)
        bepsr = Buf('epsr')
        S.op('dve', lambda e: e.memset(epsr[:, :], RMS_EPS), w=[bepsr])
    for g in range(ng):
        for t in range(4):
            r0 = g * G + t * 128
            P.load('sp', xin[:, t, :], x_d[r0:r0 + 128, :], bxin[t])
            if kind == 'even':
                P.load('sp', mix[:, t, 0:512], yp_d[r0:r0 + 128, :], bmix[t])
                P.load('sp', mix[:, t, 512:1536], yg_d[r0:r0 + 128, :], bmix[t])
                P.load('sp', ssq[:, :], ssq_d[r0:r0 + 128, :], bss)
                S.op('dve', lambda e: e.tensor_reduce(out=sm[:, 0:1], in_=ssq[:, :], axis=AX.X, op=ALU.add), r=[bss], w=[bss])
                S.op('act', lambda e: e.activation(out=sm[:, 1:2], in_=sm[:, 0:1], func=AF.Sqrt, bias=epsr[:, 0:1], scale=1.0 / D),
                     r=[bss, bepsr], w=[bss])
                S.op('dve', lambda e: e.reciprocal(out=sm[:, 2:3], in_=sm[:, 1:2]), r=[bss], w=[bss])
                S.op('act', lambda e, t=t: e.activation(out=mix[:, t, 512:1536], in_=mix[:, t, 512:1536], func=AF.Identity, scale=sm[:, 2:3]),
                     r=[bss, bmix[t]], w=[bmix[t]])
                S.op('dve', lambda e, t=t: e.tensor_tensor(out=mix[:, t, 512:1536], in0=mix[:, t, 512:1536], in1=nw[:, :], op=ALU.mult),
                     r=[bmix[t], bnw], w=[bmix[t]])
            else:
                P.load('sp', mix[:, t, :], o_d[r0:r0 + 128, :], bmix[t])
        for c in range(KC):
            pt, pb = P.ps()
            for t in range(4):
                S.op('pe', lambda e, t=t, c=c, pt=pt: e.transpose(pt[:, t * 128:(t + 1) * 128], mix[:, t, c * 128:(c + 1) * 128], P.ident[:, :]),
                     r=[bmix[t], P.b_ident], w=[pb], signal=(t == 3))
            if c % 2 == 0:
                S.op('act', lambda e, c=c, pt=pt: e.activation(out=mixT[:, c, :], in_=pt[:, :], func=AF.Copy), r=[pb], w=[bmixT[c]])
            else:
                S.op('dve', lambda e, c=c, pt=pt: e.tensor_copy(out=mixT[:, c, :], in_=pt[:, :]), r=[pb], w=[bmixT[c]])
        nkg = npan // 2
        for hf in range(2):
            wos = [P.load_w(wo_d, hf * nkg + kg) for kg in range(nkg)]
            for t in range(4):
                po, bo = P.ps()
                for k in range(KC):
                    wt, bw = wos[k // 8]
                    S.op('pe', lambda e, k=k, t=t, po=po, wt=wt: e.matmul(po[:, :], lhsT=mixT[:, k, t * 128:(t + 1) * 128], rhs=wt[:, k % 8, :],
                                                                         start=(k == 0), stop=(k == KC - 1)),
                         r=[bw, bmixT[k]], w=[bo], signal=(k == KC - 1))
                S.op('dve', lambda e, t=t, hf=hf, po=po: e.scalar_tensor_tensor(out=C.r[:, t, hf * 512:(hf + 1) * 512], in0=xin[:, t, hf * 512:(hf + 1) * 512],
                                                                               scalar=ALPHA, op0=ALU.mult, in1=po[:, :], op1=ALU.add),
                     r=[bo, bxin[t]], w=[C.br[t]])
        emit_tail(P, C, wg_d, wu_d, wd_d, [out_d[g * G + t * 128: g * G + (t + 1) * 128, :] for t in range(4)])
    return P.finish()


def build_even_mixer(L, P=None):
    standalone = P is None
    if standalone:
        P = Prog(nwb=2)
    S = P.S
    ng = L // G
    x_d = P.dram_in('x', [L, D])
    w_d = P.dram_in('w', [2, 128, 8, 512])
    pw_d = P.dram_in('poolw', [128, 128])
    pcol_d = P.dram_in('pcol', [128, 8])
    cvw_d = P.dram_in('cvw', [128, 20])
    rows_d = P.dram_in('rows', [128, 12])
    rcnt_d = P.dram_in('rcnt', [128, 16])
    cst_d = P.dram_in('consts', [128, 4, 128])
    psrow_d = P.dram_in('psrow', [128, 128])
    yp_d = P.dram_out('ypool', [L, 128])
    yg_d = P.dram_out('yg', [L, 257])

    cst = P.sb('cst', [128, 4, 128], F32)
    bcst = Buf('cst')
    P.load('sp', cst[:, :, :], cst_d, bcst)
    ident, tri, Um, ones = cst[:, 0, :], cst[:, 1, :], cst[:, 2, :], cst[:, 3, :]
    wA = P.wp[0][0]
    wB = P.wp[1][0]
    bW = Buf('w')
    S.op('pool', lambda e: e.dma_start(out=wA[:], in_=w_d[0]), w=[bW], dma=True)
    S.op('pool', lambda e: e.dma_start(out=wB[:], in_=w_d[1]), w=[bW], dma=True)
    pw = P.sb('pw', [128, 128], BF16)
    S.op('pool', lambda e: e.dma_start(out=pw[:, :], in_=pw_d), w=[bW], dma=True)
    pcol = P.sb('pcol', [128, 8], F32)
    cvw = P.sb('cvw', [128, 20], F32)
    rows = P.sb('rows', [128, 12], F32)
    rcnt = P.sb('rcnt', [128, 16], F32)
    bpar = Buf('par')
    P.load('sp', pcol[:, :], pcol_d, bpar)
    P.load('sp', cvw[:, :], cvw_d, bpar)
    P.load('sp', rows[:, :], rows_d, bpar)
    P.load('sp', rcnt[:, :], rcnt_d, bpar)
    arow = P.sb('arow', [128, 4], F32)
    one1 = P.sb('one1', [128, 1], F32)
    S.op('dve', lambda e: e.memset(one1[:, :], 1.0), w=[bpar])
    S.op('act', lambda e: e.activation(out=arow[:, :], in_=rows[:, 4:8], func=AF.Exp), r=[bpar], w=[bpar])
    S.op('dve', lambda e: e.tensor_scalar(out=arow[:, :], in0=arow[:, :], scalar1=-1.0, scalar2=None, op0=ALU.mult), r=[bpar], w=[bpar])

    xt = P.sb('xt', [128, 4, D], F32)
    bxt = [Buf(f'xt{t}') for t in range(4)]
    xT = P.sb('xT', [128, 8, G], BF16)
    bxT = [Buf(f'xT{c}') for c in range(8)]
    HW = 16
    uh = P.sb('uh', [128, HW + G], F32)
    buh = Buf('uh')
    pa = P.sb('pa', [128, HW + G], F32)
    pb_ = P.sb('pb', [128, HW + G], F32)
    bpab = Buf('pab')
    diff = P.sb('diff', [128, G], BF16)
    bdiff = Buf('diff')
    t16 = P.sb('t16', [128, 16], F32)
    ypg = P.sb('ypg', [128, 4, 128], F32)
    bypg = [Buf(f'ypg{t}') for t in range(4)]
    psrow = P.sb('psrow', [128, 128], F32)
    P.load('sp', psrow[:, :], psrow_d, bpar)
    cvin = [P.sb(f'cvin{i}', [128, HW + G], F32) for i in range(4)]
    bcvin = [Buf(f'cvin{i}') for i in range(4)]
    cacc = P.sb('cacc', [128, G], F32)
    bcacc = Buf('cacc')
    cvo = [P.sb(f'cvo{i}', [128, G], F32) for i in range(4)]
    bcvo = [Buf(f'cvo{i}') for i in range(4)]
    BTb = P.sb('BTb', [128, G], BF16)
    CTb = P.sb('CTb', [128, G], BF16)
    bBTb, bCTb = Buf('BTb'), Buf('CTb')
    zs = P.sb('zs', [128, 4, 256], F32)
    bzs = [Buf(f'zs{t}') for t in range(4)]
    dtv = P.sb('dtv', [128, 4, 4], F32)
    dtt = P.sb('dtt', [128, 4, 4], F32)
    da = P.sb('da', [128, 4, 4], F32)
    bdt = [Buf(f'dt{t}') for t in range(4)]
    xs_tm = P.sb('xs_tm', [128, 256], F32)
    B_tm = P.sb('B_tm', [128, 128], BF16)
    btm = Buf('tm')
    ex = P.sb('ex', [128, 8], F32)
    bex = Buf('ex')
    R = P.sb('R', [128, 512], F32)
    bR = Buf('R')
    LT = P.sb('LT', [128, 512], F32)
    bLT = Buf('LT')
    cbm = P.sb('cbm', [128, 128], F32)
    bcbm = Buf('cbm')
    MT = P.sb('MT', [128, 512], BF16)
    bMT = Buf('MT')
    xdt = P.sb('xdt', [128, 256], BF16)
    xdtd = P.sb('xdtd', [128, 256], BF16)
    bxdt = Buf('xdt')
    y_sb = P.sb('y_sb', [128, 256], F32)
    by = Buf('y')
    ygt = [P.sb(f'ygt{i}', [128, 257], F32) for i in range(2)]
    sq = P.sb('sq', [128, 256], F32)
    bsq = Buf('sq')
    ssqt = [P.sb(f'ssqt{i}', [128, 1], F32) for i in range(2)]
    bygt = [Buf(f'ygt{i}') for i in range(2)]
    h = P.sb('h', [128, 256], F32)
    h_bf = P.sb('h_bf', [128, 256], BF16)
    bh = Buf('h')
    bhbf = Buf('hbf')
    S.op('dve', lambda e: e.memset(h[:, :], 0.0), w=[bh])
    S.op('dve', lambda e: e.memset(h_bf[:, :], 0.0), w=[bhbf])
    S.op('dve', lambda e: e.memset(uh[:, 0:HW], 0.0), w=[buh])
    for i in range(4):
        S.op('dve', lambda e, i=i: e.memset(cvin[i][:, 0:HW], 0.0), w=[bcvin[i]])
    nchunk = 0
    for g in range(ng):
        for t in range(4):
            r0 = g * G + t * 128
            P.load('sp', xt[:, t, :], x_d[P.xr(r0):P.xr(r0) + 128, :], bxt[t])
        for c in range(8):
            pt, pb = P.ps()
            for t in range(4):
                S.op('pe', lambda e, t=t, c=c, pt=pt: e.transpose(pt[:, t * 128:(t + 1) * 128], xt[:, t, c * 128:(c + 1) * 128], ident),
                     r=[bxt[t], bcst], w=[pb], signal=(t == 3))
            if c % 2 == 0:
                S.op('act', lambda e, c=c, pt=pt: e.activation(out=xT[:, c, :], in_=pt[:, :], func=AF.Copy), r=[pb], w=[bxT[c]])
            else:
                S.op('dve', lambda e, c=c, pt=pt: e.tensor_copy(out=xT[:, c, :], in_=pt[:, :]), r=[pb], w=[bxT[c]])
        fm = [(wA, 0, uh, buh), (wA, 128, cvin[0], bcvin[0]), (wA, 256, cvin[1], bcvin[1]), (wA, 384, cvin[2], bcvin[2]),
              (wB, 0, cvin[3], bcvin[3])]
        for wt, c0, dst, bdst in fm:
            pp, bp = P.ps()
            for k in range(8):
                S.op('pe', lambda e, k=k, pp=pp, wt=wt, c0=c0: e.matmul(pp[:, :], lhsT=wt[:, k, c0:c0 + 128], rhs=xT[:, k, :], start=(k == 0), stop=(k == 7)),
                     r=[bW, bxT[k]], w=[bp], signal=(k == 7))
            S.op('act', lambda e, pp=pp, dst=dst: e.activation(out=dst[:, HW:HW + G], in_=pp[:, :], func=AF.Copy), r=[bp], w=[bdst])
        for t in range(4):
            pz, bz = P.ps()
            for k in range(8):
                S.op('pe', lambda e, k=k, t=t, pz=pz: e.matmul(pz[:, 0:256], lhsT=xT[:, k, t * 128:(t + 1) * 128], rhs=wB[:, k, 128:384], start=(k == 0), stop=(k == 7)),
                     r=[bW, bxT[k]], w=[bz], signal=False)
            for k in range(8):
                S.op('pe', lambda e, k=k, t=t, pz=pz: e.matmul(pz[:, 256:260], lhsT=xT[:, k, t * 128:(t + 1) * 128], rhs=wB[:, k, 384:388], start=(k == 0), stop=(k == 7)),
                     r=[bW, bxT[k]], w=[bz], signal=(k == 7))
            S.op('act', lambda e, t=t, pz=pz: e.activation(out=zs[:, t, :], in_=pz[:, 0:256], func=AF.Silu), r=[bz], w=[bzs[t]])
            S.op('dve', lambda e, t=t, pz=pz: e.tensor_tensor(out=dtv[:, t, :], in0=pz[:, 256:260], in1=rows[:, 0:4], op=ALU.add), r=[bz, bpar], w=[bdt[t]])
            S.op('act', lambda e, t=t: e.activation(out=dtv[:, t, :], in_=dtv[:, t, :], func=AF.Exp), r=[bdt[t]], w=[bdt[t]])
            S.op('act', lambda e, t=t: e.activation(out=dtt[:, t, :], in_=dtv[:, t, :], func=AF.Ln, bias=one1[:, 0:1], scale=1.0), r=[bdt[t], bpar], w=[bdt[t]])
            S.op('dve', lambda e, t=t: e.tensor_tensor(out=da[:, t, :], in0=dtt[:, t, :], in1=arow[:, :], op=ALU.mult), r=[bdt[t], bpar], w=[bdt[t]])
        S.op('dve', lambda e: e.scalar_tensor_tensor(out=pa[:, 1:HW + G], in0=uh[:, 0:HW + G - 1], scalar=pcol[:, 1:2], op0=ALU.mult, in1=uh[:, 1:HW + G], op1=ALU.add),
             r=[buh, bpar], w=[bpab])
        S.op('dve', lambda e: e.scalar_tensor_tensor(out=pb_[:, 3:HW + G], in0=pa[:, 1:HW + G - 2], scalar=pcol[:, 2:3], op0=ALU.mult, in1=pa[:, 3:HW + G], op1=ALU.add),
             r=[bpab, bpar], w=[bpab])
        S.op('dve', lambda e: e.scalar_tensor_tensor(out=pa[:, 7:HW + G], in0=pb_[:, 3:HW + G - 4], scalar=pcol[:, 3:4], op0=ALU.mult, in1=pb_[:, 7:HW + G], op1=ALU.add),
             r=[bpab, bpar], w=[bpab])
        S.op('dve', lambda e: e.scalar_tensor_tensor(out=pb_[:, 15:HW + G], in0=pa[:, 7:HW + G - 8], scalar=pcol[:, 4:5], op0=ALU.mult, in1=pa[:, 15:HW + G], op1=ALU.add),
             r=[bpab, bpar], w=[bpab])
        S.op('dve', lambda e: e.scalar_tensor_tensor(out=diff[:, :], in0=pb_[:, HW:HW + G], scalar=pcol[:, 5:6], op0=ALU.mult, in1=uh[:, HW:HW + G], op1=ALU.subtract),
             r=[bpab, buh, bpar], w=[bdiff])
        if g == 0:
            S.op('dve', lambda e: e.tensor_tensor(out=t16[:, :], in0=pb_[:, HW:HW + 16], in1=rcnt[:, :], op=ALU.mult), r=[bpab, bpar], w=[bpab])
            S.op('dve', lambda e: e.tensor_tensor(out=diff[:, 0:16], in0=t16[:, :], in1=uh[:, HW:HW + 16], op=ALU.subtract), r=[bpab, buh], w=[bdiff])
        S.op('dve', lambda e: e.tensor_copy(out=uh[:, 0:HW], in_=uh[:, G:G + HW]), r=[buh], w=[buh])
        pp, bp = P.ps()
        for t in range(4):
            S.op('pe', lambda e, pp=pp, t=t: e.matmul(pp[:, t * 128:(t + 1) * 128], lhsT=diff[:, t * 128:(t + 1) * 128], rhs=pw[:, :], start=True, stop=True),
                 r=[bW, bdiff], w=[bp], signal=(t == 3))
        for t in range(4):
            S.op('dve', lambda e, pp=pp, t=t: e.tensor_tensor(out=ypg[:, t, :], in0=pp[:, t * 128:(t + 1) * 128], in1=psrow[:, :], op=ALU.mult),
                 r=[bp, bpar], w=[bypg[t]])
            ob = Buf('o')
            r0 = g * G + t * 128
            S.op('sp', lambda e, t=t, r0=r0: e.dma_start(out=yp_d[r0:r0 + 128, :], in_=ypg[:, t, :]), r=[bypg[t]], w=[ob], dma=True)
            P.outs.append(ob)
        for i in range(4):
            ci = cvin[i]
            S.op('dve', lambda e, ci=ci, i=i: e.tensor_scalar(out=cacc[:, :], in0=ci[:, HW:HW + G], scalar1=cvw[:, i * 5 + 3:i * 5 + 4], scalar2=cvw[:, i * 5 + 4:i * 5 + 5],
                                                             op0=ALU.mult, op1=ALU.add), r=[bcvin[i], bpar], w=[bcacc])
            for kk in range(1, 4):
                S.op('dve', lambda e, ci=ci, i=i, kk=kk: e.scalar_tensor_tensor(out=cacc[:, :], in0=ci[:, HW - kk:HW + G - kk], scalar=cvw[:, i * 5 + 3 - kk:i * 5 + 4 - kk],
                                                                               op0=ALU.mult, in1=cacc[:, :], op1=ALU.add), r=[bcvin[i], bpar, bcacc], w=[bcacc])
            S.op('act', lambda e, i=i: e.activation(out=cvo[i][:, :], in_=cacc[:, :], func=AF.Silu), r=[bcacc], w=[bcvo[i]])
            S.op('dve', lambda e, ci=ci: e.tensor_copy(out=ci[:, 0:HW], in_=ci[:, G:G + HW]), r=[bcvin[i]], w=[bcvin[i]])
        S.op('pool', lambda e: e.tensor_copy(out=BTb[:, :], in_=cvo[2][:, :]), r=[bcvo[2]], w=[bBTb])
        S.op('pool', lambda e: e.tensor_copy(out=CTb[:, :], in_=cvo[3][:, :]), r=[bcvo[3]], w=[bCTb])
        for t in range(4):
            sl = slice(t * 128, (t + 1) * 128)
            pt, pb = P.ps()
            for i in range(3):
                S.op('pe', lambda e, i=i, pt=pt, sl=sl: e.transpose(pt[:, i * 128:(i + 1) * 128], cvo[i][:, sl], ident), r=[bcvo[i], bcst], w=[pb], signal=(i == 2))
            S.op('act', lambda e, pt=pt: e.activation(out=xs_tm[:, :], in_=pt[:, 0:256], func=AF.Copy), r=[pb], w=[btm])
            S.op('dve', lambda e, pt=pt: e.tensor_copy(out=B_tm[:, :], in_=pt[:, 256:384]), r=[pb], w=[btm])
            p2, b2 = P.ps()
            S.op('pe', lambda e, p2=p2, t=t: e.matmul(p2[:, 0:4], lhsT=tri, rhs=da[:, t, :], start=True, stop=True), r=[bcst, bdt[t]], w=[b2], signal=False)
            S.op('pe', lambda e, p2=p2, t=t: e.matmul(p2[:, 4:8], lhsT=ones, rhs=da[:, t, :], start=True, stop=True), r=[bcst, bdt[t]], w=[b2])
            S.op('act', lambda e, p2=p2: e.activation(out=ex[:, :], in_=p2[:, 0:8], func=AF.Exp), r=[b2], w=[bex])
            for hh in range(4):
                S.op('dve', lambda e, hh=hh, t=t: e.tensor_scalar(out=R[:, hh * 128:(hh + 1) * 128], in0=tri, scalar1=da[:, t, hh:hh + 1], scalar2=None, op0=ALU.mult),
                     r=[bcst, bdt[t]], w=[bR])
            p3, b3 = P.ps()
            S.op('pe', lambda e, p3=p3: e.matmul(p3[:, :], lhsT=Um, rhs=R[:, :], start=True, stop=True), r=[bcst, bR], w=[b3])
            S.op('act', lambda e, p3=p3: e.activation(out=LT[:, :], in_=p3[:, :], func=AF.Exp), r=[b3], w=[bLT])
            p4, b4 = P.ps()
            S.op('pe', lambda e, p4=p4, sl=sl: e.matmul(p4[:, 0:128], lhsT=BTb[:, sl], rhs=CTb[:, sl], start=True, stop=True), r=[bBTb, bCTb], w=[b4])
            S.op('dve', lambda e, p4=p4: e.tensor_tensor(out=cbm[:, :], in0=p4[:, 0:128], in1=tri, op=ALU.mult), r=[b4, bcst], w=[bcbm])
            for hh in range(4):
                S.op('dve', lambda e, hh=hh: e.tensor_tensor(out=MT[:, hh * 128:(hh + 1) * 128], in0=LT[:, hh * 128:(hh + 1) * 128], in1=cbm[:, :], op=ALU.mult), r=[bLT, bcbm], w=[bMT])
            for hh in range(4):
                cs = slice(hh * 64, (hh + 1) * 64)
                S.op('dve', lambda e, hh=hh, cs=cs, t=t: e.tensor_scalar(out=xdt[:, cs], in0=xs_tm[:, cs], scalar1=dtt[:, t, hh:hh + 1], scalar2=None, op0=ALU.mult),
                     r=[btm, bdt[t]], w=[bxdt])
                S.op('dve', lambda e, hh=hh, cs=cs, t=t: e.tensor_scalar(out=xdtd[:, cs], in0=xs_tm[:, cs], scalar1=dtt[:, t, hh:hh + 1], scalar2=LT[:, hh * 128 + 127:hh * 128 + 128],
                                                                        op0=ALU.mult, op1=ALU.mult), r=[btm, bdt[t], bLT], w=[bxdt])
            p5, b5 = P.ps()
            for hh in range(4):
                S.op('pe', lambda e, hh=hh, p5=p5: e.matmul(p5[:, hh * 64:(hh + 1) * 64], lhsT=MT[:, hh * 128:(hh + 1) * 128], rhs=xdt[:, hh * 64:(hh + 1) * 64], start=True, stop=True),
                     r=[bMT, bxdt], w=[b5], signal=(hh == 3))
            p6, b6 = P.ps()
            S.op('pe', lambda e, p6=p6, sl=sl: e.matmul(p6[:, 0:256], lhsT=CTb[:, sl], rhs=h_bf[:, :], start=True, stop=True), r=[bCTb, bhbf], w=[b6])
            S.op('act', lambda e, p5=p5: e.activation(out=y_sb[:, :], in_=p5[:, 0:256], func=AF.Copy), r=[b5], w=[by])
            for hh in range(4):
                cs = slice(hh * 64, (hh + 1) * 64)
                S.op('dve', lambda e, hh=hh, cs=cs, p6=p6: e.scalar_tensor_tensor(out=y_sb[:, cs], in0=p6[:, cs], scalar=ex[:, hh:hh + 1], op0=ALU.mult, in1=y_sb[:, cs], op1=ALU.add),
                     r=[b6, bex, by], w=[by])
                S.op('dve', lambda e, hh=hh, cs=cs: e.scalar_tensor_tensor(out=y_sb[:, cs], in0=xs_tm[:, cs], scalar=rows[:, 8 + hh:9 + hh], op0=ALU.mult, in1=y_sb[:, cs], op1=ALU.add),
                     r=[btm, bpar, by], w=[by])
            yb = nchunk % 2
            S.op('dve', lambda e, yb=yb, t=t: e.tensor_tensor(out=ygt[yb][:, 0:256], in0=y_sb[:, :], in1=zs[:, t, :], op=ALU.mult), r=[by, bzs[t]], w=[bygt[yb]])
            S.op('pool', lambda e, yb=yb: e.tensor_tensor(out=sq[:, :], in0=ygt[yb][:, 0:256], in1=ygt[yb][:, 0:256], op=ALU.mult), r=[bygt[yb]], w=[bsq])
            S.op('dve', lambda e, yb=yb: e.tensor_reduce(out=ygt[yb][:, 256:257], in_=sq[:, :], axis=AX.X, op=ALU.add), r=[bsq], w=[bygt[yb]])
            r0 = g * G + t * 128
            o1 = Buf('o')
            S.op('sp', lambda e, yb=yb, r0=r0: e.dma_start(out=yg_d[r0:r0 + 128, :], in_=ygt[yb][:, :]), r=[bygt[yb]], w=[o1], dma=True)
            P.outs += [o1]
            p7, b7 = P.ps()
            S.op('pe', lambda e, p7=p7: e.matmul(p7[:, 0:256], lhsT=B_tm[:, :], rhs=xdtd[:, :], start=True, stop=True), r=[btm, bxdt], w=[b7])
            for hh in range(4):
                cs = slice(hh * 64, (hh + 1) * 64)
                S.op('dve', lambda e, hh=hh, cs=cs, p7=p7: e.scalar_tensor_tensor(out=h[:, cs], in0=h[:, cs], scalar=ex[:, 4 + hh:5 + hh], op0=ALU.mult, in1=p7[:, cs], op1=ALU.add),
                     r=[b7, bex, bh], w=[bh])
            S.op('act', lambda e: e.activation(out=h_bf[:, :], in_=h[:, :], func=AF.Copy), r=[bh], w=[bhbf])
            nchunk += 1
        P.group_done(g)
    if standalone:
        return P.finish()
    return None


def _consts():
    i = np.arange(128)
    ident = np.eye(128, dtype=np.float32)
    tri = (i[:, None] <= i[None, :]).astype(np.float32)
    U = (i[:, None] > i[None, :]).astype(np.float32)
    ones = np.ones((128, 128), np.float32)
    return np.ascontiguousarray(np.stack([ident, tri, U, ones], axis=1))


def even_mixer_inputs(x_b, j, w_in, pool_w, pool_scale, conv_w, conv_b, dt_bias, a_log, d_skip):
    g = j // 2
    hs = slice(4 * j, 4 * j + 4)
    c_u = w_in[:, j * 128:(j + 1) * 128]
    c_z = w_in[:, 512 + 256 * j:512 + 256 * (j + 1)]
    xo = 1536
    c_xs = w_in[:, xo + 256 * j:xo + 256 * (j + 1)]
    c_B = w_in[:, xo + 1024 + 128 * g:xo + 1024 + 128 * (g + 1)]
    c_C = w_in[:, xo + 1280 + 128 * g:xo + 1280 + 128 * (g + 1)]
    c_dt = w_in[:, 3072 + 4 * j:3072 + 4 * (j + 1)]
    Wcat = np.concatenate([c_u, c_xs, c_B, c_C, c_z, c_dt], axis=1)
    w = 2 ** (j + 1)
    pcol = np.zeros((128, 8), np.float32)
    pcol[:, 0] = pool_scale[j * 128:(j + 1) * 128]
    for s in range(4):
        pcol[:, 1 + s] = 1.0 if s <= j else 0.0
    pcol[:, 5] = 1.0 / w
    cvw = np.zeros((128, 20), np.float32)
    cols = [np.arange(256 * j, 256 * j + 128), np.arange(256 * j + 128, 256 * j + 256),
            np.arange(1024 + 128 * g, 1024 + 128 * (g + 1)), np.arange(1280 + 128 * g, 1280 + 128 * (g + 1))]
    for i, cc in enumerate(cols):
        cvw[:, i * 5:i * 5 + 4] = conv_w[:, cc].T
        cvw[:, i * 5 + 4] = conv_b[cc]
    rows = bcast_rows(np.concatenate([dt_bias[hs], a_log[hs], d_skip[hs]]))
    rcnt = bcast_rows((1.0 / np.minimum(np.arange(16) + 1, w)).astype(np.float32))
    return dict(x=np.ascontiguousarray(x_b), w=panelize(np.ascontiguousarray(Wcat)), poolw=np.ascontiguousarray(pool_w[j]),
                pcol=pcol, cvw=cvw, rows=rows, rcnt=rcnt, consts=_consts(),
                psrow=bcast_rows(pool_scale[j * 128:(j + 1) * 128]))


NEG = -30000.0


def build_odd_mixer(L, P=None):
    standalone = P is None
    if standalone:
        P = Prog(nwb=5, ps_banks=(4, 5, 6, 7))
    S = P.S
    ng = L // G
    nblk = L // 128
    x_d = P.dram_in('x', [L, D])
    w_d = P.dram_in('w', [5, 128, 8, 512])
    qnw_d = P.dram_in('qnw', [128, 512])
    kvnw_d = P.dram_in('kvnw', [128, 256])
    nfb_d = P.dram_in('nfb', [128, 2])
    sel_d = P.dram_in('sel', [128, 8 * 70])
    rope_d = P.dram_in('rope', [2, 32, L])
    mask_d = P.dram_in('mask', [128, 4 * 512])
    cst_d = P.dram_in('consts', [128, 4, 128])
    o_d = P.dram_out('o', [L, 256])

    cst = P.sb('cst', [128, 4, 128], F32)
    bcst = Buf('cst')
    P.load('sp', cst[:, :, :], cst_d, bcst)
    ident = cst[:, 0, :]
    W = [P.wp[i][0] for i in range(5)]
    bW = Buf('w')
    for i in range(5):
        S.op('pool', lambda e, i=i: e.dma_start(out=W[i][:], in_=w_d[i]), w=[bW], dma=True)
    identb = P.sb('identb', [128, 128], BF16)
    maskz = P.sb('maskz', [128, 4 * 512], BF16)
    sel = P.sb('sel', [1, 8 * 70], BF16)
    S.op('pool', lambda e: e.dma_start(out=identb[:, :], in_=cst_d[:, 0, :]), w=[bW], dma=True)
    S.op('pool', lambda e: e.dma_start(out=maskz[:, :], in_=mask_d), w=[bW], dma=True)
    S.op('pool', lambda e: e.dma_start(out=sel[:, :], in_=sel_d[0:1, :]), w=[bW], dma=True)
    qnw = P.sb('qnw', [128, 512], F32)
    kvnw = P.sb('kvnw', [128, 256], F32)
    nfb = P.sb('nfb', [128, 2], F32)
    bpar = Buf('par')
    P.load('sp', qnw[:, :], qnw_d, bpar)
    P.load('sp', kvnw[:, :], kvnw_d, bpar)
    P.load('sp', nfb[:, :], nfb_d, bpar)
    S.op('dve', lambda e: e.tensor_scalar(out=nfb[:, :], in0=nfb[:, :], scalar1=-1.0, scalar2=None, op0=ALU.mult), r=[bpar], w=[bpar])
    one1 = P.sb('one1', [128, 1], F32)
    epsr = P.sb('epsr', [128, 1], F32)
    S.op('dve', lambda e: e.memset(one1[:, :], 1.0), w=[bpar])
    S.op('dve', lambda e: e.memset(epsr[:, :], RMS_EPS), w=[bpar])
    onesb = P.sb('onesb', [1, G], BF16)
    onesf = P.sb('onesf', [1, G], F32)
    S.op('dve', lambda e: e.memset(onesb[:, :], 1.0), w=[bpar])
    S.op('dve', lambda e: e.memset(onesf[:, :], 1.0), w=[bpar])

    xt = P.sb('xt', [128, 4, D], F32)
    bxt = [Buf(f'xt{t}') for t in range(4)]
    xT = P.sb('xT', [128, 8, G], BF16)
    bxT = [Buf(f'xT{c}') for c in range(8)]
    KA = P.sb('KA', [128, L], BF16)
    KB = P.sb('KB', [96, L], BF16)
    KfT = KA
    KX = KB
    KmT = [KA, KB]
    Vf = P.sb('Vf', [128, nblk, 2, 65], BF16)
    Vm = Vf
    bK = Buf('K')
    S.op('pool', lambda e: e.memset(Vf[:, :, :, :], 1.0), w=[bK])
    S.op('pool', lambda e: e.memset(KA[64:96, :], 0.0), w=[bK])
    S.op('pool', lambda e: e.memset(KB[64:96, :], 0.0), w=[bK])
    QmT = [P.sb(f'QmT{h}', [96, G], BF16) for h in range(2)]
    bQ = Buf('Q')
    for h_ in range(2):
        S.op('pool', lambda e, h_=h_: e.memset(QmT[h_][64:96, :], 0.0), w=[bQ])
    fv = P.sb('fv', [1, G], F32)
    fc = [[P.sb(f'fc{h}{i}', [1, G], F32) for i in range(2)] for h in range(2)]
    r1 = P.sb('r1', [1, G], F32)
    hib = P.sb('hib', [1, G], BF16)
    midb = P.sb('midb', [1, G], BF16)
    lob = P.sb('lob', [1, G], BF16)
    bf_ = Buf('f')
    bfc = [Buf('fc0'), Buf('fc1')]
    cqs = P.sb('cqs', [128, 4, 512], F32)
    ckvs = P.sb('ckvs', [128, 4, 256], F32)
    bcq = [Buf(f'cq{t}') for t in range(4)]
    bckv = [Buf(f'ckv{t}') for t in range(4)]
    sqt = P.sb('sqt', [128, 512], F32)
    bsq = Buf('sq')
    st4 = P.sb('st4', [128, 4], F32)
    bst = Buf('st')
    cqnT = P.sb('cqnT', [128, 4, G], BF16)
    ckvnT = P.sb('ckvnT', [128, 2, G], BF16)
    bcqnT = Buf('cqnT')
    bckvnT = Buf('ckvnT')
    rt = P.sb('rt', [128, 2, G], F32)
    brt = Buf('rt')
    qtmp = P.sb('qtmp', [128, G], F32)
    rA = P.sb('rA', [128, G], F32)
    rB = P.sb('rB', [128, G], F32)
    bqt = Buf('qtmp')
    PT = [P.sb(f'PT{i}', [128, G], BF16) for i in range(3)]
    bPT = [Buf(f'PT{i}') for i in range(3)]
    pti = 0
    og = P.sb('og', [128, 4, 256], F32)
    bog = [Buf(f'og{i}') for i in range(4)]
    rc = P.sb('rc', [128, 1], F32)
    brc = Buf('rc')
    SCM = 96.0 ** -0.5

    def proj_fm(wt, c0, m, rhs, brhs, nk):
        pp, bp = P.ps()
        for k in range(nk):
            S.op('pe', lambda e, k=k, pp=pp: e.matmul(pp[0:m, :], lhsT=wt[:, k, c0:c0 + m], rhs=rhs[:, k, :], start=(k == 0), stop=(k == nk - 1)),
                 r=[bW] + brhs, w=[bp], signal=(k == nk - 1))
        return pp, bp

    for ph in ('f', 'm'):
        for g in range(ng):
            gs = slice(g * G, (g + 1) * G)
            for t in range(4):
                r0 = g * G + t * 128
                P.load('sp', xt[:, t, :], x_d[P.xr(r0):P.xr(r0) + 128, :], bxt[t])
            P.load('sp', rt[64:96, 0, :], rope_d[0, :, gs], brt)
            P.load('sp', rt[64:96, 1, :], rope_d[1, :, gs], brt)
            for c in range(8):
                pt, pb = P.ps()
                for t in range(4):
                    S.op('pe', lambda e, t=t, c=c, pt=pt: e.transpose(pt[:, t * 128:(t + 1) * 128], xt[:, t, c * 128:(c + 1) * 128], ident),
                         r=[bxt[t], bcst], w=[pb], signal=(t == 3))
                if c % 2 == 0:
                    S.op('act', lambda e, c=c, pt=pt: e.activation(out=xT[:, c, :], in_=pt[:, :], func=AF.Copy), r=[pb], w=[bxT[c]])
                else:
                    S.op('dve', lambda e, c=c, pt=pt: e.tensor_copy(out=xT[:, c, :], in_=pt[:, :]), r=[pb], w=[bxT[c]])
            if ph == 'f':
                for hh in range(2):
                    pp, bp = proj_fm(W[0], hh * 64, 64, xT, bxT, 8)
                    S.op('act', lambda e, pp=pp, hh=hh: e.activation(out=QmT[hh][0:64, :], in_=pp[0:64, :], func=AF.Copy, scale=0.125), r=[bp], w=[bQ])
                    pp, bp = proj_fm(W[0], 128 + hh * 64, 64, xT, bxT, 8)
                    S.op('dve', lambda e, pp=pp, gs=gs, hh=hh: e.tensor_copy(out=KmT[hh][0:64, gs], in_=pp[0:64, :]), r=[bp], w=[bK])
                for t in range(4):
                    blk = g * 4 + t
                    pv, bv = P.ps()
                    for k in range(8):
                        S.op('pe', lambda e, k=k, t=t, pv=pv: e.matmul(pv[:, 0:128], lhsT=xT[:, k, t * 128:(t + 1) * 128], rhs=W[0][:, k, 256:384], start=(k == 0), stop=(k == 7)),
                             r=[bW] + [bxT[k]], w=[bv], signal=(k == 7))
                    for hh in range(2):
                        S.op('act' if hh == 0 else 'dve',
                             (lambda e, hh=hh, pv=pv, blk=blk: e.activation(out=Vf[:, blk, hh, 0:64], in_=pv[:, hh * 64:(hh + 1) * 64], func=AF.Copy)) if hh == 0 else
                             (lambda e, hh=hh, pv=pv, blk=blk: e.tensor_copy(out=Vf[:, blk, hh, 0:64], in_=pv[:, hh * 64:(hh + 1) * 64])),
                             r=[bv], w=[bK])
                for hh in range(2):
                    px, bpx = P.ps()
                    pk, bpk = P.ps()
                    pf, bpf = proj_fm(W[0], 384 + hh, 1, xT, bxT, 8)
                    S.op('act', lambda e, pf=pf, hh=hh: e.activation(out=fv[:, :], in_=pf[0:1, :], func=AF.Exp, bias=nfb[0:1, hh:hh + 1], scale=-1.0), r=[bpf, bpar], w=[bf_])
                    S.op('act', lambda e: e.activation(out=fv[:, :], in_=fv[:, :], func=AF.Ln, bias=one1[0:1, 0:1], scale=1.0), r=[bf_, bpar], w=[bf_])
                    cur = fc[hh][g % 2]
                    prev = fc[hh][(g + 1) % 2]
                    init = 0.0 if g == 0 else prev[0:1, G - 1:G]
                    S.op('dve', lambda e, cur=cur, init=init: e.tensor_tensor_scan(out=cur[:, :], data0=onesf[:, :], data1=fv[:, :], initial=init, op0=ALU.mult, op1=ALU.subtract),
                         r=[bf_, bfc[hh], bpar], w=[bfc[hh]])
                    S.op('dve', lambda e, cur=cur: e.tensor_copy(out=hib[:, :], in_=cur[:, :]), r=[bfc[hh]], w=[bf_])
                    S.op('dve', lambda e, cur=cur: e.tensor_tensor(out=r1[:, :], in0=cur[:, :], in1=hib[:, :], op=ALU.subtract), r=[bfc[hh], bf_], w=[bf_])
                    S.op('dve', lambda e: e.tensor_copy(out=midb[:, :], in_=r1[:, :]), r=[bf_], w=[bf_])
                    S.op('dve', lambda e: e.tensor_tensor(out=r1[:, :], in0=r1[:, :], in1=midb[:, :], op=ALU.subtract), r=[bf_], w=[bf_])
                    S.op('dve', lambda e: e.tensor_copy(out=lob[:, :], in_=r1[:, :]), r=[bf_], w=[bf_])
                    srcs = [hib, midb, lob, onesb]
                    for i in range(4):
                        S.op('pe', lambda e, i=i, px=px, srcs=srcs: e.matmul(px[0:70, :], lhsT=sel[0:1, i * 70:(i + 1) * 70], rhs=srcs[i][0:1, :],
                                                                           start=(i == 0), stop=(i == 3)), r=[bW, bf_, bpar], w=[bpx], signal=(i == 3))
                    ksrcs = [onesb, hib, midb, lob]
                    for i in range(4):
                        S.op('pe', lambda e, i=i, pk=pk, ksrcs=ksrcs: e.matmul(pk[0:70, :], lhsT=sel[0:1, (4 + i) * 70:(5 + i) * 70], rhs=ksrcs[i][0:1, :],
                                                                             start=(i == 0), stop=(i == 3)), r=[bW, bf_, bpar], w=[bpk], signal=True)
                    S.op('act', lambda e, px=px, hh=hh: e.activation(out=QmT[hh][64:70, :], in_=px[64:70, :], func=AF.Copy), r=[bpx], w=[bQ])
                    S.op('dve', lambda e, pk=pk, gs=gs, hh=hh: e.tensor_copy(out=KmT[hh][64:70, gs], in_=pk[64:70, :]), r=[bpk], w=[bK])
            if ph == 'm':
                for t in range(4):
                    pq, bq = P.ps()
                    for k in range(8):
                        S.op('pe', lambda e, k=k, t=t, pq=pq: e.matmul(pq[:, :], lhsT=xT[:, k, t * 128:(t + 1) * 128], rhs=W[1][:, k, :], start=(k == 0), stop=(k == 7)),
                             r=[bW, bxT[k]], w=[bq], signal=(k == 7))
                    S.op('act', lambda e, t=t, pq=pq: e.activation(out=cqs[:, t, :], in_=pq[:, :], func=AF.Copy), r=[bq], w=[bcq[t]])
                    pc, bc = P.ps()
                    for k in range(8):
                        S.op('pe', lambda e, k=k, t=t, pc=pc: e.matmul(pc[:, 0:256], lhsT=xT[:, k, t * 128:(t + 1) * 128], rhs=W[2][:, k, 0:256], start=(k == 0), stop=(k == 7)),
                             r=[bW, bxT[k]], w=[bc], signal=(k == 7))
                    S.op('act', lambda e, t=t, pc=pc: e.activation(out=ckvs[:, t, :], in_=pc[:, 0:256], func=AF.Copy), r=[bc], w=[bckv[t]])
                    for (src, bsrc, n, nwt) in ((cqs, bcq, 512, qnw), (ckvs, bckv, 256, kvnw)):
                        S.op('pool', lambda e, src=src, t=t, n=n: e.tensor_tensor(out=sqt[:, 0:n], in0=src[:, t, :], in1=src[:, t, :], op=ALU.mult), r=[bsrc[t]], w=[bsq])
                        S.op('dve', lambda e, n=n: e.tensor_reduce(out=st4[:, 0:1], in_=sqt[:, 0:n], axis=AX.X, op=ALU.add), r=[bsq], w=[bst])
                        S.op('act', lambda e, n=n: e.activation(out=st4[:, 1:2], in_=st4[:, 0:1], func=AF.Sqrt, bias=epsr[:, 0:1], scale=1.0 / n), r=[bst, bpar], w=[bst])
                        S.op('dve', lambda e: e.reciprocal(out=st4[:, 2:3], in_=st4[:, 1:2]), r=[bst], w=[bst])
                        S.op('act', lambda e, src=src, t=t: e.activation(out=src[:, t, :], in_=src[:, t, :], func=AF.Identity, scale=st4[:, 2:3]), r=[bst, bsrc[t]], w=[bsrc[t]])
                        S.op('dve', lambda e, src=src, t=t, nwt=nwt: e.tensor_tensor(out=src[:, t, :], in0=src[:, t, :], in1=nwt[:, :], op=ALU.mult), r=[bsrc[t], bpar], w=[bsrc[t]])
                for c in range(4):
                    pt, pb = P.ps()
                    for t in range(4):
                        S.op('pe', lambda e, t=t, c=c, pt=pt: e.transpose(pt[:, t * 128:(t + 1) * 128], cqs[:, t, c * 128:(c + 1) * 128], ident), r=[bcq[t], bcst], w=[pb], signal=(t == 3))
                    S.op('act', lambda e, c=c, pt=pt: e.activation(out=cqnT[:, c, :], in_=pt[:, :], func=AF.Copy), r=[pb], w=[bcqnT])
                for c in range(2):
                    pt, pb = P.ps()
                    for t in range(4):
                        S.op('pe', lambda e, t=t, c=c, pt=pt: e.transpose(pt[:, t * 128:(t + 1) * 128], ckvs[:, t, c * 128:(c + 1) * 128], ident), r=[bckv[t], bcst], w=[pb], signal=(t == 3))
                    S.op('dve', lambda e, c=c, pt=pt: e.tensor_copy(out=ckvnT[:, c, :], in_=pt[:, :]), r=[pb], w=[bckvnT])
                for hh in range(2):
                    pm, bm = proj_fm(W[3], hh * 192, 96, cqnT, [bcqnT], 4)
                    pp2, bp2 = proj_fm(W[3], hh * 192 + 96, 96, cqnT, [bcqnT], 4)
                    S.op('act', lambda e, pm=pm: e.activation(out=qtmp[0:64, :], in_=pm[0:64, :], func=AF.Copy), r=[bm], w=[bqt])
                    S.op('dve', lambda e, pm=pm: e.tensor_tensor(out=rA[64:96, :], in0=pm[64:96, :], in1=rt[64:96, 0, :], op=ALU.mult), r=[bm, brt], w=[bqt])
                    S.op('dve', lambda e, pp2=pp2: e.tensor_tensor(out=rB[64:96, :], in0=pp2[64:96, :], in1=rt[64:96, 1, :], op=ALU.mult), r=[bp2, brt], w=[bqt])
                    S.op('dve', lambda e: e.tensor_tensor(out=qtmp[64:96, :], in0=rA[64:96, :], in1=rB[64:96, :], op=ALU.add), r=[bqt], w=[bqt])
                    S.op('act', lambda e, hh=hh: e.activation(out=QmT[hh][0:64, :], in_=qtmp[0:64, :], func=AF.Copy, scale=SCM), r=[bqt], w=[bQ])
                    S.op('act', lambda e, hh=hh: e.activation(out=QmT[hh][64:96, :], in_=qtmp[64:96, :], func=AF.Copy, scale=SCM), r=[bqt], w=[bQ])
                    pn, bn = proj_fm(W[4], hh * 64, 64, ckvnT, [bckvnT], 2)
                    S.op('dve', lambda e, hh=hh, pn=pn, gs=gs: e.tensor_copy(out=KmT[hh][0:64, gs], in_=pn[0:64, :]), r=[bn], w=[bK])
                pkr, bkr = proj_fm(W[2], 256, 96, xT, bxT, 8)
                pkp, bkp = proj_fm(W[2], 352, 96, xT, bxT, 8)
                S.op('dve', lambda e, pkr=pkr: e.tensor_tensor(out=rA[64:96, :], in0=pkr[64:96, :], in1=rt[64:96, 0, :], op=ALU.mult), r=[bkr, brt], w=[bqt])
                S.op('dve', lambda e, pkp=pkp: e.tensor_tensor(out=rB[64:96, :], in0=pkp[64:96, :], in1=rt[64:96, 1, :], op=ALU.mult), r=[bkp, brt], w=[bqt])
                for hh in range(2):
                    S.op('dve', lambda e, hh=hh, gs=gs: e.tensor_tensor(out=KmT[hh][64:96, gs], in0=rA[64:96, :], in1=rB[64:96, :], op=ALU.add), r=[bqt], w=[bK])
                for t in range(4):
                    blk = g * 4 + t
                    pv, bv = P.ps()
                    for k in range(2):
                        S.op('pe', lambda e, k=k, t=t, pv=pv: e.matmul(pv[:, 0:128], lhsT=ckvnT[:, k, t * 128:(t + 1) * 128], rhs=W[4][:, k, 128:256], start=(k == 0), stop=(k == 1)),
                             r=[bW, bckvnT], w=[bv], signal=(k == 1))
                    S.op('act', lambda e, pv=pv, blk=blk: e.activation(out=Vm[:, blk, 0, 0:64], in_=pv[:, 0:64], func=AF.Copy), r=[bv], w=[bK])
                    S.op('dve', lambda e, pv=pv, blk=blk: e.tensor_copy(out=Vm[:, blk, 1, 0:64], in_=pv[:, 64:128]), r=[bv], w=[bK])
            nkb = 4 * g + 4
            heads = [(ph, 0), (ph, 1)]
            hoff = 0 if ph == 'f' else 2
            obanks = [P.psb[i] for i in range(4)]
            blocks = [(hi_, kind, hh, j) for hi_, (kind, hh) in enumerate(heads) for j in range(nkb)]
            st = {}

            def emit_S(n):
                hi_, kind, hh, j = blocks[n]
                ks = slice(j * 128, (j + 1) * 128)
                zone = j >= 4 * g
                jj = j - 4 * g
                ps_s, bs = P.ps()
                kr_ = 96
                S.op('pe', lambda e: e.matmul(ps_s[:, :], lhsT=KmT[hh][0:kr_, ks], rhs=QmT[hh][0:kr_, :], start=True, stop=(not zone)),
                     r=[bK, bQ], w=[bs], signal=(not zone))
                if zone:
                    S.op('pe', lambda e: e.matmul(ps_s[:, :], lhsT=identb[:, :], rhs=maskz[:, jj * 512:(jj + 1) * 512], start=False, stop=True),
                         r=[bW], w=[bs], signal=True)
                st[n] = [ps_s, bs, None, None]

            def emit_exp(n):
                nonlocal pti
                ps_s, bs = st[n][0], st[n][1]
                pt_, bpt = PT[pti % 3], bPT[pti % 3]
                pti += 1
                S.op('act', lambda e: e.activation(out=pt_[:, :], in_=ps_s[:, :], func=AF.Exp), r=[bs], w=[bpt])
                st[n][2], st[n][3] = pt_, bpt

            def emit_PV(n):
                hi_, kind, hh, j = blocks[n]
                zone = j >= 4 * g
                jj = j - 4 * g
                pt_, bpt = st[n][2], st[n][3]
                V = Vf if kind == 'f' else Vm
                for i in range(4):
                    if zone and i < jj:
                        continue
                    last = (j == 4 * g + i)
                    ot, bo = obanks[i]
                    S.op('pe', lambda e, ot=ot, i=i, last=last: e.matmul(ot[:, 0:65], lhsT=pt_[:, i * 128:(i + 1) * 128], rhs=V[:, j, hh, :],
                                                                          start=(j == 0), stop=last),
                         r=[bpt, bK], w=[bo], signal=last)
                    if last:
                        S.op('dve', lambda e, ot=ot: e.reciprocal(out=rc[:, :], in_=ot[:, 64:65]), r=[bo], w=[brc])
                        S.op('act', lambda e, ot=ot, i=i, hoff=hoff: e.activation(out=og[:, i, (hoff + hi_) * 64:(hoff + hi_ + 1) * 64], in_=ot[:, 0:64],
                                                                     func=AF.Identity, scale=rc[:, 0:1]),
                             r=[bo, brc], w=[bog[i]])
                del st[n]

            LOOK = 2
            for n in range(min(LOOK, len(blocks))):
                emit_S(n)
            for n in range(len(blocks)):
                if n + LOOK < len(blocks):
                    emit_S(n + LOOK)
                emit_exp(n)
                emit_PV(n)
            for i in range(4):
                r0 = g * G + i * 128
                ob = Buf('o')
                S.op('sp', lambda e, i=i, r0=r0, hoff=hoff: e.dma_start(out=o_d[r0:r0 + 128, hoff * 64:hoff * 64 + 128], in_=og[:, i, hoff * 64:hoff * 64 + 128]), r=[bog[i]], w=[ob], dma=True)
                P.outs.append(ob)
            if ph == 'm':
                P.group_done(g)
    if standalone:
        return P.finish()
    return None


def odd_mixer_inputs(x_b, j, L, w_in, fgate_b, q_norm_w, w_uq, kv_norm_w, w_ukv):
    h0 = 2 * j
    z64 = np.zeros((1024, 64), np.float32)
    wq = w_in[:, h0 * 64:(h0 + 2) * 64]
    wk = w_in[:, 512 + h0 * 64:512 + (h0 + 2) * 64]
    wv = w_in[:, 1024 + h0 * 64:1024 + (h0 + 2) * 64]
    wfl = w_in[:, 1536 + h0:1536 + h0 + 2]
    wcq = w_in[:, 1544:2056]
    wckv = w_in[:, 2056:2312]
    wkr = w_in[:, 2312:2344]
    perm = np.concatenate([np.arange(16, 32), np.arange(0, 16)])
    P0 = np.concatenate([wq, wk, wv, wfl], axis=1)
    P2 = np.concatenate([wckv, z64, wkr, z64, wkr[:, perm]], axis=1)
    uq = []
    for hh in (h0, h0 + 1):
        blk = w_uq[:, hh * 96:(hh + 1) * 96]
        uq += [blk, np.concatenate([blk[:, :64], blk[:, 64:][:, perm]], axis=1)]
    P3 = np.concatenate(uq, axis=1)
    ukn = [w_ukv[:, hh * 128:hh * 128 + 64] for hh in (h0, h0 + 1)]
    ukv = [w_ukv[:, hh * 128 + 64:hh * 128 + 128] for hh in (h0, h0 + 1)]
    P4 = np.concatenate(ukn + ukv, axis=1)
    w = np.concatenate([panelize(np.ascontiguousarray(m)) for m in (P0, wcq, P2, P3, P4)], axis=0)
    sel = np.zeros((8, 70), np.float32)
    sel[0, 64] = 1; sel[1, 65] = 1; sel[2, 66] = 1
    sel[3, 67:70] = 1
    sel[4, 64:67] = 1
    sel[5, 67] = -1; sel[6, 68] = -1; sel[7, 69] = -1
    half = 16
    freqs = np.power(np.float32(10000.0), -np.arange(half, dtype=np.float32) / half)
    ang = np.arange(L, dtype=np.float32)[None, :] * freqs[:, None]
    cos, sin = np.cos(ang), np.sin(ang)
    rope = np.stack([np.concatenate([cos, cos], 0), np.concatenate([-sin, sin], 0)]).astype(np.float32)
    k = np.arange(128)[:, None]
    q = np.arange(512)[None, :]
    mask = np.zeros((128, 4, 512), np.float32)
    for jj in range(4):
        qi = q // 128
        mask[:, jj, :] = np.where((qi < jj) | ((qi == jj) & (k > q % 128)), NEG, 0.0)
    return dict(x=np.ascontiguousarray(x_b), w=w, qnw=bcast_rows(q_norm_w), kvnw=bcast_rows(kv_norm_w),
                nfb=bcast_rows(fgate_b[h0:h0 + 2]), sel=bcast_rows(sel.reshape(-1)), rope=np.ascontiguousarray(rope),
                mask=np.ascontiguousarray(mask.reshape(128, 2048)), consts=_consts())


WE = 392
WO = 256


def emit_tail_fused(P, kind, TS, L, xsrc, mo_all, idx, bidx, out_ap, xcol=0):
    S = P.S
    ng = TS // G
    ntile = TS // 128
    KC = 12 if kind == 'even' else 8
    npan = 4 if kind == 'even' else 2
    Wm = WE if kind == 'even' else WO
    wo_d = P.dram_in('wo', [npan, 128, 8, 512])
    wg_d = P.dram_in('wg', [6, 128, 8, 512])
    wu_d = P.dram_in('wu', [6, 128, 8, 512])
    wd_d = P.dram_in('wd', [6, 128, 8, 512])
    C = Common(P, ['ln1g', 'ln1b', 'ln2g', 'ln2b'])
    xin = P.sb('xin', [128, 4, D], F32)
    bxin = [Buf(f'xin{t}') for t in range(4)]
    mg = P.sb('mg', [128, 4, 4, Wm], F32)
    bmg = [Buf(f'mg{t}') for t in range(4)]
    mixT = P.sb('mixT', [128, KC, G], BF16)
    bmixT = [Buf(f'mixT{c}') for c in range(KC)]
    if kind == 'even':
        nw_d = P.dram_in('normw', [128, D])
        nw = P.sb('nw', [128, 4, 256], F32)
        bnw = Buf('nw')
        P.load('sp', nw[:, :, :], nw_d.rearrange("p (a b) -> p a b", a=4), bnw)
        sm = P.sb('ssm', [128, 4], F32)
        bss = Buf('ssq')
        epsr = P.sb('epsr', [128, 1], F32)
        bepsr = Buf('epsr')
        S.op('dve', lambda e: e.memset(epsr[:, :], RMS_EPS), w=[bepsr])

    def chunk_src(t, c):
        if kind == 'even':
            if c < 4:
                return mg[:, t, c, 264:392]
            c2 = c - 4
            return mg[:, t, c2 // 2, (c2 % 2) * 128:(c2 % 2) * 128 + 128]
        if c < 4:
            return mg[:, t, c, 0:128]
        return mg[:, t, c - 4, 128:256]

    def issue_gathers(g, tiles=(0, 1, 2, 3)):
        for t in tiles:
            tile_no = g * 4 + t
            P.gather(xin[:, t, :], xsrc, idx[:, xcol + tile_no:xcol + tile_no + 1], bxin[t], r=[bidx])
            for i in range(4):
                k = 16 * (1 + i) + tile_no
                P.gather(mg[:, t, i, :], mo_all, idx[:, k:k + 1], bmg[t], r=[bidx])

    issue_gathers(0)
    for g in range(ng):
        for t in range(4):
            if kind == 'even':
                S.op('dve', lambda e, t=t: e.tensor_reduce(out=sm[:, 0:1], in_=mg[:, t, :, 256], axis=AX.X, op=ALU.add), r=[bmg[t]], w=[bss])
                S.op('act', lambda e: e.activation(out=sm[:, 1:2], in_=sm[:, 0:1], func=AF.Sqrt, bias=epsr[:, 0:1], scale=1.0 / D),
                     r=[bss, bepsr], w=[bss])
                S.op('dve', lambda e: e.reciprocal(out=sm[:, 2:3], in_=sm[:, 1:2]), r=[bss], w=[bss])
                for i in range(4):
                    S.op('act', lambda e, t=t, i=i: e.activation(out=mg[:, t, i, 0:256], in_=mg[:, t, i, 0:256], func=AF.Identity, scale=sm[:, 2:3]),
                         r=[bss, bmg[t]], w=[bmg[t]])
                    S.op('dve', lambda e, t=t, i=i: e.tensor_tensor(out=mg[:, t, i, 0:256], in0=mg[:, t, i, 0:256], in1=nw[:, i, :], op=ALU.mult),
                         r=[bmg[t], bnw], w=[bmg[t]])
        for c in range(KC):
            pt, pb = P.ps()
            for t in range(4):
                S.op('pe', lambda e, t=t, c=c, pt=pt: e.transpose(pt[:, t * 128:(t + 1) * 128], chunk_src(t, c), P.ident[:, :]),
                     r=[bmg[t], P.b_ident], w=[pb], signal=(t == 3))
            if c % 2 == 0:
                S.op('act', lambda e, c=c, pt=pt: e.activation(out=mixT[:, c, :], in_=pt[:, :], func=AF.Copy), r=[pb], w=[bmixT[c]])
            else:
                S.op('dve', lambda e, c=c, pt=pt: e.tensor_copy(out=mixT[:, c, :], in_=pt[:, :]), r=[pb], w=[bmixT[c]])
        nkg = npan // 2
        for hf in range(2):
            wos = [P.load_w(wo_d, hf * nkg + kg) for kg in range(nkg)]
            for t in range(4):
                po, bo = P.ps()
                for k in range(KC):
                    wt, bw = wos[k // 8]
                    S.op('pe', lambda e, k=k, t=t, po=po, wt=wt: e.matmul(po[:, :], lhsT=mixT[:, k, t * 128:(t + 1) * 128], rhs=wt[:, k % 8, :],
                                                                         start=(k == 0), stop=(k == KC - 1)),
                         r=[bw, bmixT[k]], w=[bo], signal=(k == KC - 1))
                S.op('dve', lambda e, t=t, hf=hf, po=po: e.scalar_tensor_tensor(out=C.r[:, t, hf * 512:(hf + 1) * 512], in0=xin[:, t, hf * 512:(hf + 1) * 512],
                                                                               scalar=ALPHA, op0=ALU.mult, in1=po[:, :], op1=ALU.add),
                     r=[bo, bxin[t]], w=[C.br[t]])
        emit_tail(P, C, wg_d, wu_d, wd_d, [out_ap[g * G + t * 128: g * G + (t + 1) * 128, :] for t in range(4)],
                  mid_hook=(lambda cp, g=g: issue_gathers(g + 1, (cp,)) if cp < 4 else None) if g + 1 < ng else None)
        P.group_done(g)


def build_fused(L, nlayers=4, dbg=()):
    TS = L // 4
    P = Prog(fused=True)
    S = P.S
    nc = P.nc
    U32 = mybir.dt.uint32
    x_in = P.dram_in('x', [L, D])
    idx_d = P.dram_in('gidx', [128, 96], U32)
    out_d = P.dram_out('out', [TS, D])
    mo_e = nc.dram_tensor('mo_e', [L, WE], F32).ap()
    mo_e_all = nc.dram_tensor('mo_e_all', [4 * L, WE], F32).ap()
    mo_o = nc.dram_tensor('mo_o', [L, WO], F32).ap()
    mo_o_all = nc.dram_tensor('mo_o_all', [4 * L, WO], F32).ap()
    xg = nc.dram_tensor('xg', [TS, D], F32).ap()
    xg_all = nc.dram_tensor('xg_all', [L, D], F32).ap()
    groups = [[0, 1, 2, 3], [4, 5, 6, 7]]
    wsc = {nm: nc.dram_tensor('wsc_' + nm, [n, 128, 8, 512], BF16).ap() for nm, n in (('wo', 4), ('wg', 6), ('wu', 6), ('wd', 6))}
    idx = P.sb('gidx', [128, 96], U32)
    bidx = Buf('gidx')
    P.load('sp', idx[:, :], idx_d, bidx)
    for l in range(nlayers):
        xsrc = x_in if l == 0 else xg_all
        P.xmap = None if l == 0 else (lambda r0: ((r0 % TS) // 128) * 512 + (r0 // TS) * 128)
        xcol = 0 if l == 0 else 80
        casts = []
        for nm, n in (('wo', 4 if l % 2 == 0 else 2), ('wg', 6), ('wu', 6), ('wd', 6)):
            src = nc.dram_tensor(f'{nm}_t{l}', [n, 128, 8, 512], F32, kind="ExternalInput").ap()
            casts += [(wsc[nm][i], src[i]) for i in range(n)]

        def issue_casts(k):
            for _ in range(k):
                if casts:
                    dst, src = casts.pop(0)
                    S.op('pool', lambda e, dst=dst, src=src: e.dma_start(out=dst, in_=src), dma=True)
        P.sfx = f'_m{l}'
        if l % 2 == 0:
            P.io = {'x': xsrc, 'yg': mo_e[:, 0:257], 'ypool': mo_e[:, 264:392]}
            def hook_e(g):
                P.cc_async("AllGather", groups, mo_e[g * 512:(g + 1) * 512, :], mo_e_all[g * 2048:(g + 1) * 2048, :], r=P.outs)
                P.outs = []
                issue_casts(2)
            P.hook = hook_e
            P.begin_phase(2)
            build_even_mixer(L, P)
            P.hook = None
            issue_casts(100)
            P.end_phase()
            mo_all = mo_e_all
        else:
            P.io = {'x': xsrc, 'o': mo_o}
            def hook_o(g):
                P.cc_async("AllGather", groups, mo_o[g * 512:(g + 1) * 512, :], mo_o_all[g * 2048:(g + 1) * 2048, :], r=P.outs)
                P.outs = []
                issue_casts(2)
            P.hook = hook_o
            P.begin_phase(5, ps_banks=(4, 5, 6, 7))
            build_odd_mixer(L, P)
            P.hook = None
            issue_casts(100)
            P.end_phase()
            mo_all = mo_o_all
        P.outs = []
        P.sfx = f'_t{l}'
        P.io = dict(wsc)
        P.w_eng = 'sp'
        def hook_t(g):
            for c in range(4):
                k = g * 4 + c
                P.cc_async("AllGather", groups, xg[k * 128:(k + 1) * 128, :], xg_all[k * 512:(k + 1) * 512, :], r=P.outs)
            P.outs = []
        P.hook = hook_t if l < nlayers - 1 else None
        P.begin_phase(8)
        if 'notail' in dbg:
            tt = P.sb('tt', [128, D], F32)
            btt = Buf('tt')
            P.load('sp', tt[:, :], xsrc[0:128, :], btt)
            ob = Buf('o')
            S.op('sp', lambda e: e.dma_start(out=out_d[0:128, :], in_=tt[:, :]), r=[btt], w=[ob], dma=True)
            P.outs.append(ob)
        else:
            emit_tail_fused(P, 'even' if l % 2 == 0 else 'odd', TS, L, xsrc, mo_all, idx, bidx, out_d if l == nlayers - 1 else xg, xcol)
        if l == nlayers - 1:
            S.wait_for('sp', P.outs)
        P.hook = None
        P.w_eng = 'pool'
        P.io = {}
        P.end_phase()
    S.emit()
    return nc


def fused_inputs(inp, c, L, nlayers=4):
    TS = L // 4
    b, j = c // 4, c % 4
    x = inp['x']
    d = {'x': np.ascontiguousarray(x[b], dtype=np.float32)}
    p = np.arange(128)[:, None]
    tile = np.arange(16)[None, :]
    tok = j * TS + tile * 128 + p
    mo = [(tok // 512) * 2048 + i * 512 + (tok % 512) for i in range(4)]
    xg = tile * 512 + j * 128 + p + 0 * tok
    d['gidx'] = np.ascontiguousarray(np.concatenate([tok] + mo + [xg], axis=1).astype(np.uint32))
    for l in range(nlayers):
        i = l // 2
        if l % 2 == 0:
            m = even_mixer_inputs(x[b], j, inp['even_w_in'][i], inp['pool_w'][i], inp['pool_scale'][i], inp['conv_w'][i],
                                  inp['conv_b'][i], inp['dt_bias'][i], inp['a_log'][i], inp['d_skip'][i])
        else:
            m = odd_mixer_inputs(x[b], j, L, inp['odd_w_in'][i], inp['fgate_b'][i], inp['q_norm_w'][i], inp['w_uq'][i],
                                 inp['kv_norm_w'][i], inp['w_ukv'][i])
        m.pop('x')
        for k, v in m.items():
            d[f'{k}_m{l}'] = v
    return d


def fused_shared_inputs(inp, nlayers=4):
    sh = {}
    eye = np.eye(128, dtype=np.float32)
    for l in range(nlayers):
        i = l // 2
        t = dict(wg=panelize(inp['ffn_w_gate'][l]), wu=panelize(inp['ffn_w_up'][l]), wd=panelize(inp['ffn_w_down'][l]),
                 ident=eye, ln1g=bcast_rows(inp['ln_mix_g'][l]), ln1b=bcast_rows(inp['ln_mix_b'][l]),
                 ln2g=bcast_rows(inp['ln_ffn_g'][l]), ln2b=bcast_rows(inp['ln_ffn_b'][l]))
        if l % 2 == 0:
            t['wo'] = panelize(inp['even_w_out'][i])
            t['normw'] = bcast_rows(inp['ssm_norm_w'][i])
        else:
            t['wo'] = panelize(inp['odd_w_out'][i])
        for k, v in t.items():
            sh[f'{k}_t{l}'] = v
    return sh


def kernel(**inp):
    inp = {k: np.asarray(v) for k, v in inp.items()}
    B, L, _ = inp['x'].shape
    TS = L // 4
    nc = build_fused(L)
    sh = fused_shared_inputs(inp)
    ins = [dict(sh, **fused_inputs(inp, c, L)) for c in range(8)]
    res = run_bass_kernel_spmd(nc, ins, core_ids=list(range(8))).results
    out = np.empty((B, L, D), np.float32)
    for c in range(8):
        b, j = c // 4, c % 4
        out[b, j * TS:(j + 1) * TS] = res[c]['out']
    return out
```

```python
import contextlib
import numpy as np
import concourse.bass as bass
import concourse.mybir as mybir
from concourse.bass_utils import run_bass_kernel_spmd

F32 = mybir.dt.float32
BF16 = mybir.dt.bfloat16
AF = mybir.ActivationFunctionType
ALU = mybir.AluOpType
AX = mybir.AxisListType

D = 1024
DFF = 2816
NFF = DFF // 128
G = 512
ALPHA = 8.0 ** 0.25
LN_EPS = 1e-5
RMS_EPS = 1e-6

ENGS = ['pe', 'act', 'dve', 'pool', 'sp']
DRING = 16


class Buf:
    __slots__ = ('name', 'lw', 'rd')

    def __init__(self, name=''):
        self.name = name
        self.lw = None
        self.rd = {}


class Sched:
    def __init__(self, nc, ctx):
        self.nc = nc
        self.semh = {}
        for e in ENGS:
            self.semh[e] = ctx.enter_context(nc.semaphore('s_' + e))
        for e in ('sp', 'pool'):
            for i in range(DRING):
                self.semh[f'd_{e}{i}'] = ctx.enter_context(nc.semaphore(f'd_{e}{i}'))
        self.semh['cc'] = ctx.enter_context(nc.semaphore('s_cc'))
        self.cccnt = 0
        self.cnt = {e: 0 for e in ENGS}
        self.dcnt = {e: 0 for e in ENGS}
        self.seen = {e: {} for e in ENGS}
        self.q = {e: [] for e in ENGS}

    def _deps(self, r, w):
        deps = {}

        def add(tok):
            if tok is None:
                return
            k, v = tok
            if deps.get(k, 0) < v:
                deps[k] = v
        for b in r:
            add(b.lw)
        for b in w:
            add(b.lw)
            for k, v in b.rd.items():
                add((k, v))
        return deps

    def op(self, eng, fn, r=(), w=(), signal=True, dma=False, cc=False, cc_wait=True):
        deps = self._deps(r, w)
        waits = []
        seen = self.seen[eng]
        for k, v in deps.items():
            if eng == 'pe' and k == 'pe':
                continue
            if seen.get(k, 0) >= v:
                continue
            seen[k] = v
            waits.append((k, v))
        if cc:
            self.cccnt += 1
            tok = ('cc', self.cccnt)
            inc = 1
            sig = True
        elif dma:
            i = self.dcnt[eng]
            self.dcnt[eng] += 1
            slot, rnd = i % DRING, i // DRING
            key = f'd_{eng}{slot}'
            if rnd > 0 and seen.get(key, 0) < 16 * rnd:
                seen[key] = 16 * rnd
                waits.append((key, 16 * rnd))
            tok = (key, 16 * (rnd + 1))
            inc = 16
            sig = True
        else:
            if signal:
                self.cnt[eng] += 1
                tok = (eng, self.cnt[eng])
            else:
                tok = (eng, self.cnt[eng] + 1)
            inc = 1
            sig = signal
        self.q[eng].append((waits, fn, tok if sig else None, inc))
        if cc and cc_wait:
            seen['cc'] = tok[1]
            self.q[eng].append(([tok], None, None, 0))
        for b in w:
            b.lw = tok
            b.rd = {}
        for b in r:
            k, v = tok
            if b.rd.get(k, 0) < v:
                b.rd[k] = v
        return tok

    def wait_for(self, eng, bufs):
        deps = self._deps((), bufs)
        waits = []
        seen = self.seen[eng]
        for k, v in deps.items():
            if seen.get(k, 0) >= v:
                continue
            seen[k] = v
            waits.append((k, v))
        self.q[eng].append((waits, None, None, 0))

    def barrier(self):
        cur = {}
        for e in ENGS:
            if self.cnt[e] > 0:
                cur[e] = self.cnt[e]
        for e in ('sp', 'pool'):
            n = self.dcnt[e]
            for slot in range(DRING):
                rounds = (n - slot + DRING - 1) // DRING if n > slot else 0
                if rounds > 0:
                    cur[f'd_{e}{slot}'] = 16 * rounds
        if self.cccnt > 0:
            cur['cc'] = self.cccnt
        for e in ENGS:
            seen = self.seen[e]
            waits = []
            for k, v in cur.items():
                if k == e and e == 'pe':
                    continue
                if seen.get(k, 0) >= v:
                    continue
                seen[k] = v
                waits.append((k, v))
            self.q[e].append((waits, None, None, 0))

    def emit(self):
        semh = self.semh
        with self.nc.Block() as block:
            def mk(eng):
                items = self.q[eng]

                def body(e):
                    for waits, fn, tok, inc in items:
                        for k, v in waits:
                            e.wait_ge(semh[k], v)
                        if fn is None:
                            continue
                        ins = fn(e)
                        if tok is not None:
                            ins.then_inc(semh[tok[0]], inc)
                return body
            block.tensor(mk('pe'))
            block.scalar(mk('act'))
            block.vector(mk('dve'))
            block.gpsimd(mk('pool'))
            block.sync(mk('sp'))
        self.q = {e: [] for e in ENGS}


def panelize(W):
    K, N = W.shape
    nkc = -(-K // 128)
    nkg = -(-nkc // 8)
    ncp = -(-N // 512)
    Wp = np.zeros((nkg * 8 * 128, ncp * 512), np.float32)
    Wp[:K, :N] = W
    Wp = Wp.reshape(nkg, 8, 128, ncp, 512).transpose(3, 0, 2, 1, 4)
    return np.ascontiguousarray(Wp).reshape(ncp * nkg, 128, 8, 512)


def bcast_rows(v, n=128):
    return np.ascontiguousarray(np.broadcast_to(np.asarray(v, np.float32).reshape(1, -1), (n, v.size)))


class Prog:
    def __init__(self, name='k', nwb=8, ps_banks=(0, 1, 2, 3, 4, 5, 6, 7), fused=False):
        self.nc = bass.Bass("TRN2", target_bir_lowering=False)
        self.ps_banks = list(ps_banks)
        self.ctx = contextlib.ExitStack()
        self.pctx = None
        self.phase = 0
        self.fused = fused
        self.sfx = ''
        self.io = {}
        self.xmap = None
        self.hook = None
        self.w_eng = 'pool'
        self.S = Sched(self.nc, self.ctx)
        nc = self.nc
        self.psb = []
        for i in range(8):
            t = self.ctx.enter_context(nc.psum_tensor(f'ps{i}', [128, 512], F32))
            self.psb.append((t, Buf(f'ps{i}')))
        self.psi = 0
        self.ident = self.sb('ident_sb', [128, 128], F32)
        self.b_ident = Buf('ident')
        self.outs = []
        self.wp = []
        self.wpi = 0
        self.NWB = nwb
        if not fused:
            self.alloc_wp(nwb)

    def alloc_wp(self, n):
        self.NWB = n
        self.wp = []
        for i in range(n):
            t = self.sb(f'wp{i}', [128, 8, 512], BF16)
            self.wp.append((t, Buf(f'wp{i}')))
        self.wpi = 0

    def begin_phase(self, nwb, ps_banks=(0, 1, 2, 3, 4, 5, 6, 7)):
        self.phase += 1
        self.pctx = contextlib.ExitStack()
        self.ps_banks = list(ps_banks)
        self.psi = 0
        self.alloc_wp(nwb)

    def end_phase(self):
        self.S.barrier()
        self.S.emit()
        self.pctx.close()
        self.pctx = None

    def cc_async(self, kind, groups, src, dst, r=()):
        self.S.op('pool', lambda e: e.collective_compute(kind, ALU.bypass, replica_groups=groups, ins=[src], outs=[dst]),
                  r=list(r), cc=True, cc_wait=False)

    def collective(self, kind, groups, src, dst, rows):
        n = len(groups[0])
        nrow = src.shape[0]
        for k in range(nrow // rows):
            self.S.op('pool', lambda e, k=k: e.collective_compute(kind, ALU.bypass, replica_groups=groups,
                                                                  ins=[src[k * rows:(k + 1) * rows, :]],
                                                                  outs=[dst[k * n * rows:(k + 1) * n * rows, :]]), cc=True)
        self.S.barrier()

    def group_done(self, g):
        if self.hook is not None:
            self.hook(g)

    def xr(self, r0):
        return r0 if self.xmap is None else self.xmap(r0)

    def sb(self, name, shape, dt):
        ctx = self.pctx if self.pctx is not None else self.ctx
        return ctx.enter_context(self.nc.sbuf_tensor(f'sb{self.phase}_' + name, shape, dt))

    def dram_in(self, name, shape, dt=F32):
        if name in self.io:
            return self.io[name]
        return self.nc.dram_tensor(name + self.sfx, list(shape), dt, kind="ExternalInput").ap()

    def dram_out(self, name, shape, dt=F32):
        if name in self.io:
            return self.io[name]
        return self.nc.dram_tensor(name + self.sfx, list(shape), dt, kind="ExternalOutput").ap()

    def ps(self):
        t, b = self.psb[self.ps_banks[self.psi % len(self.ps_banks)]]
        self.psi += 1
        return t, b

    def load_w(self, wd, idx):
        t, b = self.wp[self.wpi % self.NWB]
        self.wpi += 1
        self.S.op(self.w_eng, lambda e: e.dma_start(out=t[:], in_=wd[idx]), w=[b], dma=True)
        return t, b

    def load(self, eng, dst_ap, src_ap, buf, r=()):
        return self.S.op(eng, lambda e: e.dma_start(out=dst_ap, in_=src_ap), r=list(r), w=[buf], dma=True)

    def gather(self, dst_ap, src_ap, idx_ap, buf, r=()):
        return self.S.op('pool', lambda e: e.indirect_dma_start(out=dst_ap, out_offset=None, in_=src_ap,
                                                                in_offset=bass.IndirectOffsetOnAxis(ap=idx_ap, axis=0)),
                         r=list(r), w=[buf], dma=True)

    def finish(self):
        self.S.wait_for('sp', self.outs)
        self.S.emit()
        return self.nc


def emit_ln(P, x, bx, g_t, b_t, bgb, tmp):
    S = P.S
    st, mv, sd, rstd, nmr, eps = tmp['st'], tmp['mv'], tmp['sd'], tmp['rstd'], tmp['nmr'], tmp['eps']
    bt = tmp['buf']
    S.op('dve', lambda e: e.bn_stats(out=st[:, 0, :], in_=x[:, 0:512]), r=[bx], w=[bt])
    S.op('dve', lambda e: e.bn_stats(out=st[:, 1, :], in_=x[:, 512:1024]), r=[bx], w=[bt])
    S.op('dve', lambda e: e.bn_aggr(out=mv[:, :], in_=st[:, :, :]), r=[bt], w=[bt])
    S.op('act', lambda e: e.activation(out=sd[:, :], in_=mv[:, 1:2], func=AF.Sqrt, bias=eps[:, 0:1], scale=1.0),
         r=[bt, tmp['beps']], w=[bt])
    S.op('dve', lambda e: e.reciprocal(out=rstd[:, :], in_=sd[:, :]), r=[bt], w=[bt])
    S.op('dve', lambda e: e.tensor_scalar(out=nmr[:, :], in0=mv[:, 0:1], scalar1=rstd[:, 0:1], scalar2=-1.0,
                                          op0=ALU.mult, op1=ALU.mult), r=[bt], w=[bt])
    S.op('act', lambda e: e.activation(out=x, in_=x, func=AF.Identity, bias=nmr[:, 0:1], scale=rstd[:, 0:1]),
         r=[bt, bx], w=[bx])
    S.op('dve', lambda e: e.tensor_tensor(out=x, in0=x, in1=g_t[:, :], op=ALU.mult), r=[bx, bgb], w=[bx])
    S.op('pool', lambda e: e.tensor_tensor(out=x, in0=x, in1=b_t[:, :], op=ALU.add), r=[bx, bgb], w=[bx])


def emit_transpose_group(P, xs, bxs, xT, bxT, ntile=4):
    S = P.S
    for c in range(8):
        pt, pb = P.ps()
        for t in range(ntile):
            S.op('pe', lambda e, t=t, c=c, pt=pt: e.transpose(pt[:, t * 128:(t + 1) * 128], xs[t][:, c * 128:(c + 1) * 128], P.ident[:, :]),
                 r=[bxs[t], P.b_ident], w=[pb], signal=(t == ntile - 1))
        eng = 'act' if c % 2 == 0 else 'dve'
        if eng == 'act':
            S.op('act', lambda e, c=c, pt=pt: e.activation(out=xT[:, c, 0:ntile * 128], in_=pt[:, 0:ntile * 128], func=AF.Copy),
                 r=[pb], w=[bxT[c]])
        else:
            S.op('dve', lambda e, c=c, pt=pt: e.tensor_copy(out=xT[:, c, 0:ntile * 128], in_=pt[:, 0:ntile * 128]),
                 r=[pb], w=[bxT[c]])


def emit_ffn(P, xT, bxT, hT, bhT, wg_d, wu_d, wd_d, x2, bx2, tmpsg, ntile=4, mid_hook=None):
    S = P.S
    n = ntile * 128
    ncp = -(-DFF // 512)
    for cp in range(ncp):
        wg, bwg = P.load_w(wg_d, cp)
        wu, bwu = P.load_w(wu_d, cp)
        if mid_hook is not None:
            mid_hook(cp)
        nch = min(4, NFF - cp * 4)
        for cc in range(nch):
            f = cp * 4 + cc
            pg, bg = P.ps()
            for k in range(8):
                S.op('pe', lambda e, k=k, cc=cc, pg=pg, wg=wg: e.matmul(pg[:, 0:n], lhsT=wg[:, k, cc * 128:(cc + 1) * 128], rhs=xT[:, k, 0:n],
                                                                       start=(k == 0), stop=(k == 7)),
                     r=[bwg, bxT[k]], w=[bg], signal=(k == 7))
            pu, bu = P.ps()
            for k in range(8):
                S.op('pe', lambda e, k=k, cc=cc, pu=pu, wu=wu: e.matmul(pu[:, 0:n], lhsT=wu[:, k, cc * 128:(cc + 1) * 128], rhs=xT[:, k, 0:n],
                                                                       start=(k == 0), stop=(k == 7)),
                     r=[bwu, bxT[k]], w=[bu], signal=(k == 7))
            sg, bsg = tmpsg[f % 2]
            S.op('act', lambda e, pg=pg, sg=sg: e.activation(out=sg[:, 0:n], in_=pg[:, 0:n], func=AF.Silu), r=[bg], w=[bsg])
            S.op('dve', lambda e, f=f, pu=pu, sg=sg: e.tensor_tensor(out=hT[:, f, 0:n], in0=sg[:, 0:n], in1=pu[:, 0:n], op=ALU.mult),
                 r=[bsg, bu], w=[bhT[f]])
    for hf in range(2):
        wds = [P.load_w(wd_d, hf * 3 + kg) for kg in range(3)]
        for t in range(ntile):
            po, bo = P.ps()
            for f in range(NFF):
                wt, bw = wds[f // 8]
                S.op('pe', lambda e, f=f, t=t, po=po, wt=wt: e.matmul(po[:, :], lhsT=hT[:, f, t * 128:(t + 1) * 128], rhs=wt[:, f % 8, :],
                                                                     start=(f == 0), stop=(f == NFF - 1)),
                     r=[bw, bhT[f]], w=[bo], signal=(f == NFF - 1))
            xs = x2[t][:, hf * 512:(hf + 1) * 512]
            S.op('dve', lambda e, xs=xs, po=po: e.scalar_tensor_tensor(out=xs, in0=xs, scalar=ALPHA, op0=ALU.mult, in1=po[:, :], op1=ALU.add),
                 r=[bo, bx2[t]], w=[bx2[t]])


class Common:
    def __init__(self, P, lng_names):
        nc = P.nc
        S = P.S
        self.P = P
        ident_d = P.dram_in('ident', [128, 128])
        P.load('sp', P.ident[:, :], ident_d, P.b_ident)
        self.eps = P.sb('eps', [128, 1], F32)
        self.beps = Buf('eps')
        S.op('dve', lambda e: e.memset(self.eps[:, :], LN_EPS), w=[self.beps])
        self.lnp = {}
        self.blnp = Buf('lnp')
        for nm in lng_names:
            d = P.dram_in(nm, [128, D])
            t = P.sb('t_' + nm, [128, D], F32)
            P.load('sp', t[:, :], d, self.blnp)
            self.lnp[nm] = t
        self.lntmp = {
            'st': P.sb('ln_st', [128, 2, 6], F32), 'mv': P.sb('ln_mv', [128, 2], F32),
            'sd': P.sb('ln_sd', [128, 1], F32), 'rstd': P.sb('ln_rstd', [128, 1], F32),
            'nmr': P.sb('ln_nmr', [128, 1], F32), 'eps': self.eps, 'beps': self.beps, 'buf': Buf('lntmp'),
        }
        self.xT = P.sb('xT', [128, 8, G], BF16)
        self.bxT = [Buf(f'xT{c}') for c in range(8)]
        self.hT = P.sb('hT', [128, NFF, G], BF16)
        self.bhT = [Buf(f'hT{f}') for f in range(NFF)]
        self.sg = [(P.sb(f'sg{i}', [128, G], F32), Buf(f'sg{i}')) for i in range(2)]
        self.r = P.sb('r', [128, 4, D], F32)
        self.br = [Buf(f'r{t}') for t in range(4)]


def emit_tail(P, C, wg_d, wu_d, wd_d, out_ap_tiles, mid_hook=None):
    S = P.S
    xs = [C.r[:, t, :] for t in range(4)]
    for t in range(4):
        emit_ln(P, xs[t], C.br[t], C.lnp['ln1g'], C.lnp['ln1b'], C.blnp, C.lntmp)
    emit_transpose_group(P, xs, C.br, C.xT, C.bxT)
    emit_ffn(P, C.xT, C.bxT, C.hT, C.bhT, wg_d, wu_d, wd_d, xs, C.br, C.sg, mid_hook=mid_hook)
    for t in range(4):
        emit_ln(P, xs[t], C.br[t], C.lnp['ln2g'], C.lnp['ln2b'], C.blnp, C.lntmp)
        ob = Buf('out')
        S.op('sp', lambda e, t=t: e.dma_start(out=out_ap_tiles[t], in_=xs[t]), r=[C.br[t]], w=[ob], dma=True)
        P.outs.append(ob)


def build_tail_test(ngroups):
    P = Prog()
    T = ngroups * G
    r_d = P.dram_in('r', [T, D])
    wg_d = P.dram_in('wg', [6, 128, 8, 512])
    wu_d = P.dram_in('wu', [6, 128, 8, 512])
    wd_d = P.dram_in('wd', [6, 128, 8, 512])
    C = Common(P, ['ln1g', 'ln1b', 'ln2g', 'ln2b'])
    out_d = P.dram_out('out', [T, D])
    for g in range(ngroups):
        for t in range(4):
            r0 = g * G + t * 128
            P.load('sp', C.r[:, t, :], r_d[r0:r0 + 128, :], C.br[t])
        emit_tail(P, C, wg_d, wu_d, wd_d, [out_d[g * G + t * 128: g * G + (t + 1) * 128, :] for t in range(4)])
    return P.finish()


def build_tail(kind, T):
    P = Prog()
    S = P.S
    ng = T // G
    x_d = P.dram_in('x', [T, D])
    if kind == 'even':
        yp_d = P.dram_in('ypool', [T, 512])
        yg_d = P.dram_in('yg', [T, D])
        ssq_d = P.dram_in('ssq', [T, 4])
        nw_d = P.dram_in('normw', [128, D])
        KC = 12
        npan = 4
    else:
        o_d = P.dram_in('o', [T, D])
        KC = 8
        npan = 2
    wo_d = P.dram_in('wo', [npan, 128, 8, 512])
    wg_d = P.dram_in('wg', [6, 128, 8, 512])
    wu_d = P.dram_in('wu', [6, 128, 8, 512])
    wd_d = P.dram_in('wd', [6, 128, 8, 512])
    C = Common(P, ['ln1g', 'ln1b', 'ln2g', 'ln2b'])
    out_d = P.dram_out('out', [T, D])
    xin = P.sb('xin', [128, 4, D], F32)
    bxin = [Buf(f'xin{t}') for t in range(4)]
    mix = P.sb('mix', [128, 4, KC * 128], F32)
    bmix = [Buf(f'mix{t}') for t in range(4)]
    mixT = P.sb('mixT', [128, KC, G], BF16)
    bmixT = [Buf(f'mixT{c}') for c in range(KC)]
    if kind == 'even':
        nw = P.sb('nw', [128, D], F32)
        bnw = Buf('nw')
        P.load('sp', nw[:, :], nw_d, bnw)
        ssq = P.sb('ssqt', [128, 4], F32)
        sm = P.sb('ssm', [128, 4], F32)
        bss = Buf('ssq')
        epsr = P.sb('epsr', [128, 1], F32)
        bepsr = Buf('epsr')
        S.op('dve', lambda e: e.memset(epsr[:, :], RMS_EPS), w=[bepsr])
    for g in range(ng):
        for t in range(4):
            r0 = g * G + t * 128
            P.load('sp', xin[:, t, :], x_d[r0:r0 + 128, :], bxin[t])
            if kind == 'even':
                P.load('sp', mix[:, t, 0:512], yp_d[r0:r0 + 128, :], bmix[t])
                P.load('sp', mix[:, t, 512:1536], yg_d[r0:r0 + 128, :], bmix[t])
                P.load('sp', ssq[:, :], ssq_d[r0:r0 + 128, :], bss)
                S.op('dve', lambda e: e.tensor_reduce(out=sm[:, 0:1], in_=ssq[:, :], axis=AX.X, op=ALU.add), r=[bss], w=[bss])
                S.op('act', lambda e: e.activation(out=sm[:, 1:2], in_=sm[:, 0:1], func=AF.Sqrt, bias=epsr[:, 0:1], scale=1.0 / D),
                     r=[bss, bepsr], w=[bss])
                S.op('dve', lambda e: e.reciprocal(out=sm[:, 2:3], in_=sm[:, 1:2]), r=[bss], w=[bss])
                S.op('act', lambda e, t=t: e.activation(out=mix[:, t, 512:1536], in_=mix[:, t, 512:1536], func=AF.Identity, scale=sm[:, 2:3]),
                     r=[bss, bmix[t]], w=[bmix[t]])
                S.op('dve', lambda e, t=t: e.tensor_tensor(out=mix[:, t, 512:1536], in0=mix[:, t, 512:1536], in1=nw[:, :], op=ALU.mult),
                     r=[bmix[t], bnw], w=[bmix[t]])
            else:
                P.load('sp', mix[:, t, :], o_d[r0:r0 + 128, :], bmix[t])
        for c in range(KC):
            pt, pb = P.ps()
            for t in range(4):
                S.op('pe', lambda e, t=t, c=c, pt=pt: e.transpose(pt[:, t * 128:(t + 1) * 128], mix[:, t, c * 128:(c + 1) * 128], P.ident[:, :]),
                     r=[bmix[t], P.b_ident], w=[pb], signal=(t == 3))
            if c % 2 == 0:
                S.op('act', lambda e, c=c, pt=pt: e.activation(out=mixT[:, c, :], in_=pt[:, :], func=AF.Copy), r=[pb], w=[bmixT[c]])
            else:
                S.op('dve', lambda e, c=c, pt=pt: e.tensor_copy(out=mixT[:, c, :], in_=pt[:, :]), r=[pb], w=[bmixT[c]])
        nkg = npan // 2
        for hf in range(2):
            wos = [P.load_w(wo_d, hf * nkg + kg) for kg in range(nkg)]
            for t in range(4):
                po, bo = P.ps()
                for k in range(KC):
                    wt, bw = wos[k // 8]
                    S.op('pe', lambda e, k=k, t=t, po=po, wt=wt: e.matmul(po[:, :], lhsT=mixT[:, k, t * 128:(t + 1) * 128], rhs=wt[:, k % 8, :],
                                                                         start=(k == 0), stop=(k == KC - 1)),
                         r=[bw, bmixT[k]], w=[bo], signal=(k == KC - 1))
                S.op('dve', lambda e, t=t, hf=hf, po=po: e.scalar_tensor_tensor(out=C.r[:, t, hf * 512:(hf + 1) * 512], in0=xin[:, t, hf * 512:(hf + 1) * 512],
                                                                               scalar=ALPHA, op0=ALU.mult, in1=po[:, :], op1=ALU.add),
                     r=[bo, bxin[t]], w=[C.br[t]])
        emit_tail(P, C, wg_d, wu_d, wd_d, [out_d[g * G + t * 128: g * G + (t + 1) * 128, :] for t in range(4)])
    return P.finish()


def build_even_mixer(L, P=None):
    standalone = P is None
    if standalone:
        P = Prog(nwb=2)
    S = P.S
    ng = L // G
    x_d = P.dram_in('x', [L, D])
    w_d = P.dram_in('w', [2, 128, 8, 512])
    pw_d = P.dram_in('poolw', [128, 128])
    pcol_d = P.dram_in('pcol', [128, 8])
    cvw_d = P.dram_in('cvw', [128, 20])
    rows_d = P.dram_in('rows', [128, 12])
    rcnt_d = P.dram_in('rcnt', [128, 16])
    cst_d = P.dram_in('consts', [128, 4, 128])
    psrow_d = P.dram_in('psrow', [128, 128])
    yp_d = P.dram_out('ypool', [L, 128])
    yg_d = P.dram_out('yg', [L, 257])

    cst = P.sb('cst', [128, 4, 128], F32)
    bcst = Buf('cst')
    P.load('sp', cst[:, :, :], cst_d, bcst)
    ident, tri, Um, ones = cst[:, 0, :], cst[:, 1, :], cst[:, 2, :], cst[:, 3, :]
    wA = P.wp[0][0]
    wB = P.wp[1][0]
    bW = Buf('w')
    S.op('pool', lambda e: e.dma_start(out=wA[:], in_=w_d[0]), w=[bW], dma=True)
    S.op('pool', lambda e: e.dma_start(out=wB[:], in_=w_d[1]), w=[bW], dma=True)
    pw = P.sb('pw', [128, 128], BF16)
    S.op('pool', lambda e: e.dma_start(out=pw[:, :], in_=pw_d), w=[bW], dma=True)
    pcol = P.sb('pcol', [128, 8], F32)
    cvw = P.sb('cvw', [128, 20], F32)
    rows = P.sb('rows', [128, 12], F32)
    rcnt = P.sb('rcnt', [128, 16], F32)
    bpar = Buf('par')
    P.load('sp', pcol[:, :], pcol_d, bpar)
    P.load('sp', cvw[:, :], cvw_d, bpar)
    P.load('sp', rows[:, :], rows_d, bpar)
    P.load('sp', rcnt[:, :], rcnt_d, bpar)
    arow = P.sb('arow', [128, 4], F32)
    one1 = P.sb('one1', [128, 1], F32)
    S.op('dve', lambda e: e.memset(one1[:, :], 1.0), w=[bpar])
    S.op('act', lambda e: e.activation(out=arow[:, :], in_=rows[:, 4:8], func=AF.Exp), r=[bpar], w=[bpar])
    S.op('dve', lambda e: e.tensor_scalar(out=arow[:, :], in0=arow[:, :], scalar1=-1.0, scalar2=None, op0=ALU.mult), r=[bpar], w=[bpar])

    xt = P.sb('xt', [128, 4, D], F32)
    bxt = [Buf(f'xt{t}') for t in range(4)]
    xT = P.sb('xT', [128, 8, G], BF16)
    bxT = [Buf(f'xT{c}') for c in range(8)]
    HW = 16
    uh = P.sb('uh', [128, HW + G], F32)
    buh = Buf('uh')
    pa = P.sb('pa', [128, HW + G], F32)
    pb_ = P.sb('pb', [128, HW + G], F32)
    bpab = Buf('pab')
    diff = P.sb('diff', [128, G], BF16)
    bdiff = Buf('diff')
    t16 = P.sb('t16', [128, 16], F32)
    ypg = P.sb('ypg', [128, 4, 128], F32)
    bypg = [Buf(f'ypg{t}') for t in range(4)]
    psrow = P.sb('psrow', [128, 128], F32)
    P.load('sp', psrow[:, :], psrow_d, bpar)
    cvin = [P.sb(f'cvin{i}', [128, HW + G], F32) for i in range(4)]
    bcvin = [Buf(f'cvin{i}') for i in range(4)]
    cacc = P.sb('cacc', [128, G], F32)
    bcacc = Buf('cacc')
    cvo = [P.sb(f'cvo{i}', [128, G], F32) for i in range(4)]
    bcvo = [Buf(f'cvo{i}') for i in range(4)]
    BTb = P.sb('BTb', [128, G], BF16)
    CTb = P.sb('CTb', [128, G], BF16)
    bBTb, bCTb = Buf('BTb'), Buf('CTb')
    zs = P.sb('zs', [128, 4, 256], F32)
    bzs = [Buf(f'zs{t}') for t in range(4)]
    dtv = P.sb('dtv', [128, 4, 4], F32)
    dtt = P.sb('dtt', [128, 4, 4], F32)
    da = P.sb('da', [128, 4, 4], F32)
    bdt = [Buf(f'dt{t}') for t in range(4)]
    xs_tm = P.sb('xs_tm', [128, 256], F32)
    B_tm = P.sb('B_tm', [128, 128], BF16)
    btm = Buf('tm')
    ex = P.sb('ex', [128, 8], F32)
    bex = Buf('ex')
    R = P.sb('R', [128, 512], F32)
    bR = Buf('R')
    LT = P.sb('LT', [128, 512], F32)
    bLT = Buf('LT')
    cbm = P.sb('cbm', [128, 128], F32)
    bcbm = Buf('cbm')
    MT = P.sb('MT', [128, 512], BF16)
    bMT = Buf('MT')
    xdt = P.sb('xdt', [128, 256], BF16)
    xdtd = P.sb('xdtd', [128, 256], BF16)
    bxdt = Buf('xdt')
    y_sb = P.sb('y_sb', [128, 256], F32)
    by = Buf('y')
    ygt = [P.sb(f'ygt{i}', [128, 257], F32) for i in range(2)]
    sq = P.sb('sq', [128, 256], F32)
    bsq = Buf('sq')
    ssqt = [P.sb(f'ssqt{i}', [128, 1], F32) for i in range(2)]
    bygt = [Buf(f'ygt{i}') for i in range(2)]
    h = P.sb('h', [128, 256], F32)
    h_bf = P.sb('h_bf', [128, 256], BF16)
    bh = Buf('h')
    bhbf = Buf('hbf')
    S.op('dve', lambda e: e.memset(h[:, :], 0.0), w=[bh])
    S.op('dve', lambda e: e.memset(h_bf[:, :], 0.0), w=[bhbf])
    S.op('dve', lambda e: e.memset(uh[:, 0:HW], 0.0), w=[buh])
    for i in range(4):
        S.op('dve', lambda e, i=i: e.memset(cvin[i][:, 0:HW], 0.0), w=[bcvin[i]])
    nchunk = 0
    for g in range(ng):
        for t in range(4):
            r0 = g * G + t * 128
            P.load('sp', xt[:, t, :], x_d[P.xr(r0):P.xr(r0) + 128, :], bxt[t])
        for c in range(8):
            pt, pb = P.ps()
            for t in range(4):
                S.op('pe', lambda e, t=t, c=c, pt=pt: e.transpose(pt[:, t * 128:(t + 1) * 128], xt[:, t, c * 128:(c + 1) * 128], ident),
                     r=[bxt[t], bcst], w=[pb], signal=(t == 3))
            if c % 2 == 0:
                S.op('act', lambda e, c=c, pt=pt: e.activation(out=xT[:, c, :], in_=pt[:, :], func=AF.Copy), r=[pb], w=[bxT[c]])
            else:
                S.op('dve', lambda e, c=c, pt=pt: e.tensor_copy(out=xT[:, c, :], in_=pt[:, :]), r=[pb], w=[bxT[c]])
        fm = [(wA, 0, uh, buh), (wA, 128, cvin[0], bcvin[0]), (wA, 256, cvin[1], bcvin[1]), (wA, 384, cvin[2], bcvin[2]),
              (wB, 0, cvin[3], bcvin[3])]
        for wt, c0, dst, bdst in fm:
            pp, bp = P.ps()
            for k in range(8):
                S.op('pe', lambda e, k=k, pp=pp, wt=wt, c0=c0: e.matmul(pp[:, :], lhsT=wt[:, k, c0:c0 + 128], rhs=xT[:, k, :], start=(k == 0), stop=(k == 7)),
                     r=[bW, bxT[k]], w=[bp], signal=(k == 7))
            S.op('act', lambda e, pp=pp, dst=dst: e.activation(out=dst[:, HW:HW + G], in_=pp[:, :], func=AF.Copy), r=[bp], w=[bdst])
        for t in range(4):
            pz, bz = P.ps()
            for k in range(8):
                S.op('pe', lambda e, k=k, t=t, pz=pz: e.matmul(pz[:, 0:256], lhsT=xT[:, k, t * 128:(t + 1) * 128], rhs=wB[:, k, 128:384], start=(k == 0), stop=(k == 7)),
                     r=[bW, bxT[k]], w=[bz], signal=False)
            for k in range(8):
                S.op('pe', lambda e, k=k, t=t, pz=pz: e.matmul(pz[:, 256:260], lhsT=xT[:, k, t * 128:(t + 1) * 128], rhs=wB[:, k, 384:388], start=(k == 0), stop=(k == 7)),
                     r=[bW, bxT[k]], w=[bz], signal=(k == 7))
            S.op('act', lambda e, t=t, pz=pz: e.activation(out=zs[:, t, :], in_=pz[:, 0:256], func=AF.Silu), r=[bz], w=[bzs[t]])
            S.op('dve', lambda e, t=t, pz=pz: e.tensor_tensor(out=dtv[:, t, :], in0=pz[:, 256:260], in1=rows[:, 0:4], op=ALU.add), r=[bz, bpar], w=[bdt[t]])
            S.op('act', lambda e, t=t: e.activation(out=dtv[:, t, :], in_=dtv[:, t, :], func=AF.Exp), r=[bdt[t]], w=[bdt[t]])
            S.op('act', lambda e, t=t: e.activation(out=dtt[:, t, :], in_=dtv[:, t, :], func=AF.Ln, bias=one1[:, 0:1], scale=1.0), r=[bdt[t], bpar], w=[bdt[t]])
            S.op('dve', lambda e, t=t: e.tensor_tensor(out=da[:, t, :], in0=dtt[:, t, :], in1=arow[:, :], op=ALU.mult), r=[bdt[t], bpar], w=[bdt[t]])
        S.op('dve', lambda e: e.scalar_tensor_tensor(out=pa[:, 1:HW + G], in0=uh[:, 0:HW + G - 1], scalar=pcol[:, 1:2], op0=ALU.mult, in1=uh[:, 1:HW + G], op1=ALU.add),
             r=[buh, bpar], w=[bpab])
        S.op('dve', lambda e: e.scalar_tensor_tensor(out=pb_[:, 3:HW + G], in0=pa[:, 1:HW + G - 2], scalar=pcol[:, 2:3], op0=ALU.mult, in1=pa[:, 3:HW + G], op1=ALU.add),
             r=[bpab, bpar], w=[bpab])
        S.op('dve', lambda e: e.scalar_tensor_tensor(out=pa[:, 7:HW + G], in0=pb_[:, 3:HW + G - 4], scalar=pcol[:, 3:4], op0=ALU.mult, in1=pb_[:, 7:HW + G], op1=ALU.add),
             r=[bpab, bpar], w=[bpab])
        S.op('dve', lambda e: e.scalar_tensor_tensor(out=pb_[:, 15:HW + G], in0=pa[:, 7:HW + G - 8], scalar=pcol[:, 4:5], op0=ALU.mult, in1=pa[:, 15:HW + G], op1=ALU.add),
             r=[bpab, bpar], w=[bpab])
        S.op('dve', lambda e: e.scalar_tensor_tensor(out=diff[:, :], in0=pb_[:, HW:HW + G], scalar=pcol[:, 5:6], op0=ALU.mult, in1=uh[:, HW:HW + G], op1=ALU.subtract),
             r=[bpab, buh, bpar], w=[bdiff])
        if g == 0:
            S.op('dve', lambda e: e.tensor_tensor(out=t16[:, :], in0=pb_[:, HW:HW + 16], in1=rcnt[:, :], op=ALU.mult), r=[bpab, bpar], w=[bpab])
            S.op('dve', lambda e: e.tensor_tensor(out=diff[:, 0:16], in0=t16[:, :], in1=uh[:, HW:HW + 16], op=ALU.subtract), r=[bpab, buh], w=[bdiff])
        S.op('dve', lambda e: e.tensor_copy(out=uh[:, 0:HW], in_=uh[:, G:G + HW]), r=[buh], w=[buh])
        pp, bp = P.ps()
        for t in range(4):
            S.op('pe', lambda e, pp=pp, t=t: e.matmul(pp[:, t * 128:(t + 1) * 128], lhsT=diff[:, t * 128:(t + 1) * 128], rhs=pw[:, :], start=True, stop=True),
                 r=[bW, bdiff], w=[bp], signal=(t == 3))
        for t in range(4):
            S.op('dve', lambda e, pp=pp, t=t: e.tensor_tensor(out=ypg[:, t, :], in0=pp[:, t * 128:(t + 1) * 128], in1=psrow[:, :], op=ALU.mult),
                 r=[bp, bpar], w=[bypg[t]])
            ob = Buf('o')
            r0 = g * G + t * 128
            S.op('sp', lambda e, t=t, r0=r0: e.dma_start(out=yp_d[r0:r0 + 128, :], in_=ypg[:, t, :]), r=[bypg[t]], w=[ob], dma=True)
            P.outs.append(ob)
        for i in range(4):
            ci = cvin[i]
            S.op('dve', lambda e, ci=ci, i=i: e.tensor_scalar(out=cacc[:, :], in0=ci[:, HW:HW + G], scalar1=cvw[:, i * 5 + 3:i * 5 + 4], scalar2=cvw[:, i * 5 + 4:i * 5 + 5],
                                                             op0=ALU.mult, op1=ALU.add), r=[bcvin[i], bpar], w=[bcacc])
            for kk in range(1, 4):
                S.op('dve', lambda e, ci=ci, i=i, kk=kk: e.scalar_tensor_tensor(out=cacc[:, :], in0=ci[:, HW - kk:HW + G - kk], scalar=cvw[:, i * 5 + 3 - kk:i * 5 + 4 - kk],
                                                                               op0=ALU.mult, in1=cacc[:, :], op1=ALU.add), r=[bcvin[i], bpar, bcacc], w=[bcacc])
            S.op('act', lambda e, i=i: e.activation(out=cvo[i][:, :], in_=cacc[:, :], func=AF.Silu), r=[bcacc], w=[bcvo[i]])
            S.op('dve', lambda e, ci=ci: e.tensor_copy(out=ci[:, 0:HW], in_=ci[:, G:G + HW]), r=[bcvin[i]], w=[bcvin[i]])
        S.op('pool', lambda e: e.tensor_copy(out=BTb[:, :], in_=cvo[2][:, :]), r=[bcvo[2]], w=[bBTb])
        S.op('pool', lambda e: e.tensor_copy(out=CTb[:, :], in_=cvo[3][:, :]), r=[bcvo[3]], w=[bCTb])
        for t in range(4):
            sl = slice(t * 128, (t + 1) * 128)
            pt, pb = P.ps()
            for i in range(3):
                S.op('pe', lambda e, i=i, pt=pt, sl=sl: e.transpose(pt[:, i * 128:(i + 1) * 128], cvo[i][:, sl], ident), r=[bcvo[i], bcst], w=[pb], signal=(i == 2))
            S.op('act', lambda e, pt=pt: e.activation(out=xs_tm[:, :], in_=pt[:, 0:256], func=AF.Copy), r=[pb], w=[btm])
            S.op('dve', lambda e, pt=pt: e.tensor_copy(out=B_tm[:, :], in_=pt[:, 256:384]), r=[pb], w=[btm])
            p2, b2 = P.ps()
            S.op('pe', lambda e, p2=p2, t=t: e.matmul(p2[:, 0:4], lhsT=tri, rhs=da[:, t, :], start=True, stop=True), r=[bcst, bdt[t]], w=[b2], signal=False)
            S.op('pe', lambda e, p2=p2, t=t: e.matmul(p2[:, 4:8], lhsT=ones, rhs=da[:, t, :], start=True, stop=True), r=[bcst, bdt[t]], w=[b2])
            S.op('act', lambda e, p2=p2: e.activation(out=ex[:, :], in_=p2[:, 0:8], func=AF.Exp), r=[b2], w=[bex])
            for hh in range(4):
                S.op('dve', lambda e, hh=hh, t=t: e.tensor_scalar(out=R[:, hh * 128:(hh + 1) * 128], in0=tri, scalar1=da[:, t, hh:hh + 1], scalar2=None, op0=ALU.mult),
                     r=[bcst, bdt[t]], w=[bR])
            p3, b3 = P.ps()
            S.op('pe', lambda e, p3=p3: e.matmul(p3[:, :], lhsT=Um, rhs=R[:, :], start=True, stop=True), r=[bcst, bR], w=[b3])
            S.op('act', lambda e, p3=p3: e.activation(out=LT[:, :], in_=p3[:, :], func=AF.Exp), r=[b3], w=[bLT])
            p4, b4 = P.ps()
            S.op('pe', lambda e, p4=p4, sl=sl: e.matmul(p4[:, 0:128], lhsT=BTb[:, sl], rhs=CTb[:, sl], start=True, stop=True), r=[bBTb, bCTb], w=[b4])
            S.op('dve', lambda e, p4=p4: e.tensor_tensor(out=cbm[:, :], in0=p4[:, 0:128], in1=tri, op=ALU.mult), r=[b4, bcst], w=[bcbm])
            for hh in range(4):
                S.op('dve', lambda e, hh=hh: e.tensor_tensor(out=MT[:, hh * 128:(hh + 1) * 128], in0=LT[:, hh * 128:(hh + 1) * 128], in1=cbm[:, :], op=ALU.mult), r=[bLT, bcbm], w=[bMT])
            for hh in range(4):
                cs = slice(hh * 64, (hh + 1) * 64)
                S.op('dve', lambda e, hh=hh, cs=cs, t=t: e.tensor_scalar(out=xdt[:, cs], in0=xs_tm[:, cs], scalar1=dtt[:, t, hh:hh + 1], scalar2=None, op0=ALU.mult),
                     r=[btm, bdt[t]], w=[bxdt])
                S.op('dve', lambda e, hh=hh, cs=cs, t=t: e.tensor_scalar(out=xdtd[:, cs], in0=xs_tm[:, cs], scalar1=dtt[:, t, hh:hh + 1], scalar2=LT[:, hh * 128 + 127:hh * 128 + 128],
                                                                        op0=ALU.mult, op1=ALU.mult), r=[btm, bdt[t], bLT], w=[bxdt])
            p5, b5 = P.ps()
            for hh in range(4):
                S.op('pe', lambda e, hh=hh, p5=p5: e.matmul(p5[:, hh * 64:(hh + 1) * 64], lhsT=MT[:, hh * 128:(hh + 1) * 128], rhs=xdt[:, hh * 64:(hh + 1) * 64], start=True, stop=True),
                     r=[bMT, bxdt], w=[b5], signal=(hh == 3))
            p6, b6 = P.ps()
            S.op('pe', lambda e, p6=p6, sl=sl: e.matmul(p6[:, 0:256], lhsT=CTb[:, sl], rhs=h_bf[:, :], start=True, stop=True), r=[bCTb, bhbf], w=[b6])
            S.op('act', lambda e, p5=p5: e.activation(out=y_sb[:, :], in_=p5[:, 0:256], func=AF.Copy), r=[b5], w=[by])
            for hh in range(4):
                cs = slice(hh * 64, (hh + 1) * 64)
                S.op('dve', lambda e, hh=hh, cs=cs, p6=p6: e.scalar_tensor_tensor(out=y_sb[:, cs], in0=p6[:, cs], scalar=ex[:, hh:hh + 1], op0=ALU.mult, in1=y_sb[:, cs], op1=ALU.add),
                     r=[b6, bex, by], w=[by])
                S.op('dve', lambda e, hh=hh, cs=cs: e.scalar_tensor_tensor(out=y_sb[:, cs], in0=xs_tm[:, cs], scalar=rows[:, 8 + hh:9 + hh], op0=ALU.mult, in1=y_sb[:, cs], op1=ALU.add),
                     r=[btm, bpar, by], w=[by])
            yb = nchunk % 2
            S.op('dve', lambda e, yb=yb, t=t: e.tensor_tensor(out=ygt[yb][:, 0:256], in0=y_sb[:, :], in1=zs[:, t, :], op=ALU.mult), r=[by, bzs[t]], w=[bygt[yb]])
            S.op('pool', lambda e, yb=yb: e.tensor_tensor(out=sq[:, :], in0=ygt[yb][:, 0:256], in1=ygt[yb][:, 0:256], op=ALU.mult), r=[bygt[yb]], w=[bsq])
            S.op('dve', lambda e, yb=yb: e.tensor_reduce(out=ygt[yb][:, 256:257], in_=sq[:, :], axis=AX.X, op=ALU.add), r=[bsq], w=[bygt[yb]])
            r0 = g * G + t * 128
            o1 = Buf('o')
            S.op('sp', lambda e, yb=yb, r0=r0: e.dma_start(out=yg_d[r0:r0 + 128, :], in_=ygt[yb][:, :]), r=[bygt[yb]], w=[o1], dma=True)
            P.outs += [o1]
            p7, b7 = P.ps()
            S.op('pe', lambda e, p7=p7: e.matmul(p7[:, 0:256], lhsT=B_tm[:, :], rhs=xdtd[:, :], start=True, stop=True), r=[btm, bxdt], w=[b7])
            for hh in range(4):
                cs = slice(hh * 64, (hh + 1) * 64)
                S.op('dve', lambda e, hh=hh, cs=cs, p7=p7: e.scalar_tensor_tensor(out=h[:, cs], in0=h[:, cs], scalar=ex[:, 4 + hh:5 + hh], op0=ALU.mult, in1=p7[:, cs], op1=ALU.add),
                     r=[b7, bex, bh], w=[bh])
            S.op('act', lambda e: e.activation(out=h_bf[:, :], in_=h[:, :], func=AF.Copy), r=[bh], w=[bhbf])
            nchunk += 1
        P.group_done(g)
    if standalone:
        return P.finish()
    return None


def _consts():
    i = np.arange(128)
    ident = np.eye(128, dtype=np.float32)
    tri = (i[:, None] <= i[None, :]).astype(np.float32)
    U = (i[:, None] > i[None, :]).astype(np.float32)
    ones = np.ones((128, 128), np.float32)
    return np.ascontiguousarray(np.stack([ident, tri, U, ones], axis=1))


def even_mixer_inputs(x_b, j, w_in, pool_w, pool_scale, conv_w, conv_b, dt_bias, a_log, d_skip):
    g = j // 2
    hs = slice(4 * j, 4 * j + 4)
    c_u = w_in[:, j * 128:(j + 1) * 128]
    c_z = w_in[:, 512 + 256 * j:512 + 256 * (j + 1)]
    xo = 1536
    c_xs = w_in[:, xo + 256 * j:xo + 256 * (j + 1)]
    c_B = w_in[:, xo + 1024 + 128 * g:xo + 1024 + 128 * (g + 1)]
    c_C = w_in[:, xo + 1280 + 128 * g:xo + 1280 + 128 * (g + 1)]
    c_dt = w_in[:, 3072 + 4 * j:3072 + 4 * (j + 1)]
    Wcat = np.concatenate([c_u, c_xs, c_B, c_C, c_z, c_dt], axis=1)
    w = 2 ** (j + 1)
    pcol = np.zeros((128, 8), np.float32)
    pcol[:, 0] = pool_scale[j * 128:(j + 1) * 128]
    for s in range(4):
        pcol[:, 1 + s] = 1.0 if s <= j else 0.0
    pcol[:, 5] = 1.0 / w
    cvw = np.zeros((128, 20), np.float32)
    cols = [np.arange(256 * j, 256 * j + 128), np.arange(256 * j + 128, 256 * j + 256),
            np.arange(1024 + 128 * g, 1024 + 128 * (g + 1)), np.arange(1280 + 128 * g, 1280 + 128 * (g + 1))]
    for i, cc in enumerate(cols):
        cvw[:, i * 5:i * 5 + 4] = conv_w[:, cc].T
        cvw[:, i * 5 + 4] = conv_b[cc]
    rows = bcast_rows(np.concatenate([dt_bias[hs], a_log[hs], d_skip[hs]]))
    rcnt = bcast_rows((1.0 / np.minimum(np.arange(16) + 1, w)).astype(np.float32))
    return dict(x=np.ascontiguousarray(x_b), w=panelize(np.ascontiguousarray(Wcat)), poolw=np.ascontiguousarray(pool_w[j]),
                pcol=pcol, cvw=cvw, rows=rows, rcnt=rcnt, consts=_consts(),
                psrow=bcast_rows(pool_scale[j * 128:(j + 1) * 128]))


NEG = -30000.0


def build_odd_mixer(L, P=None):
    standalone = P is None
    if standalone:
        P = Prog(nwb=5, ps_banks=(4, 5, 6, 7))
    S = P.S
    ng = L // G
    nblk = L // 128
    x_d = P.dram_in('x', [L, D])
    w_d = P.dram_in('w', [5, 128, 8, 512])
    qnw_d = P.dram_in('qnw', [128, 512])
    kvnw_d = P.dram_in('kvnw', [128, 256])
    nfb_d = P.dram_in('nfb', [128, 2])
    sel_d = P.dram_in('sel', [128, 8 * 70])
    rope_d = P.dram_in('rope', [2, 32, L])
    mask_d = P.dram_in('mask', [128, 4 * 512])
    cst_d = P.dram_in('consts', [128, 4, 128])
    o_d = P.dram_out('o', [L, 256])

    cst = P.sb('cst', [128, 4, 128], F32)
    bcst = Buf('cst')
    P.load('sp', cst[:, :, :], cst_d, bcst)
    ident = cst[:, 0, :]
    W = [P.wp[i][0] for i in range(5)]
    bW = Buf('w')
    for i in range(5):
        S.op('pool', lambda e, i=i: e.dma_start(out=W[i][:], in_=w_d[i]), w=[bW], dma=True)
    identb = P.sb('identb', [128, 128], BF16)
    maskz = P.sb('maskz', [128, 4 * 512], BF16)
    sel = P.sb('sel', [1, 8 * 70], BF16)
    S.op('pool', lambda e: e.dma_start(out=identb[:, :], in_=cst_d[:, 0, :]), w=[bW], dma=True)
    S.op('pool', lambda e: e.dma_start(out=maskz[:, :], in_=mask_d), w=[bW], dma=True)
    S.op('pool', lambda e: e.dma_start(out=sel[:, :], in_=sel_d[0:1, :]), w=[bW], dma=True)
    qnw = P.sb('qnw', [128, 512], F32)
    kvnw = P.sb('kvnw', [128, 256], F32)
    nfb = P.sb('nfb', [128, 2], F32)
    bpar = Buf('par')
    P.load('sp', qnw[:, :], qnw_d, bpar)
    P.load('sp', kvnw[:, :], kvnw_d, bpar)
    P.load('sp', nfb[:, :], nfb_d, bpar)
    S.op('dve', lambda e: e.tensor_scalar(out=nfb[:, :], in0=nfb[:, :], scalar1=-1.0, scalar2=None, op0=ALU.mult), r=[bpar], w=[bpar])
    one1 = P.sb('one1', [128, 1], F32)
    epsr = P.sb('epsr', [128, 1], F32)
    S.op('dve', lambda e: e.memset(one1[:, :], 1.0), w=[bpar])
    S.op('dve', lambda e: e.memset(epsr[:, :], RMS_EPS), w=[bpar])
    onesb = P.sb('onesb', [1, G], BF16)
    onesf = P.sb('onesf', [1, G], F32)
    S.op('dve', lambda e: e.memset(onesb[:, :], 1.0), w=[bpar])
    S.op('dve', lambda e: e.memset(onesf[:, :], 1.0), w=[bpar])

    xt = P.sb('xt', [128, 4, D], F32)
    bxt = [Buf(f'xt{t}') for t in range(4)]
    xT = P.sb('xT', [128, 8, G], BF16)
    bxT = [Buf(f'xT{c}') for c in range(8)]
    KA = P.sb('KA', [128, L], BF16)
    KB = P.sb('KB', [96, L], BF16)
    KfT = KA
    KX = KB
    KmT = [KA, KB]
    Vf = P.sb('Vf', [128, nblk, 2, 65], BF16)
    Vm = Vf
    bK = Buf('K')
    S.op('pool', lambda e: e.memset(Vf[:, :, :, :], 1.0), w=[bK])
    S.op('pool', lambda e: e.memset(KA[64:96, :], 0.0), w=[bK])
    S.op('pool', lambda e: e.memset(KB[64:96, :], 0.0), w=[bK])
    QmT = [P.sb(f'QmT{h}', [96, G], BF16) for h in range(2)]
    bQ = Buf('Q')
    for h_ in range(2):
        S.op('pool', lambda e, h_=h_: e.memset(QmT[h_][64:96, :], 0.0), w=[bQ])
    fv = P.sb('fv', [1, G], F32)
    fc = [[P.sb(f'fc{h}{i}', [1, G], F32) for i in range(2)] for h in range(2)]
    r1 = P.sb('r1', [1, G], F32)
    hib = P.sb('hib', [1, G], BF16)
    midb = P.sb('midb', [1, G], BF16)
    lob = P.sb('lob', [1, G], BF16)
    bf_ = Buf('f')
    bfc = [Buf('fc0'), Buf('fc1')]
    cqs = P.sb('cqs', [128, 4, 512], F32)
    ckvs = P.sb('ckvs', [128, 4, 256], F32)
    bcq = [Buf(f'cq{t}') for t in range(4)]
    bckv = [Buf(f'ckv{t}') for t in range(4)]
    sqt = P.sb('sqt', [128, 512], F32)
    bsq = Buf('sq')
    st4 = P.sb('st4', [128, 4], F32)
    bst = Buf('st')
    cqnT = P.sb('cqnT', [128, 4, G], BF16)
    ckvnT = P.sb('ckvnT', [128, 2, G], BF16)
    bcqnT = Buf('cqnT')
    bckvnT = Buf('ckvnT')
    rt = P.sb('rt', [128, 2, G], F32)
    brt = Buf('rt')
    qtmp = P.sb('qtmp', [128, G], F32)
    rA = P.sb('rA', [128, G], F32)
    rB = P.sb('rB', [128, G], F32)
    bqt = Buf('qtmp')
    PT = [P.sb(f'PT{i}', [128, G], BF16) for i in range(3)]
    bPT = [Buf(f'PT{i}') for i in range(3)]
    pti = 0
    og = P.sb('og', [128, 4, 256], F32)
    bog = [Buf(f'og{i}') for i in range(4)]
    rc = P.sb('rc', [128, 1], F32)
    brc = Buf('rc')
    SCM = 96.0 ** -0.5

    def proj_fm(wt, c0, m, rhs, brhs, nk):
        pp, bp = P.ps()
        for k in range(nk):
            S.op('pe', lambda e, k=k, pp=pp: e.matmul(pp[0:m, :], lhsT=wt[:, k, c0:c0 + m], rhs=rhs[:, k, :], start=(k == 0), stop=(k == nk - 1)),
                 r=[bW] + brhs, w=[bp], signal=(k == nk - 1))
        return pp, bp

    for ph in ('f', 'm'):
        for g in range(ng):
            gs = slice(g * G, (g + 1) * G)
            for t in range(4):
                r0 = g * G + t * 128
                P.load('sp', xt[:, t, :], x_d[P.xr(r0):P.xr(r0) + 128, :], bxt[t])
            P.load('sp', rt[64:96, 0, :], rope_d[0, :, gs], brt)
            P.load('sp', rt[64:96, 1, :], rope_d[1, :, gs], brt)
            for c in range(8):
                pt, pb = P.ps()
                for t in range(4):
                    S.op('pe', lambda e, t=t, c=c, pt=pt: e.transpose(pt[:, t * 128:(t + 1) * 128], xt[:, t, c * 128:(c + 1) * 128], ident),
                         r=[bxt[t], bcst], w=[pb], signal=(t == 3))
                if c % 2 == 0:
                    S.op('act', lambda e, c=c, pt=pt: e.activation(out=xT[:, c, :], in_=pt[:, :], func=AF.Copy), r=[pb], w=[bxT[c]])
                else:
                    S.op('dve', lambda e, c=c, pt=pt: e.tensor_copy(out=xT[:, c, :], in_=pt[:, :]), r=[pb], w=[bxT[c]])
            if ph == 'f':
                for hh in range(2):
                    pp, bp = proj_fm(W[0], hh * 64, 64, xT, bxT, 8)
                    S.op('act', lambda e, pp=pp, hh=hh: e.activation(out=QmT[hh][0:64, :], in_=pp[0:64, :], func=AF.Copy, scale=0.125), r=[bp], w=[bQ])
                    pp, bp = proj_fm(W[0], 128 + hh * 64, 64, xT, bxT, 8)
                    S.op('dve', lambda e, pp=pp, gs=gs, hh=hh: e.tensor_copy(out=KmT[hh][0:64, gs], in_=pp[0:64, :]), r=[bp], w=[bK])
                for t in range(4):
                    blk = g * 4 + t
                    pv, bv = P.ps()
                    for k in range(8):
                        S.op('pe', lambda e, k=k, t=t, pv=pv: e.matmul(pv[:, 0:128], lhsT=xT[:, k, t * 128:(t + 1) * 128], rhs=W[0][:, k, 256:384], start=(k == 0), stop=(k == 7)),
                             r=[bW] + [bxT[k]], w=[bv], signal=(k == 7))
                    for hh in range(2):
                        S.op('act' if hh == 0 else 'dve',
                             (lambda e, hh=hh, pv=pv, blk=blk: e.activation(out=Vf[:, blk, hh, 0:64], in_=pv[:, hh * 64:(hh + 1) * 64], func=AF.Copy)) if hh == 0 else
                             (lambda e, hh=hh, pv=pv, blk=blk: e.tensor_copy(out=Vf[:, blk, hh, 0:64], in_=pv[:, hh * 64:(hh + 1) * 64])),
                             r=[bv], w=[bK])
                for hh in range(2):
                    px, bpx = P.ps()
                    pk, bpk = P.ps()
                    pf, bpf = proj_fm(W[0], 384 + hh, 1, xT, bxT, 8)
                    S.op('act', lambda e, pf=pf, hh=hh: e.activation(out=fv[:, :], in_=pf[0:1, :], func=AF.Exp, bias=nfb[0:1, hh:hh + 1], scale=-1.0), r=[bpf, bpar], w=[bf_])
                    S.op('act', lambda e: e.activation(out=fv[:, :], in_=fv[:, :], func=AF.Ln, bias=one1[0:1, 0:1], scale=1.0), r=[bf_, bpar], w=[bf_])
                    cur = fc[hh][g % 2]
                    prev = fc[hh][(g + 1) % 2]
                    init = 0.0 if g == 0 else prev[0:1, G - 1:G]
                    S.op('dve', lambda e, cur=cur, init=init: e.tensor_tensor_scan(out=cur[:, :], data0=onesf[:, :], data1=fv[:, :], initial=init, op0=ALU.mult, op1=ALU.subtract),
                         r=[bf_, bfc[hh], bpar], w=[bfc[hh]])
                    S.op('dve', lambda e, cur=cur: e.tensor_copy(out=hib[:, :], in_=cur[:, :]), r=[bfc[hh]], w=[bf_])
                    S.op('dve', lambda e, cur=cur: e.tensor_tensor(out=r1[:, :], in0=cur[:, :], in1=hib[:, :], op=ALU.subtract), r=[bfc[hh], bf_], w=[bf_])
                    S.op('dve', lambda e: e.tensor_copy(out=midb[:, :], in_=r1[:, :]), r=[bf_], w=[bf_])
                    S.op('dve', lambda e: e.tensor_tensor(out=r1[:, :], in0=r1[:, :], in1=midb[:, :], op=ALU.subtract), r=[bf_], w=[bf_])
                    S.op('dve', lambda e: e.tensor_copy(out=lob[:, :], in_=r1[:, :]), r=[bf_], w=[bf_])
                    srcs = [hib, midb, lob, onesb]
                    for i in range(4):
                        S.op('pe', lambda e, i=i, px=px, srcs=srcs: e.matmul(px[0:70, :], lhsT=sel[0:1, i * 70:(i + 1) * 70], rhs=srcs[i][0:1, :],
                                                                           start=(i == 0), stop=(i == 3)), r=[bW, bf_, bpar], w=[bpx], signal=(i == 3))
                    ksrcs = [onesb, hib, midb, lob]
                    for i in range(4):
                        S.op('pe', lambda e, i=i, pk=pk, ksrcs=ksrcs: e.matmul(pk[0:70, :], lhsT=sel[0:1, (4 + i) * 70:(5 + i) * 70], rhs=ksrcs[i][0:1, :],
                                                                             start=(i == 0), stop=(i == 3)), r=[bW, bf_, bpar], w=[bpk], signal=True)
                    S.op('act', lambda e, px=px, hh=hh: e.activation(out=QmT[hh][64:70, :], in_=px[64:70, :], func=AF.Copy), r=[bpx], w=[bQ])
                    S.op('dve', lambda e, pk=pk, gs=gs, hh=hh: e.tensor_copy(out=KmT[hh][64:70, gs], in_=pk[64:70, :]), r=[bpk], w=[bK])
            if ph == 'm':
                for t in range(4):
                    pq, bq = P.ps()
                    for k in range(8):
                        S.op('pe', lambda e, k=k, t=t, pq=pq: e.matmul(pq[:, :], lhsT=xT[:, k, t * 128:(t + 1) * 128], rhs=W[1][:, k, :], start=(k == 0), stop=(k == 7)),
                             r=[bW, bxT[k]], w=[bq], signal=(k == 7))
                    S.op('act', lambda e, t=t, pq=pq: e.activation(out=cqs[:, t, :], in_=pq[:, :], func=AF.Copy), r=[bq], w=[bcq[t]])
                    pc, bc = P.ps()
                    for k in range(8):
                        S.op('pe', lambda e, k=k, t=t, pc=pc: e.matmul(pc[:, 0:256], lhsT=xT[:, k, t * 128:(t + 1) * 128], rhs=W[2][:, k, 0:256], start=(k == 0), stop=(k == 7)),
                             r=[bW, bxT[k]], w=[bc], signal=(k == 7))
                    S.op('act', lambda e, t=t, pc=pc: e.activation(out=ckvs[:, t, :], in_=pc[:, 0:256], func=AF.Copy), r=[bc], w=[bckv[t]])
                    for (src, bsrc, n, nwt) in ((cqs, bcq, 512, qnw), (ckvs, bckv, 256, kvnw)):
                        S.op('pool', lambda e, src=src, t=t, n=n: e.tensor_tensor(out=sqt[:, 0:n], in0=src[:, t, :], in1=src[:, t, :], op=ALU.mult), r=[bsrc[t]], w=[bsq])
                        S.op('dve', lambda e, n=n: e.tensor_reduce(out=st4[:, 0:1], in_=sqt[:, 0:n], axis=AX.X, op=ALU.add), r=[bsq], w=[bst])
                        S.op('act', lambda e, n=n: e.activation(out=st4[:, 1:2], in_=st4[:, 0:1], func=AF.Sqrt, bias=epsr[:, 0:1], scale=1.0 / n), r=[bst, bpar], w=[bst])
                        S.op('dve', lambda e: e.reciprocal(out=st4[:, 2:3], in_=st4[:, 1:2]), r=[bst], w=[bst])
                        S.op('act', lambda e, src=src, t=t: e.activation(out=src[:, t, :], in_=src[:, t, :], func=AF.Identity, scale=st4[:, 2:3]), r=[bst, bsrc[t]], w=[bsrc[t]])
                        S.op('dve', lambda e, src=src, t=t, nwt=nwt: e.tensor_tensor(out=src[:, t, :], in0=src[:, t, :], in1=nwt[:, :], op=ALU.mult), r=[bsrc[t], bpar], w=[bsrc[t]])
                for c in range(4):
                    pt, pb = P.ps()
                    for t in range(4):
                        S.op('pe', lambda e, t=t, c=c, pt=pt: e.transpose(pt[:, t * 128:(t + 1) * 128], cqs[:, t, c * 128:(c + 1) * 128], ident), r=[bcq[t], bcst], w=[pb], signal=(t == 3))
                    S.op('act', lambda e, c=c, pt=pt: e.activation(out=cqnT[:, c, :], in_=pt[:, :], func=AF.Copy), r=[pb], w=[bcqnT])
                for c in range(2):
                    pt, pb = P.ps()
                    for t in range(4):
                        S.op('pe', lambda e, t=t, c=c, pt=pt: e.transpose(pt[:, t * 128:(t + 1) * 128], ckvs[:, t, c * 128:(c + 1) * 128], ident), r=[bckv[t], bcst], w=[pb], signal=(t == 3))
                    S.op('dve', lambda e, c=c, pt=pt: e.tensor_copy(out=ckvnT[:, c, :], in_=pt[:, :]), r=[pb], w=[bckvnT])
                for hh in range(2):
                    pm, bm = proj_fm(W[3], hh * 192, 96, cqnT, [bcqnT], 4)
                    pp2, bp2 = proj_fm(W[3], hh * 192 + 96, 96, cqnT, [bcqnT], 4)
                    S.op('act', lambda e, pm=pm: e.activation(out=qtmp[0:64, :], in_=pm[0:64, :], func=AF.Copy), r=[bm], w=[bqt])
                    S.op('dve', lambda e, pm=pm: e.tensor_tensor(out=rA[64:96, :], in0=pm[64:96, :], in1=rt[64:96, 0, :], op=ALU.mult), r=[bm, brt], w=[bqt])
                    S.op('dve', lambda e, pp2=pp2: e.tensor_tensor(out=rB[64:96, :], in0=pp2[64:96, :], in1=rt[64:96, 1, :], op=ALU.mult), r=[bp2, brt], w=[bqt])
                    S.op('dve', lambda e: e.tensor_tensor(out=qtmp[64:96, :], in0=rA[64:96, :], in1=rB[64:96, :], op=ALU.add), r=[bqt], w=[bqt])
                    S.op('act', lambda e, hh=hh: e.activation(out=QmT[hh][0:64, :], in_=qtmp[0:64, :], func=AF.Copy, scale=SCM), r=[bqt], w=[bQ])
                    S.op('act', lambda e, hh=hh: e.activation(out=QmT[hh][64:96, :], in_=qtmp[64:96, :], func=AF.Copy, scale=SCM), r=[bqt], w=[bQ])
                    pn, bn = proj_fm(W[4], hh * 64, 64, ckvnT, [bckvnT], 2)
                    S.op('dve', lambda e, hh=hh, pn=pn, gs=gs: e.tensor_copy(out=KmT[hh][0:64, gs], in_=pn[0:64, :]), r=[bn], w=[bK])
                pkr, bkr = proj_fm(W[2], 256, 96, xT, bxT, 8)
                pkp, bkp = proj_fm(W[2], 352, 96, xT, bxT, 8)
                S.op('dve', lambda e, pkr=pkr: e.tensor_tensor(out=rA[64:96, :], in0=pkr[64:96, :], in1=rt[64:96, 0, :], op=ALU.mult), r=[bkr, brt], w=[bqt])
                S.op('dve', lambda e, pkp=pkp: e.tensor_tensor(out=rB[64:96, :], in0=pkp[64:96, :], in1=rt[64:96, 1, :], op=ALU.mult), r=[bkp, brt], w=[bqt])
                for hh in range(2):
                    S.op('dve', lambda e, hh=hh, gs=gs: e.tensor_tensor(out=KmT[hh][64:96, gs], in0=rA[64:96, :], in1=rB[64:96, :], op=ALU.add), r=[bqt], w=[bK])
                for t in range(4):
                    blk = g * 4 + t
                    pv, bv = P.ps()
                    for k in range(2):
                        S.op('pe', lambda e, k=k, t=t, pv=pv: e.matmul(pv[:, 0:128], lhsT=ckvnT[:, k, t * 128:(t + 1) * 128], rhs=W[4][:, k, 128:256], start=(k == 0), stop=(k == 1)),
                             r=[bW, bckvnT], w=[bv], signal=(k == 1))
                    S.op('act', lambda e, pv=pv, blk=blk: e.activation(out=Vm[:, blk, 0, 0:64], in_=pv[:, 0:64], func=AF.Copy), r=[bv], w=[bK])
                    S.op('dve', lambda e, pv=pv, blk=blk: e.tensor_copy(out=Vm[:, blk, 1, 0:64], in_=pv[:, 64:128]), r=[bv], w=[bK])
            nkb = 4 * g + 4
            heads = [(ph, 0), (ph, 1)]
            hoff = 0 if ph == 'f' else 2
            obanks = [P.psb[i] for i in range(4)]
            blocks = [(hi_, kind, hh, j) for hi_, (kind, hh) in enumerate(heads) for j in range(nkb)]
            st = {}

            def emit_S(n):
                hi_, kind, hh, j = blocks[n]
                ks = slice(j * 128, (j + 1) * 128)
                zone = j >= 4 * g
                jj = j - 4 * g
                ps_s, bs = P.ps()
                kr_ = 96
                S.op('pe', lambda e: e.matmul(ps_s[:, :], lhsT=KmT[hh][0:kr_, ks], rhs=QmT[hh][0:kr_, :], start=True, stop=(not zone)),
                     r=[bK, bQ], w=[bs], signal=(not zone))
                if zone:
                    S.op('pe', lambda e: e.matmul(ps_s[:, :], lhsT=identb[:, :], rhs=maskz[:, jj * 512:(jj + 1) * 512], start=False, stop=True),
                         r=[bW], w=[bs], signal=True)
                st[n] = [ps_s, bs, None, None]

            def emit_exp(n):
                nonlocal pti
                ps_s, bs = st[n][0], st[n][1]
                pt_, bpt = PT[pti % 3], bPT[pti % 3]
                pti += 1
                S.op('act', lambda e: e.activation(out=pt_[:, :], in_=ps_s[:, :], func=AF.Exp), r=[bs], w=[bpt])
                st[n][2], st[n][3] = pt_, bpt

            def emit_PV(n):
                hi_, kind, hh, j = blocks[n]
                zone = j >= 4 * g
                jj = j - 4 * g
                pt_, bpt = st[n][2], st[n][3]
                V = Vf if kind == 'f' else Vm
                for i in range(4):
                    if zone and i < jj:
                        continue
                    last = (j == 4 * g + i)
                    ot, bo = obanks[i]
                    S.op('pe', lambda e, ot=ot, i=i, last=last: e.matmul(ot[:, 0:65], lhsT=pt_[:, i * 128:(i + 1) * 128], rhs=V[:, j, hh, :],
                                                                          start=(j == 0), stop=last),
                         r=[bpt, bK], w=[bo], signal=last)
                    if last:
                        S.op('dve', lambda e, ot=ot: e.reciprocal(out=rc[:, :], in_=ot[:, 64:65]), r=[bo], w=[brc])
                        S.op('act', lambda e, ot=ot, i=i, hoff=hoff: e.activation(out=og[:, i, (hoff + hi_) * 64:(hoff + hi_ + 1) * 64], in_=ot[:, 0:64],
                                                                     func=AF.Identity, scale=rc[:, 0:1]),
                             r=[bo, brc], w=[bog[i]])
                del st[n]

            LOOK = 2
            for n in range(min(LOOK, len(blocks))):
                emit_S(n)
            for n in range(len(blocks)):
                if n + LOOK < len(blocks):
                    emit_S(n + LOOK)
                emit_exp(n)
                emit_PV(n)
            for i in range(4):
                r0 = g * G + i * 128
                ob = Buf('o')
                S.op('sp', lambda e, i=i, r0=r0, hoff=hoff: e.dma_start(out=o_d[r0:r0 + 128, hoff * 64:hoff * 64 + 128], in_=og[:, i, hoff * 64:hoff * 64 + 128]), r=[bog[i]], w=[ob], dma=True)
                P.outs.append(ob)
            if ph == 'm':
                P.group_done(g)
    if standalone:
        return P.finish()
    return None


def odd_mixer_inputs(x_b, j, L, w_in, fgate_b, q_norm_w, w_uq, kv_norm_w, w_ukv):
    h0 = 2 * j
    z64 = np.zeros((1024, 64), np.float32)
    wq = w_in[:, h0 * 64:(h0 + 2) * 64]
    wk = w_in[:, 512 + h0 * 64:512 + (h0 + 2) * 64]
    wv = w_in[:, 1024 + h0 * 64:1024 + (h0 + 2) * 64]
    wfl = w_in[:, 1536 + h0:1536 + h0 + 2]
    wcq = w_in[:, 1544:2056]
    wckv = w_in[:, 2056:2312]
    wkr = w_in[:, 2312:2344]
    perm = np.concatenate([np.arange(16, 32), np.arange(0, 16)])
    P0 = np.concatenate([wq, wk, wv, wfl], axis=1)
    P2 = np.concatenate([wckv, z64, wkr, z64, wkr[:, perm]], axis=1)
    uq = []
    for hh in (h0, h0 + 1):
        blk = w_uq[:, hh * 96:(hh + 1) * 96]
        uq += [blk, np.concatenate([blk[:, :64], blk[:, 64:][:, perm]], axis=1)]
    P3 = np.concatenate(uq, axis=1)
    ukn = [w_ukv[:, hh * 128:hh * 128 + 64] for hh in (h0, h0 + 1)]
    ukv = [w_ukv[:, hh * 128 + 64:hh * 128 + 128] for hh in (h0, h0 + 1)]
    P4 = np.concatenate(ukn + ukv, axis=1)
    w = np.concatenate([panelize(np.ascontiguousarray(m)) for m in (P0, wcq, P2, P3, P4)], axis=0)
    sel = np.zeros((8, 70), np.float32)
    sel[0, 64] = 1; sel[1, 65] = 1; sel[2, 66] = 1
    sel[3, 67:70] = 1
    sel[4, 64:67] = 1
    sel[5, 67] = -1; sel[6, 68] = -1; sel[7, 69] = -1
    half = 16
    freqs = np.power(np.float32(10000.0), -np.arange(half, dtype=np.float32) / half)
    ang = np.arange(L, dtype=np.float32)[None, :] * freqs[:, None]
    cos, sin = np.cos(ang), np.sin(ang)
    rope = np.stack([np.concatenate([cos, cos], 0), np.concatenate([-sin, sin], 0)]).astype(np.float32)
    k = np.arange(128)[:, None]
    q = np.arange(512)[None, :]
    mask = np.zeros((128, 4, 512), np.float32)
    for jj in range(4):
        qi = q // 128
        mask[:, jj, :] = np.where((qi < jj) | ((qi == jj) & (k > q % 128)), NEG, 0.0)
    return dict(x=np.ascontiguousarray(x_b), w=w, qnw=bcast_rows(q_norm_w), kvnw=bcast_rows(kv_norm_w),
                nfb=bcast_rows(fgate_b[h0:h0 + 2]), sel=bcast_rows(sel.reshape(-1)), rope=np.ascontiguousarray(rope),
                mask=np.ascontiguousarray(mask.reshape(128, 2048)), consts=_consts())


WE = 392
WO = 256


def emit_tail_fused(P, kind, TS, L, xsrc, mo_all, idx, bidx, out_ap, xcol=0):
    S = P.S
    ng = TS // G
    ntile = TS // 128
    KC = 12 if kind == 'even' else 8
    npan = 4 if kind == 'even' else 2
    Wm = WE if kind == 'even' else WO
    wo_d = P.dram_in('wo', [npan, 128, 8, 512])
    wg_d = P.dram_in('wg', [6, 128, 8, 512])
    wu_d = P.dram_in('wu', [6, 128, 8, 512])
    wd_d = P.dram_in('wd', [6, 128, 8, 512])
    C = Common(P, ['ln1g', 'ln1b', 'ln2g', 'ln2b'])
    xin = P.sb('xin', [128, 4, D], F32)
    bxin = [Buf(f'xin{t}') for t in range(4)]
    mg = P.sb('mg', [128, 4, 4, Wm], F32)
    bmg = [Buf(f'mg{t}') for t in range(4)]
    mixT = P.sb('mixT', [128, KC, G], BF16)
    bmixT = [Buf(f'mixT{c}') for c in range(KC)]
    if kind == 'even':
        nw_d = P.dram_in('normw', [128, D])
        nw = P.sb('nw', [128, 4, 256], F32)
        bnw = Buf('nw')
        P.load('sp', nw[:, :, :], nw_d.rearrange("p (a b) -> p a b", a=4), bnw)
        sm = P.sb('ssm', [128, 4], F32)
        bss = Buf('ssq')
        epsr = P.sb('epsr', [128, 1], F32)
        bepsr = Buf('epsr')
        S.op('dve', lambda e: e.memset(epsr[:, :], RMS_EPS), w=[bepsr])

    def chunk_src(t, c):
        if kind == 'even':
            if c < 4:
                return mg[:, t, c, 264:392]
            c2 = c - 4
            return mg[:, t, c2 // 2, (c2 % 2) * 128:(c2 % 2) * 128 + 128]
        if c < 4:
            return mg[:, t, c, 0:128]
        return mg[:, t, c - 4, 128:256]

    def issue_gathers(g, tiles=(0, 1, 2, 3)):
        for t in tiles:
            tile_no = g * 4 + t
            P.gather(xin[:, t, :], xsrc, idx[:, xcol + tile_no:xcol + tile_no + 1], bxin[t], r=[bidx])
            for i in range(4):
                k = 16 * (1 + i) + tile_no
                P.gather(mg[:, t, i, :], mo_all, idx[:, k:k + 1], bmg[t], r=[bidx])

    issue_gathers(0)
    for g in range(ng):
        for t in range(4):
            if kind == 'even':
                S.op('dve', lambda e, t=t: e.tensor_reduce(out=sm[:, 0:1], in_=mg[:, t, :, 256], axis=AX.X, op=ALU.add), r=[bmg[t]], w=[bss])
                S.op('act', lambda e: e.activation(out=sm[:, 1:2], in_=sm[:, 0:1], func=AF.Sqrt, bias=epsr[:, 0:1], scale=1.0 / D),
                     r=[bss, bepsr], w=[bss])
                S.op('dve', lambda e: e.reciprocal(out=sm[:, 2:3], in_=sm[:, 1:2]), r=[bss], w=[bss])
                for i in range(4):
                    S.op('act', lambda e, t=t, i=i: e.activation(out=mg[:, t, i, 0:256], in_=mg[:, t, i, 0:256], func=AF.Identity, scale=sm[:, 2:3]),
                         r=[bss, bmg[t]], w=[bmg[t]])
                    S.op('dve', lambda e, t=t, i=i: e.tensor_tensor(out=mg[:, t, i, 0:256], in0=mg[:, t, i, 0:256], in1=nw[:, i, :], op=ALU.mult),
                         r=[bmg[t], bnw], w=[bmg[t]])
        for c in range(KC):
            pt, pb = P.ps()
            for t in range(4):
                S.op('pe', lambda e, t=t, c=c, pt=pt: e.transpose(pt[:, t * 128:(t + 1) * 128], chunk_src(t, c), P.ident[:, :]),
                     r=[bmg[t], P.b_ident], w=[pb], signal=(t == 3))
            if c % 2 == 0:
                S.op('act', lambda e, c=c, pt=pt: e.activation(out=mixT[:, c, :], in_=pt[:, :], func=AF.Copy), r=[pb], w=[bmixT[c]])
            else:
                S.op('dve', lambda e, c=c, pt=pt: e.tensor_copy(out=mixT[:, c, :], in_=pt[:, :]), r=[pb], w=[bmixT[c]])
        nkg = npan // 2
        for hf in range(2):
            wos = [P.load_w(wo_d, hf * nkg + kg) for kg in range(nkg)]
            for t in range(4):
                po, bo = P.ps()
                for k in range(KC):
                    wt, bw = wos[k // 8]
                    S.op('pe', lambda e, k=k, t=t, po=po, wt=wt: e.matmul(po[:, :], lhsT=mixT[:, k, t * 128:(t + 1) * 128], rhs=wt[:, k % 8, :],
                                                                         start=(k == 0), stop=(k == KC - 1)),
                         r=[bw, bmixT[k]], w=[bo], signal=(k == KC - 1))
                S.op('dve', lambda e, t=t, hf=hf, po=po: e.scalar_tensor_tensor(out=C.r[:, t, hf * 512:(hf + 1) * 512], in0=xin[:, t, hf * 512:(hf + 1) * 512],
                                                                               scalar=ALPHA, op0=ALU.mult, in1=po[:, :], op1=ALU.add),
                     r=[bo, bxin[t]], w=[C.br[t]])
        emit_tail(P, C, wg_d, wu_d, wd_d, [out_ap[g * G + t * 128: g * G + (t + 1) * 128, :] for t in range(4)],
                  mid_hook=(lambda cp, g=g: issue_gathers(g + 1, (cp,)) if cp < 4 else None) if g + 1 < ng else None)
        P.group_done(g)


def build_fused(L, nlayers=4, dbg=()):
    TS = L // 4
    P = Prog(fused=True)
    S = P.S
    nc = P.nc
    U32 = mybir.dt.uint32
    x_in = P.dram_in('x', [L, D])
    idx_d = P.dram_in('gidx', [128, 96], U32)
    out_d = P.dram_out('out', [TS, D])
    mo_e = nc.dram_tensor('mo_e', [L, WE], F32).ap()
    mo_e_all = nc.dram_tensor('mo_e_all', [4 * L, WE], F32).ap()
    mo_o = nc.dram_tensor('mo_o', [L, WO], F32).ap()
    mo_o_all = nc.dram_tensor('mo_o_all', [4 * L, WO], F32).ap()
    xg = nc.dram_tensor('xg', [TS, D], F32).ap()
    xg_all = nc.dram_tensor('xg_all', [L, D], F32).ap()
    groups = [[0, 1, 2, 3], [4, 5, 6, 7]]
    wscs = [{nm: nc.dram_tensor(f'wsc{k}_' + nm, [n, 128, 8, 512], BF16).ap() for nm, n in (('wo', 4), ('wg', 6), ('wu', 6), ('wd', 6))}
            for k in range(2)]
    cast_lists = []
    for l_ in range(nlayers):
        cl = []
        for nm, n in (('wo', 4 if l_ % 2 == 0 else 2), ('wg', 6), ('wu', 6), ('wd', 6)):
            src = nc.dram_tensor(f'{nm}_t{l_}', [n, 128, 8, 512], F32, kind="ExternalInput").ap()
            cl += [(wscs[l_ % 2][nm][i], src[i]) for i in range(n)]
        cast_lists.append(cl)

    def issue_casts(l_, k):
        if l_ >= nlayers:
            return
        cl = cast_lists[l_]
        for _ in range(k):
            if cl:
                dst, src = cl.pop(0)
                S.op('pool', lambda e, dst=dst, src=src: e.dma_start(out=dst, in_=src), dma=True)
    idx = P.sb('gidx', [128, 96], U32)
    bidx = Buf('gidx')
    P.load('sp', idx[:, :], idx_d, bidx)
    for l in range(nlayers):
        xsrc = x_in if l == 0 else xg_all
        P.xmap = None if l == 0 else (lambda r0: ((r0 % TS) // 128) * 512 + (r0 // TS) * 128)
        xcol = 0 if l == 0 else 80
        wsc = wscs[l % 2]
        P.sfx = f'_m{l}'
        if l % 2 == 0:
            P.io = {'x': xsrc, 'yg': mo_e[:, 0:257], 'ypool': mo_e[:, 264:392]}
            def hook_e(g):
                P.cc_async("AllGather", groups, mo_e[g * 512:(g + 1) * 512, :], mo_e_all[g * 2048:(g + 1) * 2048, :], r=P.outs)
                P.outs = []
                issue_casts(l, 2)
            P.hook = hook_e
            P.begin_phase(2)
            build_even_mixer(L, P)
            P.hook = None
            issue_casts(l, 100)
            P.end_phase()
            mo_all = mo_e_all
        else:
            P.io = {'x': xsrc, 'o': mo_o}
            def hook_o(g):
                P.cc_async("AllGather", groups, mo_o[g * 512:(g + 1) * 512, :], mo_o_all[g * 2048:(g + 1) * 2048, :], r=P.outs)
                P.outs = []
                issue_casts(l, 2)
            P.hook = hook_o
            P.begin_phase(5, ps_banks=(4, 5, 6, 7))
            build_odd_mixer(L, P)
            P.hook = None
            issue_casts(l, 100)
            P.end_phase()
            mo_all = mo_o_all
        P.outs = []
        P.sfx = f'_t{l}'
        P.io = dict(wsc)
        P.w_eng = 'sp'
        def hook_t(g, l=l):
            for c in range(4):
                k = g * 4 + c
                P.cc_async("AllGather", groups, xg[k * 128:(k + 1) * 128, :], xg_all[k * 512:(k + 1) * 512, :], r=P.outs)
            P.outs = []
            issue_casts(l + 1, 6)
        P.hook = hook_t if l < nlayers - 1 else None
        P.begin_phase(8)
        if 'notail' in dbg:
            tt = P.sb('tt', [128, D], F32)
            btt = Buf('tt')
            P.load('sp', tt[:, :], xsrc[0:128, :], btt)
            ob = Buf('o')
            S.op('sp', lambda e: e.dma_start(out=out_d[0:128, :], in_=tt[:, :]), r=[btt], w=[ob], dma=True)
            P.outs.append(ob)
        else:
            emit_tail_fused(P, 'even' if l % 2 == 0 else 'odd', TS, L, xsrc, mo_all, idx, bidx, out_d if l == nlayers - 1 else xg, xcol)
        if l == nlayers - 1:
            S.wait_for('sp', P.outs)
        P.hook = None
        P.w_eng = 'pool'
        P.io = {}
        P.end_phase()
    S.emit()
    return nc


def fused_inputs(inp, c, L, nlayers=4):
    TS = L // 4
    b, j = c // 4, c % 4
    x = inp['x']
    d = {'x': np.ascontiguousarray(x[b], dtype=np.float32)}
    p = np.arange(128)[:, None]
    tile = np.arange(16)[None, :]
    tok = j * TS + tile * 128 + p
    mo = [(tok // 512) * 2048 + i * 512 + (tok % 512) for i in range(4)]
    xg = tile * 512 + j * 128 + p + 0 * tok
    d['gidx'] = np.ascontiguousarray(np.concatenate([tok] + mo + [xg], axis=1).astype(np.uint32))
    for l in range(nlayers):
        i = l // 2
        if l % 2 == 0:
            m = even_mixer_inputs(x[b], j, inp['even_w_in'][i], inp['pool_w'][i], inp['pool_scale'][i], inp['conv_w'][i],
                                  inp['conv_b'][i], inp['dt_bias'][i], inp['a_log'][i], inp['d_skip'][i])
        else:
            m = odd_mixer_inputs(x[b], j, L, inp['odd_w_in'][i], inp['fgate_b'][i], inp['q_norm_w'][i], inp['w_uq'][i],
                                 inp['kv_norm_w'][i], inp['w_ukv'][i])
        m.pop('x')
        for k, v in m.items():
            d[f'{k}_m{l}'] = v
    return d


def fused_shared_inputs(inp, nlayers=4):
    sh = {}
    eye = np.eye(128, dtype=np.float32)
    for l in range(nlayers):
        i = l // 2
        t = dict(wg=panelize(inp['ffn_w_gate'][l]), wu=panelize(inp['ffn_w_up'][l]), wd=panelize(inp['ffn_w_down'][l]),
                 ident=eye, ln1g=bcast_rows(inp['ln_mix_g'][l]), ln1b=bcast_rows(inp['ln_mix_b'][l]),
                 ln2g=bcast_rows(inp['ln_ffn_g'][l]), ln2b=bcast_rows(inp['ln_ffn_b'][l]))
        if l % 2 == 0:
            t['wo'] = panelize(inp['even_w_out'][i])
            t['normw'] = bcast_rows(inp['ssm_norm_w'][i])
        else:
            t['wo'] = panelize(inp['odd_w_out'][i])
        for k, v in t.items():
            sh[f'{k}_t{l}'] = v
    return sh


def kernel(**inp):
    inp = {k: np.asarray(v) for k, v in inp.items()}
    B, L, _ = inp['x'].shape
    TS = L // 4
    nc = build_fused(L)
    sh = fused_shared_inputs(inp)
    ins = [dict(sh, **fused_inputs(inp, c, L)) for c in range(8)]
    res = run_bass_kernel_spmd(nc, ins, core_ids=list(range(8))).results
    out = np.empty((B, L, D), np.float32)
    for c in range(8):
        b, j = c // 4, c % 4
        out[b, j * TS:(j + 1) * TS] = res[c]['out']
    return out
```

```python
import contextlib
import numpy as np
import concourse.bass as bass
import concourse.mybir as mybir
from concourse.bass_utils import run_bass_kernel_spmd

F32 = mybir.dt.float32
BF16 = mybir.dt.bfloat16
AF = mybir.ActivationFunctionType
ALU = mybir.AluOpType
AX = mybir.AxisListType

D = 1024
DFF = 2816
NFF = DFF // 128
G = 512
ALPHA = 8.0 ** 0.25
LN_EPS = 1e-5
RMS_EPS = 1e-6

ENGS = ['pe', 'act', 'dve', 'pool', 'sp']
DRING = 16


class Buf:
    __slots__ = ('name', 'lw', 'rd')

    def __init__(self, name=''):
        self.name = name
        self.lw = None
        self.rd = {}


class Sched:
    def __init__(self, nc, ctx):
        self.nc = nc
        self.semh = {}
        for e in ENGS:
            self.semh[e] = ctx.enter_context(nc.semaphore('s_' + e))
        for e in ('sp', 'pool'):
            for i in range(DRING):
                self.semh[f'd_{e}{i}'] = ctx.enter_context(nc.semaphore(f'd_{e}{i}'))
        self.semh['cc'] = ctx.enter_context(nc.semaphore('s_cc'))
        self.cccnt = 0
        self.cnt = {e: 0 for e in ENGS}
        self.dcnt = {e: 0 for e in ENGS}
        self.seen = {e: {} for e in ENGS}
        self.q = {e: [] for e in ENGS}

    def _deps(self, r, w):
        deps = {}

        def add(tok):
            if tok is None:
                return
            k, v = tok
            if deps.get(k, 0) < v:
                deps[k] = v
        for b in r:
            add(b.lw)
        for b in w:
            add(b.lw)
            for k, v in b.rd.items():
                add((k, v))
        return deps

    def op(self, eng, fn, r=(), w=(), signal=True, dma=False, cc=False, cc_wait=True):
        deps = self._deps(r, w)
        waits = []
        seen = self.seen[eng]
        for k, v in deps.items():
            if eng == 'pe' and k == 'pe':
                continue
            if seen.get(k, 0) >= v:
                continue
            seen[k] = v
            waits.append((k, v))
        if cc:
            self.cccnt += 1
            tok = ('cc', self.cccnt)
            inc = 1
            sig = True
        elif dma:
            i = self.dcnt[eng]
            self.dcnt[eng] += 1
            slot, rnd = i % DRING, i // DRING
            key = f'd_{eng}{slot}'
            if rnd > 0 and seen.get(key, 0) < 16 * rnd:
                seen[key] = 16 * rnd
                waits.append((key, 16 * rnd))
            tok = (key, 16 * (rnd + 1))
            inc = 16
            sig = True
        else:
            if signal:
                self.cnt[eng] += 1
                tok = (eng, self.cnt[eng])
            else:
                tok = (eng, self.cnt[eng] + 1)
            inc = 1
            sig = signal
        self.q[eng].append((waits, fn, tok if sig else None, inc))
        if cc and cc_wait:
            seen['cc'] = tok[1]
            self.q[eng].append(([tok], None, None, 0))
        for b in w:
            b.lw = tok
            b.rd = {}
        for b in r:
            k, v = tok
            if b.rd.get(k, 0) < v:
                b.rd[k] = v
        return tok

    def wait_for(self, eng, bufs):
        deps = self._deps((), bufs)
        waits = []
        seen = self.seen[eng]
        for k, v in deps.items():
            if seen.get(k, 0) >= v:
                continue
            seen[k] = v
            waits.append((k, v))
        self.q[eng].append((waits, None, None, 0))

    def barrier(self):
        cur = {}
        for e in ENGS:
            if self.cnt[e] > 0:
                cur[e] = self.cnt[e]
        for e in ('sp', 'pool'):
            n = self.dcnt[e]
            for slot in range(DRING):
                rounds = (n - slot + DRING - 1) // DRING if n > slot else 0
                if rounds > 0:
                    cur[f'd_{e}{slot}'] = 16 * rounds
        if self.cccnt > 0:
            cur['cc'] = self.cccnt
        for e in ENGS:
            seen = self.seen[e]
            waits = []
            for k, v in cur.items():
                if k == e and e == 'pe':
                    continue
                if seen.get(k, 0) >= v:
                    continue
                seen[k] = v
                waits.append((k, v))
            self.q[e].append((waits, None, None, 0))

    def emit(self):
        semh = self.semh
        with self.nc.Block() as block:
            def mk(eng):
                items = self.q[eng]

                def body(e):
                    for waits, fn, tok, inc in items:
                        for k, v in waits:
                            e.wait_ge(semh[k], v)
                        if fn is None:
                            continue
                        ins = fn(e)
                        if tok is not None:
                            ins.then_inc(semh[tok[0]], inc)
                return body
            block.tensor(mk('pe'))
            block.scalar(mk('act'))
            block.vector(mk('dve'))
            block.gpsimd(mk('pool'))
            block.sync(mk('sp'))
        self.q = {e: [] for e in ENGS}


def panelize(W):
    K, N = W.shape
    nkc = -(-K // 128)
    nkg = -(-nkc // 8)
    ncp = -(-N // 512)
    Wp = np.zeros((nkg * 8 * 128, ncp * 512), np.float32)
    Wp[:K, :N] = W
    Wp = Wp.reshape(nkg, 8, 128, ncp, 512).transpose(3, 0, 2, 1, 4)
    return np.ascontiguousarray(Wp).reshape(ncp * nkg, 128, 8, 512)


def bcast_rows(v, n=128):
    return np.ascontiguousarray(np.broadcast_to(np.asarray(v, np.float32).reshape(1, -1), (n, v.size)))


class Prog:
    def __init__(self, name='k', nwb=8, ps_banks=(0, 1, 2, 3, 4, 5, 6, 7), fused=False):
        self.nc = bass.Bass("TRN2", target_bir_lowering=False)
        self.ps_banks = list(ps_banks)
        self.ctx = contextlib.ExitStack()
        self.pctx = None
        self.phase = 0
        self.fused = fused
        self.sfx = ''
        self.io = {}
        self.xmap = None
        self.hook = None
        self.w_eng = 'pool'
        self.S = Sched(self.nc, self.ctx)
        nc = self.nc
        self.psb = []
        for i in range(8):
            t = self.ctx.enter_context(nc.psum_tensor(f'ps{i}', [128, 512], F32))
            self.psb.append((t, Buf(f'ps{i}')))
        self.psi = 0
        self.ident = self.sb('ident_sb', [128, 128], F32)
        self.b_ident = Buf('ident')
        self.outs = []
        self.wp = []
        self.wpi = 0
        self.NWB = nwb
        if not fused:
            self.alloc_wp(nwb)

    def alloc_wp(self, n):
        self.NWB = n
        self.wp = []
        for i in range(n):
            t = self.sb(f'wp{i}', [128, 8, 512], BF16)
            self.wp.append((t, Buf(f'wp{i}')))
        self.wpi = 0

    def begin_phase(self, nwb, ps_banks=(0, 1, 2, 3, 4, 5, 6, 7)):
        self.phase += 1
        self.pctx = contextlib.ExitStack()
        self.ps_banks = list(ps_banks)
        self.psi = 0
        self.alloc_wp(nwb)

    def end_phase(self):
        self.S.barrier()
        self.S.emit()
        self.pctx.close()
        self.pctx = None

    def cc_async(self, kind, groups, src, dst, r=()):
        self.S.op('pool', lambda e: e.collective_compute(kind, ALU.bypass, replica_groups=groups, ins=[src], outs=[dst]),
                  r=list(r), cc=True, cc_wait=False)

    def collective(self, kind, groups, src, dst, rows):
        n = len(groups[0])
        nrow = src.shape[0]
        for k in range(nrow // rows):
            self.S.op('pool', lambda e, k=k: e.collective_compute(kind, ALU.bypass, replica_groups=groups,
                                                                  ins=[src[k * rows:(k + 1) * rows, :]],
                                                                  outs=[dst[k * n * rows:(k + 1) * n * rows, :]]), cc=True)
        self.S.barrier()

    def group_done(self, g):
        if self.hook is not None:
            self.hook(g)

    def xr(self, r0):
        return r0 if self.xmap is None else self.xmap(r0)

    def sb(self, name, shape, dt):
        ctx = self.pctx if self.pctx is not None else self.ctx
        return ctx.enter_context(self.nc.sbuf_tensor(f'sb{self.phase}_' + name, shape, dt))

    def dram_in(self, name, shape, dt=F32):
        if name in self.io:
            return self.io[name]
        return self.nc.dram_tensor(name + self.sfx, list(shape), dt, kind="ExternalInput").ap()

    def dram_out(self, name, shape, dt=F32):
        if name in self.io:
            return self.io[name]
        return self.nc.dram_tensor(name + self.sfx, list(shape), dt, kind="ExternalOutput").ap()

    def ps(self):
        t, b = self.psb[self.ps_banks[self.psi % len(self.ps_banks)]]
        self.psi += 1
        return t, b

    def load_w(self, wd, idx):
        t, b = self.wp[self.wpi % self.NWB]
        self.wpi += 1
        self.S.op(self.w_eng, lambda e: e.dma_start(out=t[:], in_=wd[idx]), w=[b], dma=True)
        return t, b

    def load(self, eng, dst_ap, src_ap, buf, r=()):
        return self.S.op(eng, lambda e: e.dma_start(out=dst_ap, in_=src_ap), r=list(r), w=[buf], dma=True)

    def gather(self, dst_ap, src_ap, idx_ap, buf, r=()):
        return self.S.op('pool', lambda e: e.indirect_dma_start(out=dst_ap, out_offset=None, in_=src_ap,
                                                                in_offset=bass.IndirectOffsetOnAxis(ap=idx_ap, axis=0)),
                         r=list(r), w=[buf], dma=True)

    def finish(self):
        self.S.wait_for('sp', self.outs)
        self.S.emit()
        return self.nc


def emit_ln(P, x, bx, g_t, b_t, bgb, tmp):
    S = P.S
    st, mv, sd, rstd, nmr, eps = tmp['st'], tmp['mv'], tmp['sd'], tmp['rstd'], tmp['nmr'], tmp['eps']
    bt = tmp['buf']
    S.op('dve', lambda e: e.bn_stats(out=st[:, 0, :], in_=x[:, 0:512]), r=[bx], w=[bt])
    S.op('dve', lambda e: e.bn_stats(out=st[:, 1, :], in_=x[:, 512:1024]), r=[bx], w=[bt])
    S.op('dve', lambda e: e.bn_aggr(out=mv[:, :], in_=st[:, :, :]), r=[bt], w=[bt])
    S.op('act', lambda e: e.activation(out=sd[:, :], in_=mv[:, 1:2], func=AF.Sqrt, bias=eps[:, 0:1], scale=1.0),
         r=[bt, tmp['beps']], w=[bt])
    S.op('dve', lambda e: e.reciprocal(out=rstd[:, :], in_=sd[:, :]), r=[bt], w=[bt])
    S.op('dve', lambda e: e.tensor_scalar(out=nmr[:, :], in0=mv[:, 0:1], scalar1=rstd[:, 0:1], scalar2=-1.0,
                                          op0=ALU.mult, op1=ALU.mult), r=[bt], w=[bt])
    S.op('act', lambda e: e.activation(out=x, in_=x, func=AF.Identity, bias=nmr[:, 0:1], scale=rstd[:, 0:1]),
         r=[bt, bx], w=[bx])
    S.op('dve', lambda e: e.tensor_tensor(out=x, in0=x, in1=g_t[:, :], op=ALU.mult), r=[bx, bgb], w=[bx])
    S.op('pool', lambda e: e.tensor_tensor(out=x, in0=x, in1=b_t[:, :], op=ALU.add), r=[bx, bgb], w=[bx])


def emit_transpose_group(P, xs, bxs, xT, bxT, ntile=4):
    S = P.S
    for c in range(8):
        pt, pb = P.ps()
        for t in range(ntile):
            S.op('pe', lambda e, t=t, c=c, pt=pt: e.transpose(pt[:, t * 128:(t + 1) * 128], xs[t][:, c * 128:(c + 1) * 128], P.ident[:, :]),
                 r=[bxs[t], P.b_ident], w=[pb], signal=(t == ntile - 1))
        eng = 'act' if c % 2 == 0 else 'dve'
        if eng == 'act':
            S.op('act', lambda e, c=c, pt=pt: e.activation(out=xT[:, c, 0:ntile * 128], in_=pt[:, 0:ntile * 128], func=AF.Copy),
                 r=[pb], w=[bxT[c]])
        else:
            S.op('dve', lambda e, c=c, pt=pt: e.tensor_copy(out=xT[:, c, 0:ntile * 128], in_=pt[:, 0:ntile * 128]),
                 r=[pb], w=[bxT[c]])


def emit_ffn(P, xT, bxT, hT, bhT, wg_d, wu_d, wd_d, x2, bx2, tmpsg, ntile=4, mid_hook=None):
    S = P.S
    n = ntile * 128
    ncp = -(-DFF // 512)
    for cp in range(ncp):
        wg, bwg = P.load_w(wg_d, cp)
        wu, bwu = P.load_w(wu_d, cp)
        if mid_hook is not None:
            mid_hook(cp)
        nch = min(4, NFF - cp * 4)
        for cc in range(nch):
            f = cp * 4 + cc
            pg, bg = P.ps()
            for k in range(8):
                S.op('pe', lambda e, k=k, cc=cc, pg=pg, wg=wg: e.matmul(pg[:, 0:n], lhsT=wg[:, k, cc * 128:(cc + 1) * 128], rhs=xT[:, k, 0:n],
                                                                       start=(k == 0), stop=(k == 7)),
                     r=[bwg, bxT[k]], w=[bg], signal=(k == 7))
            pu, bu = P.ps()
            for k in range(8):
                S.op('pe', lambda e, k=k, cc=cc, pu=pu, wu=wu: e.matmul(pu[:, 0:n], lhsT=wu[:, k, cc * 128:(cc + 1) * 128], rhs=xT[:, k, 0:n],
                                                                       start=(k == 0), stop=(k == 7)),
                     r=[bwu, bxT[k]], w=[bu], signal=(k == 7))
            sg, bsg = tmpsg[f % 2]
            S.op('act', lambda e, pg=pg, sg=sg: e.activation(out=sg[:, 0:n], in_=pg[:, 0:n], func=AF.Silu), r=[bg], w=[bsg])
            S.op('dve', lambda e, f=f, pu=pu, sg=sg: e.tensor_tensor(out=hT[:, f, 0:n], in0=sg[:, 0:n], in1=pu[:, 0:n], op=ALU.mult),
                 r=[bsg, bu], w=[bhT[f]])
    for hf in range(2):
        wds = [P.load_w(wd_d, hf * 3 + kg) for kg in range(3)]
        for t in range(ntile):
            po, bo = P.ps()
            for f in range(NFF):
                wt, bw = wds[f // 8]
                S.op('pe', lambda e, f=f, t=t, po=po, wt=wt: e.matmul(po[:, :], lhsT=hT[:, f, t * 128:(t + 1) * 128], rhs=wt[:, f % 8, :],
                                                                     start=(f == 0), stop=(f == NFF - 1)),
                     r=[bw, bhT[f]], w=[bo], signal=(f == NFF - 1))
            xs = x2[t][:, hf * 512:(hf + 1) * 512]
            S.op('dve', lambda e, xs=xs, po=po: e.scalar_tensor_tensor(out=xs, in0=xs, scalar=ALPHA, op0=ALU.mult, in1=po[:, :], op1=ALU.add),
                 r=[bo, bx2[t]], w=[bx2[t]])


class Common:
    def __init__(self, P, lng_names):
        nc = P.nc
        S = P.S
        self.P = P
        ident_d = P.dram_in('ident', [128, 128])
        P.load('sp', P.ident[:, :], ident_d, P.b_ident)
        self.eps = P.sb('eps', [128, 1], F32)
        self.beps = Buf('eps')
        S.op('dve', lambda e: e.memset(self.eps[:, :], LN_EPS), w=[self.beps])
        self.lnp = {}
        self.blnp = Buf('lnp')
        for nm in lng_names:
            d = P.dram_in(nm, [128, D])
            t = P.sb('t_' + nm, [128, D], F32)
            P.load('sp', t[:, :], d, self.blnp)
            self.lnp[nm] = t
        self.lntmp = {
            'st': P.sb('ln_st', [128, 2, 6], F32), 'mv': P.sb('ln_mv', [128, 2], F32),
            'sd': P.sb('ln_sd', [128, 1], F32), 'rstd': P.sb('ln_rstd', [128, 1], F32),
            'nmr': P.sb('ln_nmr', [128, 1], F32), 'eps': self.eps, 'beps': self.beps, 'buf': Buf('lntmp'),
        }
        self.xT = P.sb('xT', [128, 8, G], BF16)
        self.bxT = [Buf(f'xT{c}') for c in range(8)]
        self.hT = P.sb('hT', [128, NFF, G], BF16)
        self.bhT = [Buf(f'hT{f}') for f in range(NFF)]
        self.sg = [(P.sb(f'sg{i}', [128, G], F32), Buf(f'sg{i}')) for i in range(2)]
        self.r = P.sb('r', [128, 4, D], F32)
        self.br = [Buf(f'r{t}') for t in range(4)]


def emit_tail(P, C, wg_d, wu_d, wd_d, out_ap_tiles, mid_hook=None):
    S = P.S
    xs = [C.r[:, t, :] for t in range(4)]
    for t in range(4):
        emit_ln(P, xs[t], C.br[t], C.lnp['ln1g'], C.lnp['ln1b'], C.blnp, C.lntmp)
    emit_transpose_group(P, xs, C.br, C.xT, C.bxT)
    emit_ffn(P, C.xT, C.bxT, C.hT, C.bhT, wg_d, wu_d, wd_d, xs, C.br, C.sg, mid_hook=mid_hook)
    for t in range(4):
        emit_ln(P, xs[t], C.br[t], C.lnp['ln2g'], C.lnp['ln2b'], C.blnp, C.lntmp)
        ob = Buf('out')
        S.op('sp', lambda e, t=t: e.dma_start(out=out_ap_tiles[t], in_=xs[t]), r=[C.br[t]], w=[ob], dma=True)
        P.outs.append(ob)


def build_tail_test(ngroups):
    P = Prog()
    T = ngroups * G
    r_d = P.dram_in('r', [T, D])
    wg_d = P.dram_in('wg', [6, 128, 8, 512])
    wu_d = P.dram_in('wu', [6, 128, 8, 512])
    wd_d = P.dram_in('wd', [6, 128, 8, 512])
    C = Common(P, ['ln1g', 'ln1b', 'ln2g', 'ln2b'])
    out_d = P.dram_out('out', [T, D])
    for g in range(ngroups):
        for t in range(4):
            r0 = g * G + t * 128
            P.load('sp', C.r[:, t, :], r_d[r0:r0 + 128, :], C.br[t])
        emit_tail(P, C, wg_d, wu_d, wd_d, [out_d[g * G + t * 128: g * G + (t + 1) * 128, :] for t in range(4)])
    return P.finish()


def build_tail(kind, T):
    P = Prog()
    S = P.S
    ng = T // G
    x_d = P.dram_in('x', [T, D])
    if kind == 'even':
        yp_d = P.dram_in('ypool', [T, 512])
        yg_d = P.dram_in('yg', [T, D])
        ssq_d = P.dram_in('ssq', [T, 4])
        nw_d = P.dram_in('normw', [128, D])
        KC = 12
        npan = 4
    else:
        o_d = P.dram_in('o', [T, D])
        KC = 8
        npan = 2
    wo_d = P.dram_in('wo', [npan, 128, 8, 512])
    wg_d = P.dram_in('wg', [6, 128, 8, 512])
    wu_d = P.dram_in('wu', [6, 128, 8, 512])
    wd_d = P.dram_in('wd', [6, 128, 8, 512])
    C = Common(P, ['ln1g', 'ln1b', 'ln2g', 'ln2b'])
    out_d = P.dram_out('out', [T, D])
    xin = P.sb('xin', [128, 4, D], F32)
    bxin = [Buf(f'xin{t}') for t in range(4)]
    mix = P.sb('mix', [128, 4, KC * 128], F32)
    bmix = [Buf(f'mix{t}') for t in range(4)]
    mixT = P.sb('mixT', [128, KC, G], BF16)
    bmixT = [Buf(f'mixT{c}') for c in range(KC)]
    if kind == 'even':
        nw = P.sb('nw', [128, D], F32)
        bnw = Buf('nw')
        P.load('sp', nw[:, :], nw_d, bnw)
        ssq = P.sb('ssqt', [128, 4], F32)
        sm = P.sb('ssm', [128, 4], F32)
        bss = Buf('ssq')
        epsr = P.sb('epsr', [128, 1], F32)
        bepsr = Buf('epsr')
        S.op('dve', lambda e: e.memset(epsr[:, :], RMS_EPS), w=[bepsr])
    for g in range(ng):
        for t in range(4):
            r0 = g * G + t * 128
            P.load('sp', xin[:, t, :], x_d[r0:r0 + 128, :], bxin[t])
            if kind == 'even':
                P.load('sp', mix[:, t, 0:512], yp_d[r0:r0 + 128, :], bmix[t])
                P.load('sp', mix[:, t, 512:1536], yg_d[r0:r0 + 128, :], bmix[t])
                P.load('sp', ssq[:, :], ssq_d[r0:r0 + 128, :], bss)
                S.op('dve', lambda e: e.tensor_reduce(out=sm[:, 0:1], in_=ssq[:, :], axis=AX.X, op=ALU.add), r=[bss], w=[bss])
                S.op('act', lambda e: e.activation(out=sm[:, 1:2], in_=sm[:, 0:1], func=AF.Sqrt, bias=epsr[:, 0:1], scale=1.0 / D),
                     r=[bss, bepsr], w=[bss])
                S.op('dve', lambda e: e.reciprocal(out=sm[:, 2:3], in_=sm[:, 1:2]), r=[bss], w=[bss])
                S.op('act', lambda e, t=t: e.activation(out=mix[:, t, 512:1536], in_=mix[:, t, 512:1536], func=AF.Identity, scale=sm[:, 2:3]),
                     r=[bss, bmix[t]], w=[bmix[t]])
                S.op('dve', lambda e, t=t: e.tensor_tensor(out=mix[:, t, 512:1536], in0=mix[:, t, 512:1536], in1=nw[:, :], op=ALU.mult),
                     r=[bmix[t], bnw], w=[bmix[t]])
            else:
                P.load('sp', mix[:, t, :], o_d[r0:r0 + 128, :], bmix[t])
        for c in range(KC):
            pt, pb = P.ps()
            for t in range(4):
                S.op('pe', lambda e, t=t, c=c, pt=pt: e.transpose(pt[:, t * 128:(t + 1) * 128], mix[:, t, c * 128:(c + 1) * 128], P.ident[:, :]),
                     r=[bmix[t], P.b_ident], w=[pb], signal=(t == 3))
            if c % 2 == 0:
                S.op('act', lambda e, c=c, pt=pt: e.activation(out=mixT[:, c, :], in_=pt[:, :], func=AF.Copy), r=[pb], w=[bmixT[c]])
            else:
                S.op('dve', lambda e, c=c, pt=pt: e.tensor_copy(out=mixT[:, c, :], in_=pt[:, :]), r=[pb], w=[bmixT[c]])
        nkg = npan // 2
        for hf in range(2):
            wos = [P.load_w(wo_d, hf * nkg + kg) for kg in range(nkg)]
            for t in range(4):
                po, bo = P.ps()
                for k in range(KC):
                    wt, bw = wos[k // 8]
                    S.op('pe', lambda e, k=k, t=t, po=po, wt=wt: e.matmul(po[:, :], lhsT=mixT[:, k, t * 128:(t + 1) * 128], rhs=wt[:, k % 8, :],
                                                                         start=(k == 0), stop=(k == KC - 1)),
                         r=[bw, bmixT[k]], w=[bo], signal=(k == KC - 1))
                S.op('dve', lambda e, t=t, hf=hf, po=po: e.scalar_tensor_tensor(out=C.r[:, t, hf * 512:(hf + 1) * 512], in0=xin[:, t, hf * 512:(hf + 1) * 512],
                                                                               scalar=ALPHA, op0=ALU.mult, in1=po[:, :], op1=ALU.add),
                     r=[bo, bxin[t]], w=[C.br[t]])
        emit_tail(P, C, wg_d, wu_d, wd_d, [out_d[g * G + t * 128: g * G + (t + 1) * 128, :] for t in range(4)])
    return P.finish()


def build_even_mixer(L, P=None):
    standalone = P is None
    if standalone:
        P = Prog(nwb=2)
    S = P.S
    ng = L // G
    x_d = P.dram_in('x', [L, D])
    w_d = P.dram_in('w', [2, 128, 8, 512])
    pw_d = P.dram_in('poolw', [128, 128])
    pcol_d = P.dram_in('pcol', [128, 8])
    cvw_d = P.dram_in('cvw', [128, 20])
    rows_d = P.dram_in('rows', [128, 12])
    rcnt_d = P.dram_in('rcnt', [128, 16])
    cst_d = P.dram_in('consts', [128, 4, 128])
    psrow_d = P.dram_in('psrow', [128, 128])
    yp_d = P.dram_out('ypool', [L, 128])
    yg_d = P.dram_out('yg', [L, 257])

    cst = P.sb('cst', [128, 4, 128], F32)
    bcst = Buf('cst')
    P.load('sp', cst[:, :, :], cst_d, bcst)
    ident, tri, Um, ones = cst[:, 0, :], cst[:, 1, :], cst[:, 2, :], cst[:, 3, :]
    wA = P.wp[0][0]
    wB = P.wp[1][0]
    bW = Buf('w')
    S.op('pool', lambda e: e.dma_start(out=wA[:], in_=w_d[0]), w=[bW], dma=True)
    S.op('pool', lambda e: e.dma_start(out=wB[:], in_=w_d[1]), w=[bW], dma=True)
    pw = P.sb('pw', [128, 128], BF16)
    S.op('pool', lambda e: e.dma_start(out=pw[:, :], in_=pw_d), w=[bW], dma=True)
    pcol = P.sb('pcol', [128, 8], F32)
    cvw = P.sb('cvw', [128, 20], F32)
    rows = P.sb('rows', [128, 12], F32)
    rcnt = P.sb('rcnt', [128, 16], F32)
    bpar = Buf('par')
    P.load('sp', pcol[:, :], pcol_d, bpar)
    P.load('sp', cvw[:, :], cvw_d, bpar)
    P.load('sp', rows[:, :], rows_d, bpar)
    P.load('sp', rcnt[:, :], rcnt_d, bpar)
    arow = P.sb('arow', [128, 4], F32)
    one1 = P.sb('one1', [128, 1], F32)
    S.op('dve', lambda e: e.memset(one1[:, :], 1.0), w=[bpar])
    S.op('act', lambda e: e.activation(out=arow[:, :], in_=rows[:, 4:8], func=AF.Exp), r=[bpar], w=[bpar])
    S.op('dve', lambda e: e.tensor_scalar(out=arow[:, :], in0=arow[:, :], scalar1=-1.0, scalar2=None, op0=ALU.mult), r=[bpar], w=[bpar])

    xt = P.sb('xt', [128, 4, D], F32)
    bxt = [Buf(f'xt{t}') for t in range(4)]
    xT = P.sb('xT', [128, 8, G], BF16)
    bxT = [Buf(f'xT{c}') for c in range(8)]
    HW = 16
    uh = P.sb('uh', [128, HW + G], F32)
    buh = Buf('uh')
    pa = P.sb('pa', [128, HW + G], F32)
    pb_ = P.sb('pb', [128, HW + G], F32)
    bpab = Buf('pab')
    diff = P.sb('diff', [128, G], BF16)
    bdiff = Buf('diff')
    t16 = P.sb('t16', [128, 16], F32)
    ypg = P.sb('ypg', [128, 4, 128], F32)
    bypg = [Buf(f'ypg{t}') for t in range(4)]
    psrow = P.sb('psrow', [128, 128], F32)
    P.load('sp', psrow[:, :], psrow_d, bpar)
    cvin = [P.sb(f'cvin{i}', [128, HW + G], F32) for i in range(4)]
    bcvin = [Buf(f'cvin{i}') for i in range(4)]
    cacc = P.sb('cacc', [128, G], F32)
    bcacc = Buf('cacc')
    cvo = [P.sb(f'cvo{i}', [128, G], F32) for i in range(4)]
    bcvo = [Buf(f'cvo{i}') for i in range(4)]
    BTb = P.sb('BTb', [128, G], BF16)
    CTb = P.sb('CTb', [128, G], BF16)
    bBTb, bCTb = Buf('BTb'), Buf('CTb')
    zs = P.sb('zs', [128, 4, 256], F32)
    bzs = [Buf(f'zs{t}') for t in range(4)]
    dtv = P.sb('dtv', [128, 4, 4], F32)
    dtt = P.sb('dtt', [128, 4, 4], F32)
    da = P.sb('da', [128, 4, 4], F32)
    bdt = [Buf(f'dt{t}') for t in range(4)]
    xs_tm = P.sb('xs_tm', [128, 256], F32)
    B_tm = P.sb('B_tm', [128, 128], BF16)
    btm = Buf('tm')
    ex = P.sb('ex', [128, 8], F32)
    bex = Buf('ex')
    R = P.sb('R', [128, 512], F32)
    bR = Buf('R')
    LT = P.sb('LT', [128, 512], F32)
    bLT = Buf('LT')
    cbm = P.sb('cbm', [128, 128], F32)
    bcbm = Buf('cbm')
    MT = P.sb('MT', [128, 512], BF16)
    bMT = Buf('MT')
    xdt = P.sb('xdt', [128, 256], BF16)
    xdtd = P.sb('xdtd', [128, 256], BF16)
    bxdt = Buf('xdt')
    y_sb = P.sb('y_sb', [128, 256], F32)
    by = Buf('y')
    ygt = [P.sb(f'ygt{i}', [128, 257], F32) for i in range(2)]
    sq = P.sb('sq', [128, 256], F32)
    bsq = Buf('sq')
    ssqt = [P.sb(f'ssqt{i}', [128, 1], F32) for i in range(2)]
    bygt = [Buf(f'ygt{i}') for i in range(2)]
    h = P.sb('h', [128, 256], F32)
    h_bf = P.sb('h_bf', [128, 256], BF16)
    bh = Buf('h')
    bhbf = Buf('hbf')
    S.op('dve', lambda e: e.memset(h[:, :], 0.0), w=[bh])
    S.op('dve', lambda e: e.memset(h_bf[:, :], 0.0), w=[bhbf])
    S.op('dve', lambda e: e.memset(uh[:, 0:HW], 0.0), w=[buh])
    for i in range(4):
        S.op('dve', lambda e, i=i: e.memset(cvin[i][:, 0:HW], 0.0), w=[bcvin[i]])
    nchunk = 0
    for g in range(ng):
        for t in range(4):
            r0 = g * G + t * 128
            P.load('sp', xt[:, t, :], x_d[P.xr(r0):P.xr(r0) + 128, :], bxt[t])
        for c in range(8):
            pt, pb = P.ps()
            for t in range(4):
                S.op('pe', lambda e, t=t, c=c, pt=pt: e.transpose(pt[:, t * 128:(t + 1) * 128], xt[:, t, c * 128:(c + 1) * 128], ident),
                     r=[bxt[t], bcst], w=[pb], signal=(t == 3))
            if c % 2 == 0:
                S.op('act', lambda e, c=c, pt=pt: e.activation(out=xT[:, c, :], in_=pt[:, :], func=AF.Copy), r=[pb], w=[bxT[c]])
            else:
                S.op('dve', lambda e, c=c, pt=pt: e.tensor_copy(out=xT[:, c, :], in_=pt[:, :]), r=[pb], w=[bxT[c]])
        fm = [(wA, 0, uh, buh), (wA, 128, cvin[0], bcvin[0]), (wA, 256, cvin[1], bcvin[1]), (wA, 384, cvin[2], bcvin[2]),
              (wB, 0, cvin[3], bcvin[3])]
        for wt, c0, dst, bdst in fm:
            pp, bp = P.ps()
            for k in range(8):
                S.op('pe', lambda e, k=k, pp=pp, wt=wt, c0=c0: e.matmul(pp[:, :], lhsT=wt[:, k, c0:c0 + 128], rhs=xT[:, k, :], start=(k == 0), stop=(k == 7)),
                     r=[bW, bxT[k]], w=[bp], signal=(k == 7))
            S.op('act', lambda e, pp=pp, dst=dst: e.activation(out=dst[:, HW:HW + G], in_=pp[:, :], func=AF.Copy), r=[bp], w=[bdst])
        for t in range(4):
            pz, bz = P.ps()
            for k in range(8):
                S.op('pe', lambda e, k=k, t=t, pz=pz: e.matmul(pz[:, 0:256], lhsT=xT[:, k, t * 128:(t + 1) * 128], rhs=wB[:, k, 128:384], start=(k == 0), stop=(k == 7)),
                     r=[bW, bxT[k]], w=[bz], signal=False)
            for k in range(8):
                S.op('pe', lambda e, k=k, t=t, pz=pz: e.matmul(pz[:, 256:260], lhsT=xT[:, k, t * 128:(t + 1) * 128], rhs=wB[:, k, 384:388], start=(k == 0), stop=(k == 7)),
                     r=[bW, bxT[k]], w=[bz], signal=(k == 7))
            S.op('act', lambda e, t=t, pz=pz: e.activation(out=zs[:, t, :], in_=pz[:, 0:256], func=AF.Silu), r=[bz], w=[bzs[t]])
            S.op('dve', lambda e, t=t, pz=pz: e.tensor_tensor(out=dtv[:, t, :], in0=pz[:, 256:260], in1=rows[:, 0:4], op=ALU.add), r=[bz, bpar], w=[bdt[t]])
            S.op('act', lambda e, t=t: e.activation(out=dtv[:, t, :], in_=dtv[:, t, :], func=AF.Exp), r=[bdt[t]], w=[bdt[t]])
            S.op('act', lambda e, t=t: e.activation(out=dtt[:, t, :], in_=dtv[:, t, :], func=AF.Ln, bias=one1[:, 0:1], scale=1.0), r=[bdt[t], bpar], w=[bdt[t]])
            S.op('dve', lambda e, t=t: e.tensor_tensor(out=da[:, t, :], in0=dtt[:, t, :], in1=arow[:, :], op=ALU.mult), r=[bdt[t], bpar], w=[bdt[t]])
        S.op('dve', lambda e: e.scalar_tensor_tensor(out=pa[:, 1:HW + G], in0=uh[:, 0:HW + G - 1], scalar=pcol[:, 1:2], op0=ALU.mult, in1=uh[:, 1:HW + G], op1=ALU.add),
             r=[buh, bpar], w=[bpab])
        S.op('dve', lambda e: e.scalar_tensor_tensor(out=pb_[:, 3:HW + G], in0=pa[:, 1:HW + G - 2], scalar=pcol[:, 2:3], op0=ALU.mult, in1=pa[:, 3:HW + G], op1=ALU.add),
             r=[bpab, bpar], w=[bpab])
        S.op('dve', lambda e: e.scalar_tensor_tensor(out=pa[:, 7:HW + G], in0=pb_[:, 3:HW + G - 4], scalar=pcol[:, 3:4], op0=ALU.mult, in1=pb_[:, 7:HW + G], op1=ALU.add),
             r=[bpab, bpar], w=[bpab])
        S.op('dve', lambda e: e.scalar_tensor_tensor(out=pb_[:, 15:HW + G], in0=pa[:, 7:HW + G - 8], scalar=pcol[:, 4:5], op0=ALU.mult, in1=pa[:, 15:HW + G], op1=ALU.add),
             r=[bpab, bpar], w=[bpab])
        S.op('dve', lambda e: e.scalar_tensor_tensor(out=diff[:, :], in0=pb_[:, HW:HW + G], scalar=pcol[:, 5:6], op0=ALU.mult, in1=uh[:, HW:HW + G], op1=ALU.subtract),
             r=[bpab, buh, bpar], w=[bdiff])
        if g == 0:
            S.op('dve', lambda e: e.tensor_tensor(out=t16[:, :], in0=pb_[:, HW:HW + 16], in1=rcnt[:, :], op=ALU.mult), r=[bpab, bpar], w=[bpab])
            S.op('dve', lambda e: e.tensor_tensor(out=diff[:, 0:16], in0=t16[:, :], in1=uh[:, HW:HW + 16], op=ALU.subtract), r=[bpab, buh], w=[bdiff])
        S.op('dve', lambda e: e.tensor_copy(out=uh[:, 0:HW], in_=uh[:, G:G + HW]), r=[buh], w=[buh])
        pp, bp = P.ps()
        for t in range(4):
            S.op('pe', lambda e, pp=pp, t=t: e.matmul(pp[:, t * 128:(t + 1) * 128], lhsT=diff[:, t * 128:(t + 1) * 128], rhs=pw[:, :], start=True, stop=True),
                 r=[bW, bdiff], w=[bp], signal=(t == 3))
        for t in range(4):
            S.op('dve', lambda e, pp=pp, t=t: e.tensor_tensor(out=ypg[:, t, :], in0=pp[:, t * 128:(t + 1) * 128], in1=psrow[:, :], op=ALU.mult),
                 r=[bp, bpar], w=[bypg[t]])
            ob = Buf('o')
            r0 = g * G + t * 128
            S.op('sp', lambda e, t=t, r0=r0: e.dma_start(out=yp_d[r0:r0 + 128, :], in_=ypg[:, t, :]), r=[bypg[t]], w=[ob], dma=True)
            P.outs.append(ob)
        for i in range(4):
            ci = cvin[i]
            S.op('dve', lambda e, ci=ci, i=i: e.tensor_scalar(out=cacc[:, :], in0=ci[:, HW:HW + G], scalar1=cvw[:, i * 5 + 3:i * 5 + 4], scalar2=cvw[:, i * 5 + 4:i * 5 + 5],
                                                             op0=ALU.mult, op1=ALU.add), r=[bcvin[i], bpar], w=[bcacc])
            for kk in range(1, 4):
                S.op('dve', lambda e, ci=ci, i=i, kk=kk: e.scalar_tensor_tensor(out=cacc[:, :], in0=ci[:, HW - kk:HW + G - kk], scalar=cvw[:, i * 5 + 3 - kk:i * 5 + 4 - kk],
                                                                               op0=ALU.mult, in1=cacc[:, :], op1=ALU.add), r=[bcvin[i], bpar, bcacc], w=[bcacc])
            S.op('act', lambda e, i=i: e.activation(out=cvo[i][:, :], in_=cacc[:, :], func=AF.Silu), r=[bcacc], w=[bcvo[i]])
            S.op('dve', lambda e, ci=ci: e.tensor_copy(out=ci[:, 0:HW], in_=ci[:, G:G + HW]), r=[bcvin[i]], w=[bcvin[i]])
        S.op('dve', lambda e: e.tensor_copy(out=BTb[:, :], in_=cvo[2][:, :]), r=[bcvo[2]], w=[bBTb])
        S.op('dve', lambda e: e.tensor_copy(out=CTb[:, :], in_=cvo[3][:, :]), r=[bcvo[3]], w=[bCTb])
        for t in range(4):
            sl = slice(t * 128, (t + 1) * 128)
            pt, pb = P.ps()
            for i in range(3):
                S.op('pe', lambda e, i=i, pt=pt, sl=sl: e.transpose(pt[:, i * 128:(i + 1) * 128], cvo[i][:, sl], ident), r=[bcvo[i], bcst], w=[pb], signal=(i == 2))
            S.op('act', lambda e, pt=pt: e.activation(out=xs_tm[:, :], in_=pt[:, 0:256], func=AF.Copy), r=[pb], w=[btm])
            S.op('dve', lambda e, pt=pt: e.tensor_copy(out=B_tm[:, :], in_=pt[:, 256:384]), r=[pb], w=[btm])
            p2, b2 = P.ps()
            S.op('pe', lambda e, p2=p2, t=t: e.matmul(p2[:, 0:4], lhsT=tri, rhs=da[:, t, :], start=True, stop=True), r=[bcst, bdt[t]], w=[b2], signal=False)
            S.op('pe', lambda e, p2=p2, t=t: e.matmul(p2[:, 4:8], lhsT=ones, rhs=da[:, t, :], start=True, stop=True), r=[bcst, bdt[t]], w=[b2])
            S.op('act', lambda e, p2=p2: e.activation(out=ex[:, :], in_=p2[:, 0:8], func=AF.Exp), r=[b2], w=[bex])
            for hh in range(4):
                S.op('dve', lambda e, hh=hh, t=t: e.tensor_scalar(out=R[:, hh * 128:(hh + 1) * 128], in0=tri, scalar1=da[:, t, hh:hh + 1], scalar2=None, op0=ALU.mult),
                     r=[bcst, bdt[t]], w=[bR])
            p3, b3 = P.ps()
            S.op('pe', lambda e, p3=p3: e.matmul(p3[:, :], lhsT=Um, rhs=R[:, :], start=True, stop=True), r=[bcst, bR], w=[b3])
            S.op('act', lambda e, p3=p3: e.activation(out=LT[:, :], in_=p3[:, :], func=AF.Exp), r=[b3], w=[bLT])
            p4, b4 = P.ps()
            S.op('pe', lambda e, p4=p4, sl=sl: e.matmul(p4[:, 0:128], lhsT=BTb[:, sl], rhs=CTb[:, sl], start=True, stop=True), r=[bBTb, bCTb], w=[b4])
            S.op('dve', lambda e, p4=p4: e.tensor_tensor(out=cbm[:, :], in0=p4[:, 0:128], in1=tri, op=ALU.mult), r=[b4, bcst], w=[bcbm])
            for hh in range(4):
                S.op('dve', lambda e, hh=hh: e.tensor_tensor(out=MT[:, hh * 128:(hh + 1) * 128], in0=LT[:, hh * 128:(hh + 1) * 128], in1=cbm[:, :], op=ALU.mult), r=[bLT, bcbm], w=[bMT])
            for hh in range(4):
                cs = slice(hh * 64, (hh + 1) * 64)
                S.op('dve', lambda e, hh=hh, cs=cs, t=t: e.tensor_scalar(out=xdt[:, cs], in0=xs_tm[:, cs], scalar1=dtt[:, t, hh:hh + 1], scalar2=None, op0=ALU.mult),
                     r=[btm, bdt[t]], w=[bxdt])
                S.op('dve', lambda e, hh=hh, cs=cs, t=t: e.tensor_scalar(out=xdtd[:, cs], in0=xs_tm[:, cs], scalar1=dtt[:, t, hh:hh + 1], scalar2=LT[:, hh * 128 + 127:hh * 128 + 128],
                                                                        op0=ALU.mult, op1=ALU.mult), r=[btm, bdt[t], bLT], w=[bxdt])
            p5, b5 = P.ps()
            for hh in range(4):
                S.op('pe', lambda e, hh=hh, p5=p5: e.matmul(p5[:, hh * 64:(hh + 1) * 64], lhsT=MT[:, hh * 128:(hh + 1) * 128], rhs=xdt[:, hh * 64:(hh + 1) * 64], start=True, stop=True),
                     r=[bMT, bxdt], w=[b5], signal=(hh == 3))
            p6, b6 = P.ps()
            S.op('pe', lambda e, p6=p6, sl=sl: e.matmul(p6[:, 0:256], lhsT=CTb[:, sl], rhs=h_bf[:, :], start=True, stop=True), r=[bCTb, bhbf], w=[b6])
            S.op('act', lambda e, p5=p5: e.activation(out=y_sb[:, :], in_=p5[:, 0:256], func=AF.Copy), r=[b5], w=[by])
            for hh in range(4):
                cs = slice(hh * 64, (hh + 1) * 64)
                S.op('dve', lambda e, hh=hh, cs=cs, p6=p6: e.scalar_tensor_tensor(out=y_sb[:, cs], in0=p6[:, cs], scalar=ex[:, hh:hh + 1], op0=ALU.mult, in1=y_sb[:, cs], op1=ALU.add),
                     r=[b6, bex, by], w=[by])
                S.op('dve', lambda e, hh=hh, cs=cs: e.scalar_tensor_tensor(out=y_sb[:, cs], in0=xs_tm[:, cs], scalar=rows[:, 8 + hh:9 + hh], op0=ALU.mult, in1=y_sb[:, cs], op1=ALU.add),
                     r=[btm, bpar, by], w=[by])
            yb = nchunk % 2
            S.op('dve', lambda e, yb=yb, t=t: e.tensor_tensor(out=ygt[yb][:, 0:256], in0=y_sb[:, :], in1=zs[:, t, :], op=ALU.mult), r=[by, bzs[t]], w=[bygt[yb]])
            S.op('dve', lambda e, yb=yb: e.tensor_tensor(out=sq[:, :], in0=ygt[yb][:, 0:256], in1=ygt[yb][:, 0:256], op=ALU.mult), r=[bygt[yb]], w=[bsq])
            S.op('dve', lambda e, yb=yb: e.tensor_reduce(out=ygt[yb][:, 256:257], in_=sq[:, :], axis=AX.X, op=ALU.add), r=[bsq], w=[bygt[yb]])
            r0 = g * G + t * 128
            o1 = Buf('o')
            S.op('sp', lambda e, yb=yb, r0=r0: e.dma_start(out=yg_d[r0:r0 + 128, :], in_=ygt[yb][:, :]), r=[bygt[yb]], w=[o1], dma=True)
            P.outs += [o1]
            p7, b7 = P.ps()
            S.op('pe', lambda e, p7=p7: e.matmul(p7[:, 0:256], lhsT=B_tm[:, :], rhs=xdtd[:, :], start=True, stop=True), r=[btm, bxdt], w=[b7])
            for hh in range(4):
                cs = slice(hh * 64, (hh + 1) * 64)
                S.op('dve', lambda e, hh=hh, cs=cs, p7=p7: e.scalar_tensor_tensor(out=h[:, cs], in0=h[:, cs], scalar=ex[:, 4 + hh:5 + hh], op0=ALU.mult, in1=p7[:, cs], op1=ALU.add),
                     r=[b7, bex, bh], w=[bh])
            S.op('act', lambda e: e.activation(out=h_bf[:, :], in_=h[:, :], func=AF.Copy), r=[bh], w=[bhbf])
            nchunk += 1
        P.group_done(g)
    if standalone:
        return P.finish()
    return None


def _consts():
    i = np.arange(128)
    ident = np.eye(128, dtype=np.float32)
    tri = (i[:, None] <= i[None, :]).astype(np.float32)
    U = (i[:, None] > i[None, :]).astype(np.float32)
    ones = np.ones((128, 128), np.float32)
    return np.ascontiguousarray(np.stack([ident, tri, U, ones], axis=1))


def even_mixer_inputs(x_b, j, w_in, pool_w, pool_scale, conv_w, conv_b, dt_bias, a_log, d_skip):
    g = j // 2
    hs = slice(4 * j, 4 * j + 4)
    c_u = w_in[:, j * 128:(j + 1) * 128]
    c_z = w_in[:, 512 + 256 * j:512 + 256 * (j + 1)]
    xo = 1536
    c_xs = w_in[:, xo + 256 * j:xo + 256 * (j + 1)]
    c_B = w_in[:, xo + 1024 + 128 * g:xo + 1024 + 128 * (g + 1)]
    c_C = w_in[:, xo + 1280 + 128 * g:xo + 1280 + 128 * (g + 1)]
    c_dt = w_in[:, 3072 + 4 * j:3072 + 4 * (j + 1)]
    Wcat = np.concatenate([c_u, c_xs, c_B, c_C, c_z, c_dt], axis=1)
    w = 2 ** (j + 1)
    pcol = np.zeros((128, 8), np.float32)
    pcol[:, 0] = pool_scale[j * 128:(j + 1) * 128]
    for s in range(4):
        pcol[:, 1 + s] = 1.0 if s <= j else 0.0
    pcol[:, 5] = 1.0 / w
    cvw = np.zeros((128, 20), np.float32)
    cols = [np.arange(256 * j, 256 * j + 128), np.arange(256 * j + 128, 256 * j + 256),
            np.arange(1024 + 128 * g, 1024 + 128 * (g + 1)), np.arange(1280 + 128 * g, 1280 + 128 * (g + 1))]
    for i, cc in enumerate(cols):
        cvw[:, i * 5:i * 5 + 4] = conv_w[:, cc].T
        cvw[:, i * 5 + 4] = conv_b[cc]
    rows = bcast_rows(np.concatenate([dt_bias[hs], a_log[hs], d_skip[hs]]))
    rcnt = bcast_rows((1.0 / np.minimum(np.arange(16) + 1, w)).astype(np.float32))
    return dict(x=np.ascontiguousarray(x_b), w=panelize(np.ascontiguousarray(Wcat)), poolw=np.ascontiguousarray(pool_w[j]),
                pcol=pcol, cvw=cvw, rows=rows, rcnt=rcnt, consts=_consts(),
                psrow=bcast_rows(pool_scale[j * 128:(j + 1) * 128]))


NEG = -30000.0


def build_odd_mixer(L, P=None):
    standalone = P is None
    if standalone:
        P = Prog(nwb=5, ps_banks=(4, 5, 6, 7))
    S = P.S
    ng = L // G
    nblk = L // 128
    x_d = P.dram_in('x', [L, D])
    w_d = P.dram_in('w', [5, 128, 8, 512])
    qnw_d = P.dram_in('qnw', [128, 512])
    kvnw_d = P.dram_in('kvnw', [128, 256])
    nfb_d = P.dram_in('nfb', [128, 2])
    sel_d = P.dram_in('sel', [128, 8 * 70])
    rope_d = P.dram_in('rope', [2, 32, L])
    mask_d = P.dram_in('mask', [128, 4 * 512])
    cst_d = P.dram_in('consts', [128, 4, 128])
    o_d = P.dram_out('o', [L, 256])

    cst = P.sb('cst', [128, 4, 128], F32)
    bcst = Buf('cst')
    P.load('sp', cst[:, :, :], cst_d, bcst)
    ident = cst[:, 0, :]
    W = [P.wp[i][0] for i in range(5)]
    bW = Buf('w')
    for i in range(5):
        S.op('pool', lambda e, i=i: e.dma_start(out=W[i][:], in_=w_d[i]), w=[bW], dma=True)
    identb = P.sb('identb', [128, 128], BF16)
    maskz = P.sb('maskz', [128, 4 * 512], BF16)
    sel = P.sb('sel', [1, 8 * 70], BF16)
    S.op('pool', lambda e: e.dma_start(out=identb[:, :], in_=cst_d[:, 0, :]), w=[bW], dma=True)
    S.op('pool', lambda e: e.dma_start(out=maskz[:, :], in_=mask_d), w=[bW], dma=True)
    S.op('pool', lambda e: e.dma_start(out=sel[:, :], in_=sel_d[0:1, :]), w=[bW], dma=True)
    qnw = P.sb('qnw', [128, 512], F32)
    kvnw = P.sb('kvnw', [128, 256], F32)
    nfb = P.sb('nfb', [128, 2], F32)
    bpar = Buf('par')
    P.load('sp', qnw[:, :], qnw_d, bpar)
    P.load('sp', kvnw[:, :], kvnw_d, bpar)
    P.load('sp', nfb[:, :], nfb_d, bpar)
    S.op('dve', lambda e: e.tensor_scalar(out=nfb[:, :], in0=nfb[:, :], scalar1=-1.0, scalar2=None, op0=ALU.mult), r=[bpar], w=[bpar])
    one1 = P.sb('one1', [128, 1], F32)
    epsr = P.sb('epsr', [128, 1], F32)
    S.op('dve', lambda e: e.memset(one1[:, :], 1.0), w=[bpar])
    S.op('dve', lambda e: e.memset(epsr[:, :], RMS_EPS), w=[bpar])
    onesb = P.sb('onesb', [1, G], BF16)
    onesf = P.sb('onesf', [1, G], F32)
    S.op('dve', lambda e: e.memset(onesb[:, :], 1.0), w=[bpar])
    S.op('dve', lambda e: e.memset(onesf[:, :], 1.0), w=[bpar])

    xt = P.sb('xt', [128, 4, D], F32)
    bxt = [Buf(f'xt{t}') for t in range(4)]
    xT = P.sb('xT', [128, 8, G], BF16)
    bxT = [Buf(f'xT{c}') for c in range(8)]
    KA = P.sb('KA', [128, L], BF16)
    KB = P.sb('KB', [96, L], BF16)
    KfT = KA
    KX = KB
    KmT = [KA, KB]
    Vf = P.sb('Vf', [128, nblk, 2, 65], BF16)
    Vm = Vf
    bK = Buf('K')
    S.op('pool', lambda e: e.memset(Vf[:, :, :, :], 1.0), w=[bK])
    S.op('pool', lambda e: e.memset(KA[64:96, :], 0.0), w=[bK])
    S.op('pool', lambda e: e.memset(KB[64:96, :], 0.0), w=[bK])
    QmT = [P.sb(f'QmT{h}', [96, G], BF16) for h in range(2)]
    bQ = Buf('Q')
    for h_ in range(2):
        S.op('pool', lambda e, h_=h_: e.memset(QmT[h_][64:96, :], 0.0), w=[bQ])
    fv = P.sb('fv', [1, G], F32)
    fc = [[P.sb(f'fc{h}{i}', [1, G], F32) for i in range(2)] for h in range(2)]
    r1 = P.sb('r1', [1, G], F32)
    hib = P.sb('hib', [1, G], BF16)
    midb = P.sb('midb', [1, G], BF16)
    lob = P.sb('lob', [1, G], BF16)
    bf_ = Buf('f')
    bfc = [Buf('fc0'), Buf('fc1')]
    cqs = P.sb('cqs', [128, 4, 512], F32)
    ckvs = P.sb('ckvs', [128, 4, 256], F32)
    bcq = [Buf(f'cq{t}') for t in range(4)]
    bckv = [Buf(f'ckv{t}') for t in range(4)]
    sqt = P.sb('sqt', [128, 512], F32)
    bsq = Buf('sq')
    st4 = P.sb('st4', [128, 4], F32)
    bst = Buf('st')
    cqnT = P.sb('cqnT', [128, 4, G], BF16)
    ckvnT = P.sb('ckvnT', [128, 2, G], BF16)
    bcqnT = Buf('cqnT')
    bckvnT = Buf('ckvnT')
    rt = P.sb('rt', [128, 2, G], F32)
    brt = Buf('rt')
    qtmp = P.sb('qtmp', [128, G], F32)
    rA = P.sb('rA', [128, G], F32)
    rB = P.sb('rB', [128, G], F32)
    bqt = Buf('qtmp')
    PT = [P.sb(f'PT{i}', [128, G], BF16) for i in range(3)]
    bPT = [Buf(f'PT{i}') for i in range(3)]
    pti = 0
    og = P.sb('og', [128, 4, 256], F32)
    bog = [Buf(f'og{i}') for i in range(4)]
    rc = P.sb('rc', [128, 1], F32)
    brc = Buf('rc')
    SCM = 96.0 ** -0.5

    def proj_fm(wt, c0, m, rhs, brhs, nk):
        pp, bp = P.ps()
        for k in range(nk):
            S.op('pe', lambda e, k=k, pp=pp: e.matmul(pp[0:m, :], lhsT=wt[:, k, c0:c0 + m], rhs=rhs[:, k, :], start=(k == 0), stop=(k == nk - 1)),
                 r=[bW] + brhs, w=[bp], signal=(k == nk - 1))
        return pp, bp

    for ph in ('f', 'm'):
        for g in range(ng):
            gs = slice(g * G, (g + 1) * G)
            for t in range(4):
                r0 = g * G + t * 128
                P.load('sp', xt[:, t, :], x_d[P.xr(r0):P.xr(r0) + 128, :], bxt[t])
            P.load('sp', rt[64:96, 0, :], rope_d[0, :, gs], brt)
            P.load('sp', rt[64:96, 1, :], rope_d[1, :, gs], brt)
            for c in range(8):
                pt, pb = P.ps()
                for t in range(4):
                    S.op('pe', lambda e, t=t, c=c, pt=pt: e.transpose(pt[:, t * 128:(t + 1) * 128], xt[:, t, c * 128:(c + 1) * 128], ident),
                         r=[bxt[t], bcst], w=[pb], signal=(t == 3))
                if c % 2 == 0:
                    S.op('act', lambda e, c=c, pt=pt: e.activation(out=xT[:, c, :], in_=pt[:, :], func=AF.Copy), r=[pb], w=[bxT[c]])
                else:
                    S.op('dve', lambda e, c=c, pt=pt: e.tensor_copy(out=xT[:, c, :], in_=pt[:, :]), r=[pb], w=[bxT[c]])
            if ph == 'f':
                for hh in range(2):
                    pp, bp = proj_fm(W[0], hh * 64, 64, xT, bxT, 8)
                    S.op('act', lambda e, pp=pp, hh=hh: e.activation(out=QmT[hh][0:64, :], in_=pp[0:64, :], func=AF.Copy, scale=0.125), r=[bp], w=[bQ])
                    pp, bp = proj_fm(W[0], 128 + hh * 64, 64, xT, bxT, 8)
                    S.op('dve', lambda e, pp=pp, gs=gs, hh=hh: e.tensor_copy(out=KmT[hh][0:64, gs], in_=pp[0:64, :]), r=[bp], w=[bK])
                for t in range(4):
                    blk = g * 4 + t
                    pv, bv = P.ps()
                    for k in range(8):
                        S.op('pe', lambda e, k=k, t=t, pv=pv: e.matmul(pv[:, 0:128], lhsT=xT[:, k, t * 128:(t + 1) * 128], rhs=W[0][:, k, 256:384], start=(k == 0), stop=(k == 7)),
                             r=[bW] + [bxT[k]], w=[bv], signal=(k == 7))
                    for hh in range(2):
                        S.op('act' if hh == 0 else 'dve',
                             (lambda e, hh=hh, pv=pv, blk=blk: e.activation(out=Vf[:, blk, hh, 0:64], in_=pv[:, hh * 64:(hh + 1) * 64], func=AF.Copy)) if hh == 0 else
                             (lambda e, hh=hh, pv=pv, blk=blk: e.tensor_copy(out=Vf[:, blk, hh, 0:64], in_=pv[:, hh * 64:(hh + 1) * 64])),
                             r=[bv], w=[bK])
                for hh in range(2):
                    px, bpx = P.ps()
                    pk, bpk = P.ps()
                    pf, bpf = proj_fm(W[0], 384 + hh, 1, xT, bxT, 8)
                    S.op('act', lambda e, pf=pf, hh=hh: e.activation(out=fv[:, :], in_=pf[0:1, :], func=AF.Exp, bias=nfb[0:1, hh:hh + 1], scale=-1.0), r=[bpf, bpar], w=[bf_])
                    S.op('act', lambda e: e.activation(out=fv[:, :], in_=fv[:, :], func=AF.Ln, bias=one1[0:1, 0:1], scale=1.0), r=[bf_, bpar], w=[bf_])
                    cur = fc[hh][g % 2]
                    prev = fc[hh][(g + 1) % 2]
                    init = 0.0 if g == 0 else prev[0:1, G - 1:G]
                    S.op('dve', lambda e, cur=cur, init=init: e.tensor_tensor_scan(out=cur[:, :], data0=onesf[:, :], data1=fv[:, :], initial=init, op0=ALU.mult, op1=ALU.subtract),
                         r=[bf_, bfc[hh], bpar], w=[bfc[hh]])
                    S.op('dve', lambda e, cur=cur: e.tensor_copy(out=hib[:, :], in_=cur[:, :]), r=[bfc[hh]], w=[bf_])
                    S.op('dve', lambda e, cur=cur: e.tensor_tensor(out=r1[:, :], in0=cur[:, :], in1=hib[:, :], op=ALU.subtract), r=[bfc[hh], bf_], w=[bf_])
                    S.op('dve', lambda e: e.tensor_copy(out=midb[:, :], in_=r1[:, :]), r=[bf_], w=[bf_])
                    S.op('dve', lambda e: e.tensor_tensor(out=r1[:, :], in0=r1[:, :], in1=midb[:, :], op=ALU.subtract), r=[bf_], w=[bf_])
                    S.op('dve', lambda e: e.tensor_copy(out=lob[:, :], in_=r1[:, :]), r=[bf_], w=[bf_])
                    srcs = [hib, midb, lob, onesb]
                    for i in range(4):
                        S.op('pe', lambda e, i=i, px=px, srcs=srcs: e.matmul(px[0:70, :], lhsT=sel[0:1, i * 70:(i + 1) * 70], rhs=srcs[i][0:1, :],
                                                                           start=(i == 0), stop=(i == 3)), r=[bW, bf_, bpar], w=[bpx], signal=(i == 3))
                    ksrcs = [onesb, hib, midb, lob]
                    for i in range(4):
                        S.op('pe', lambda e, i=i, pk=pk, ksrcs=ksrcs: e.matmul(pk[0:70, :], lhsT=sel[0:1, (4 + i) * 70:(5 + i) * 70], rhs=ksrcs[i][0:1, :],
                                                                             start=(i == 0), stop=(i == 3)), r=[bW, bf_, bpar], w=[bpk], signal=True)
                    S.op('act', lambda e, px=px, hh=hh: e.activation(out=QmT[hh][64:70, :], in_=px[64:70, :], func=AF.Copy), r=[bpx], w=[bQ])
                    S.op('dve', lambda e, pk=pk, gs=gs, hh=hh: e.tensor_copy(out=KmT[hh][64:70, gs], in_=pk[64:70, :]), r=[bpk], w=[bK])
            if ph == 'm':
                for t in range(4):
                    pq, bq = P.ps()
                    for k in range(8):
                        S.op('pe', lambda e, k=k, t=t, pq=pq: e.matmul(pq[:, :], lhsT=xT[:, k, t * 128:(t + 1) * 128], rhs=W[1][:, k, :], start=(k == 0), stop=(k == 7)),
                             r=[bW, bxT[k]], w=[bq], signal=(k == 7))
                    S.op('act', lambda e, t=t, pq=pq: e.activation(out=cqs[:, t, :], in_=pq[:, :], func=AF.Copy), r=[bq], w=[bcq[t]])
                    pc, bc = P.ps()
                    for k in range(8):
                        S.op('pe', lambda e, k=k, t=t, pc=pc: e.matmul(pc[:, 0:256], lhsT=xT[:, k, t * 128:(t + 1) * 128], rhs=W[2][:, k, 0:256], start=(k == 0), stop=(k == 7)),
                             r=[bW, bxT[k]], w=[bc], signal=(k == 7))
                    S.op('act', lambda e, t=t, pc=pc: e.activation(out=ckvs[:, t, :], in_=pc[:, 0:256], func=AF.Copy), r=[bc], w=[bckv[t]])
                    for (src, bsrc, n, nwt) in ((cqs, bcq, 512, qnw), (ckvs, bckv, 256, kvnw)):
                        S.op('pool', lambda e, src=src, t=t, n=n: e.tensor_tensor(out=sqt[:, 0:n], in0=src[:, t, :], in1=src[:, t, :], op=ALU.mult), r=[bsrc[t]], w=[bsq])
                        S.op('dve', lambda e, n=n: e.tensor_reduce(out=st4[:, 0:1], in_=sqt[:, 0:n], axis=AX.X, op=ALU.add), r=[bsq], w=[bst])
                        S.op('act', lambda e, n=n: e.activation(out=st4[:, 1:2], in_=st4[:, 0:1], func=AF.Sqrt, bias=epsr[:, 0:1], scale=1.0 / n), r=[bst, bpar], w=[bst])
                        S.op('dve', lambda e: e.reciprocal(out=st4[:, 2:3], in_=st4[:, 1:2]), r=[bst], w=[bst])
                        S.op('act', lambda e, src=src, t=t: e.activation(out=src[:, t, :], in_=src[:, t, :], func=AF.Identity, scale=st4[:, 2:3]), r=[bst, bsrc[t]], w=[bsrc[t]])
                        S.op('dve', lambda e, src=src, t=t, nwt=nwt: e.tensor_tensor(out=src[:, t, :], in0=src[:, t, :], in1=nwt[:, :], op=ALU.mult), r=[bsrc[t], bpar], w=[bsrc[t]])
                for c in range(4):
                    pt, pb = P.ps()
                    for t in range(4):
                        S.op('pe', lambda e, t=t, c=c, pt=pt: e.transpose(pt[:, t * 128:(t + 1) * 128], cqs[:, t, c * 128:(c + 1) * 128], ident), r=[bcq[t], bcst], w=[pb], signal=(t == 3))
                    S.op('act', lambda e, c=c, pt=pt: e.activation(out=cqnT[:, c, :], in_=pt[:, :], func=AF.Copy), r=[pb], w=[bcqnT])
                for c in range(2):
                    pt, pb = P.ps()
                    for t in range(4):
                        S.op('pe', lambda e, t=t, c=c, pt=pt: e.transpose(pt[:, t * 128:(t + 1) * 128], ckvs[:, t, c * 128:(c + 1) * 128], ident), r=[bckv[t], bcst], w=[pb], signal=(t == 3))
                    S.op('dve', lambda e, c=c, pt=pt: e.tensor_copy(out=ckvnT[:, c, :], in_=pt[:, :]), r=[pb], w=[bckvnT])
                for hh in range(2):
                    pm, bm = proj_fm(W[3], hh * 192, 96, cqnT, [bcqnT], 4)
                    pp2, bp2 = proj_fm(W[3], hh * 192 + 96, 96, cqnT, [bcqnT], 4)
                    S.op('act', lambda e, pm=pm: e.activation(out=qtmp[0:64, :], in_=pm[0:64, :], func=AF.Copy), r=[bm], w=[bqt])
                    S.op('dve', lambda e, pm=pm: e.tensor_tensor(out=rA[64:96, :], in0=pm[64:96, :], in1=rt[64:96, 0, :], op=ALU.mult), r=[bm, brt], w=[bqt])
                    S.op('dve', lambda e, pp2=pp2: e.tensor_tensor(out=rB[64:96, :], in0=pp2[64:96, :], in1=rt[64:96, 1, :], op=ALU.mult), r=[bp2, brt], w=[bqt])
                    S.op('dve', lambda e: e.tensor_tensor(out=qtmp[64:96, :], in0=rA[64:96, :], in1=rB[64:96, :], op=ALU.add), r=[bqt], w=[bqt])
                    S.op('act', lambda e, hh=hh: e.activation(out=QmT[hh][0:64, :], in_=qtmp[0:64, :], func=AF.Copy, scale=SCM), r=[bqt], w=[bQ])
                    S.op('act', lambda e, hh=hh: e.activation(out=QmT[hh][64:96, :], in_=qtmp[64:96, :], func=AF.Copy, scale=SCM), r=[bqt], w=[bQ])
                    pn, bn = proj_fm(W[4], hh * 64, 64, ckvnT, [bckvnT], 2)
                    S.op('dve', lambda e, hh=hh, pn=pn, gs=gs: e.tensor_copy(out=KmT[hh][0:64, gs], in_=pn[0:64, :]), r=[bn], w=[bK])
                pkr, bkr = proj_fm(W[2], 256, 96, xT, bxT, 8)
                pkp, bkp = proj_fm(W[2], 352, 96, xT, bxT, 8)
                S.op('dve', lambda e, pkr=pkr: e.tensor_tensor(out=rA[64:96, :], in0=pkr[64:96, :], in1=rt[64:96, 0, :], op=ALU.mult), r=[bkr, brt], w=[bqt])
                S.op('dve', lambda e, pkp=pkp: e.tensor_tensor(out=rB[64:96, :], in0=pkp[64:96, :], in1=rt[64:96, 1, :], op=ALU.mult), r=[bkp, brt], w=[bqt])
                for hh in range(2):
                    S.op('dve', lambda e, hh=hh, gs=gs: e.tensor_tensor(out=KmT[hh][64:96, gs], in0=rA[64:96, :], in1=rB[64:96, :], op=ALU.add), r=[bqt], w=[bK])
                for t in range(4):
                    blk = g * 4 + t
                    pv, bv = P.ps()
                    for k in range(2):
                        S.op('pe', lambda e, k=k, t=t, pv=pv: e.matmul(pv[:, 0:128], lhsT=ckvnT[:, k, t * 128:(t + 1) * 128], rhs=W[4][:, k, 128:256], start=(k == 0), stop=(k == 1)),
                             r=[bW, bckvnT], w=[bv], signal=(k == 1))
                    S.op('act', lambda e, pv=pv, blk=blk: e.activation(out=Vm[:, blk, 0, 0:64], in_=pv[:, 0:64], func=AF.Copy), r=[bv], w=[bK])
                    S.op('dve', lambda e, pv=pv, blk=blk: e.tensor_copy(out=Vm[:, blk, 1, 0:64], in_=pv[:, 64:128]), r=[bv], w=[bK])
            nkb = 4 * g + 4
            heads = [(ph, 0), (ph, 1)]
            hoff = 0 if ph == 'f' else 2
            obanks = [P.psb[i] for i in range(4)]
            blocks = [(hi_, kind, hh, j) for hi_, (kind, hh) in enumerate(heads) for j in range(nkb)]
            st = {}

            def emit_S(n):
                hi_, kind, hh, j = blocks[n]
                ks = slice(j * 128, (j + 1) * 128)
                zone = j >= 4 * g
                jj = j - 4 * g
                ps_s, bs = P.ps()
                kr_ = 96
                S.op('pe', lambda e: e.matmul(ps_s[:, :], lhsT=KmT[hh][0:kr_, ks], rhs=QmT[hh][0:kr_, :], start=True, stop=(not zone)),
                     r=[bK, bQ], w=[bs], signal=(not zone))
                if zone:
                    S.op('pe', lambda e: e.matmul(ps_s[:, :], lhsT=identb[:, :], rhs=maskz[:, jj * 512:(jj + 1) * 512], start=False, stop=True),
                         r=[bW], w=[bs], signal=True)
                st[n] = [ps_s, bs, None, None]

            def emit_exp(n):
                nonlocal pti
                ps_s, bs = st[n][0], st[n][1]
                pt_, bpt = PT[pti % 3], bPT[pti % 3]
                pti += 1
                S.op('act', lambda e: e.activation(out=pt_[:, :], in_=ps_s[:, :], func=AF.Exp), r=[bs], w=[bpt])
                st[n][2], st[n][3] = pt_, bpt

            def emit_PV(n):
                hi_, kind, hh, j = blocks[n]
                zone = j >= 4 * g
                jj = j - 4 * g
                pt_, bpt = st[n][2], st[n][3]
                V = Vf if kind == 'f' else Vm
                for i in range(4):
                    if zone and i < jj:
                        continue
                    last = (j == 4 * g + i)
                    ot, bo = obanks[i]
                    S.op('pe', lambda e, ot=ot, i=i, last=last: e.matmul(ot[:, 0:65], lhsT=pt_[:, i * 128:(i + 1) * 128], rhs=V[:, j, hh, :],
                                                                          start=(j == 0), stop=last),
                         r=[bpt, bK], w=[bo], signal=last)
                    if last:
                        S.op('dve', lambda e, ot=ot: e.reciprocal(out=rc[:, :], in_=ot[:, 64:65]), r=[bo], w=[brc])
                        S.op('act', lambda e, ot=ot, i=i, hoff=hoff: e.activation(out=og[:, i, (hoff + hi_) * 64:(hoff + hi_ + 1) * 64], in_=ot[:, 0:64],
                                                                     func=AF.Identity, scale=rc[:, 0:1]),
                             r=[bo, brc], w=[bog[i]])
                del st[n]

            LOOK = 2
            for n in range(min(LOOK, len(blocks))):
                emit_S(n)
            for n in range(len(blocks)):
                if n + LOOK < len(blocks):
                    emit_S(n + LOOK)
                emit_exp(n)
                emit_PV(n)
            for i in range(4):
                r0 = g * G + i * 128
                ob = Buf('o')
                S.op('sp', lambda e, i=i, r0=r0, hoff=hoff: e.dma_start(out=o_d[r0:r0 + 128, hoff * 64:hoff * 64 + 128], in_=og[:, i, hoff * 64:hoff * 64 + 128]), r=[bog[i]], w=[ob], dma=True)
                P.outs.append(ob)
            if ph == 'm':
                P.group_done(g)
    if standalone:
        return P.finish()
    return None


def odd_mixer_inputs(x_b, j, L, w_in, fgate_b, q_norm_w, w_uq, kv_norm_w, w_ukv):
    h0 = 2 * j
    z64 = np.zeros((1024, 64), np.float32)
    wq = w_in[:, h0 * 64:(h0 + 2) * 64]
    wk = w_in[:, 512 + h0 * 64:512 + (h0 + 2) * 64]
    wv = w_in[:, 1024 + h0 * 64:1024 + (h0 + 2) * 64]
    wfl = w_in[:, 1536 + h0:1536 + h0 + 2]
    wcq = w_in[:, 1544:2056]
    wckv = w_in[:, 2056:2312]
    wkr = w_in[:, 2312:2344]
    perm = np.concatenate([np.arange(16, 32), np.arange(0, 16)])
    P0 = np.concatenate([wq, wk, wv, wfl], axis=1)
    P2 = np.concatenate([wckv, z64, wkr, z64, wkr[:, perm]], axis=1)
    uq = []
    for hh in (h0, h0 + 1):
        blk = w_uq[:, hh * 96:(hh + 1) * 96]
        uq += [blk, np.concatenate([blk[:, :64], blk[:, 64:][:, perm]], axis=1)]
    P3 = np.concatenate(uq, axis=1)
    ukn = [w_ukv[:, hh * 128:hh * 128 + 64] for hh in (h0, h0 + 1)]
    ukv = [w_ukv[:, hh * 128 + 64:hh * 128 + 128] for hh in (h0, h0 + 1)]
    P4 = np.concatenate(ukn + ukv, axis=1)
    w = np.concatenate([panelize(np.ascontiguousarray(m)) for m in (P0, wcq, P2, P3, P4)], axis=0)
    sel = np.zeros((8, 70), np.float32)
    sel[0, 64] = 1; sel[1, 65] = 1; sel[2, 66] = 1
    sel[3, 67:70] = 1
    sel[4, 64:67] = 1
    sel[5, 67] = -1; sel[6, 68] = -1; sel[7, 69] = -1
    half = 16
    freqs = np.power(np.float32(10000.0), -np.arange(half, dtype=np.float32) / half)
    ang = np.arange(L, dtype=np.float32)[None, :] * freqs[:, None]
    cos, sin = np.cos(ang), np.sin(ang)
    rope = np.stack([np.concatenate([cos, cos], 0), np.concatenate([-sin, sin], 0)]).astype(np.float32)
    k = np.arange(128)[:, None]
    q = np.arange(512)[None, :]
    mask = np.zeros((128, 4, 512), np.float32)
    for jj in range(4):
        qi = q // 128
        mask[:, jj, :] = np.where((qi < jj) | ((qi == jj) & (k > q % 128)), NEG, 0.0)
    return dict(x=np.ascontiguousarray(x_b), w=w, qnw=bcast_rows(q_norm_w), kvnw=bcast_rows(kv_norm_w),
                nfb=bcast_rows(fgate_b[h0:h0 + 2]), sel=bcast_rows(sel.reshape(-1)), rope=np.ascontiguousarray(rope),
                mask=np.ascontiguousarray(mask.reshape(128, 2048)), consts=_consts())


WE = 392
WO = 256


def emit_tail_fused(P, kind, TS, L, xsrc, mo_all, idx, bidx, out_ap, xcol=0):
    S = P.S
    ng = TS // G
    ntile = TS // 128
    KC = 12 if kind == 'even' else 8
    npan = 4 if kind == 'even' else 2
    Wm = WE if kind == 'even' else WO
    wo_d = P.dram_in('wo', [npan, 128, 8, 512])
    wg_d = P.dram_in('wg', [6, 128, 8, 512])
    wu_d = P.dram_in('wu', [6, 128, 8, 512])
    wd_d = P.dram_in('wd', [6, 128, 8, 512])
    C = Common(P, ['ln1g', 'ln1b', 'ln2g', 'ln2b'])
    xin = P.sb('xin', [128, 4, D], F32)
    bxin = [Buf(f'xin{t}') for t in range(4)]
    mg = P.sb('mg', [128, 4, 4, Wm], F32)
    bmg = [Buf(f'mg{t}') for t in range(4)]
    mixT = P.sb('mixT', [128, KC, G], BF16)
    bmixT = [Buf(f'mixT{c}') for c in range(KC)]
    if kind == 'even':
        nw_d = P.dram_in('normw', [128, D])
        nw = P.sb('nw', [128, 4, 256], F32)
        bnw = Buf('nw')
        P.load('sp', nw[:, :, :], nw_d.rearrange("p (a b) -> p a b", a=4), bnw)
        sm = P.sb('ssm', [128, 4], F32)
        bss = Buf('ssq')
        epsr = P.sb('epsr', [128, 1], F32)
        bepsr = Buf('epsr')
        S.op('dve', lambda e: e.memset(epsr[:, :], RMS_EPS), w=[bepsr])

    def chunk_src(t, c):
        if kind == 'even':
            if c < 4:
                return mg[:, t, c, 264:392]
            c2 = c - 4
            return mg[:, t, c2 // 2, (c2 % 2) * 128:(c2 % 2) * 128 + 128]
        if c < 4:
            return mg[:, t, c, 0:128]
        return mg[:, t, c - 4, 128:256]

    def issue_gathers(g, tiles=(0, 1, 2, 3)):
        for t in tiles:
            tile_no = g * 4 + t
            P.gather(xin[:, t, :], xsrc, idx[:, xcol + tile_no:xcol + tile_no + 1], bxin[t], r=[bidx])
            for i in range(4):
                k = 16 * (1 + i) + tile_no
                P.gather(mg[:, t, i, :], mo_all, idx[:, k:k + 1], bmg[t], r=[bidx])

    issue_gathers(0)
    for g in range(ng):
        for t in range(4):
            if kind == 'even':
                S.op('dve', lambda e, t=t: e.tensor_reduce(out=sm[:, 0:1], in_=mg[:, t, :, 256], axis=AX.X, op=ALU.add), r=[bmg[t]], w=[bss])
                S.op('act', lambda e: e.activation(out=sm[:, 1:2], in_=sm[:, 0:1], func=AF.Sqrt, bias=epsr[:, 0:1], scale=1.0 / D),
                     r=[bss, bepsr], w=[bss])
                S.op('dve', lambda e: e.reciprocal(out=sm[:, 2:3], in_=sm[:, 1:2]), r=[bss], w=[bss])
                for i in range(4):
                    S.op('act', lambda e, t=t, i=i: e.activation(out=mg[:, t, i, 0:256], in_=mg[:, t, i, 0:256], func=AF.Identity, scale=sm[:, 2:3]),
                         r=[bss, bmg[t]], w=[bmg[t]])
                    S.op('dve', lambda e, t=t, i=i: e.tensor_tensor(out=mg[:, t, i, 0:256], in0=mg[:, t, i, 0:256], in1=nw[:, i, :], op=ALU.mult),
                         r=[bmg[t], bnw], w=[bmg[t]])
        for c in range(KC):
            pt, pb = P.ps()
            for t in range(4):
                S.op('pe', lambda e, t=t, c=c, pt=pt: e.transpose(pt[:, t * 128:(t + 1) * 128], chunk_src(t, c), P.ident[:, :]),
                     r=[bmg[t], P.b_ident], w=[pb], signal=(t == 3))
            if c % 2 == 0:
                S.op('act', lambda e, c=c, pt=pt: e.activation(out=mixT[:, c, :], in_=pt[:, :], func=AF.Copy), r=[pb], w=[bmixT[c]])
            else:
                S.op('dve', lambda e, c=c, pt=pt: e.tensor_copy(out=mixT[:, c, :], in_=pt[:, :]), r=[pb], w=[bmixT[c]])
        nkg = npan // 2
        for hf in range(2):
            wos = [P.load_w(wo_d, hf * nkg + kg) for kg in range(nkg)]
            for t in range(4):
                po, bo = P.ps()
                for k in range(KC):
                    wt, bw = wos[k // 8]
                    S.op('pe', lambda e, k=k, t=t, po=po, wt=wt: e.matmul(po[:, :], lhsT=mixT[:, k, t * 128:(t + 1) * 128], rhs=wt[:, k % 8, :],
                                                                         start=(k == 0), stop=(k == KC - 1)),
                         r=[bw, bmixT[k]], w=[bo], signal=(k == KC - 1))
                S.op('dve', lambda e, t=t, hf=hf, po=po: e.scalar_tensor_tensor(out=C.r[:, t, hf * 512:(hf + 1) * 512], in0=xin[:, t, hf * 512:(hf + 1) * 512],
                                                                               scalar=ALPHA, op0=ALU.mult, in1=po[:, :], op1=ALU.add),
                     r=[bo, bxin[t]], w=[C.br[t]])
        emit_tail(P, C, wg_d, wu_d, wd_d, [out_ap[g * G + t * 128: g * G + (t + 1) * 128, :] for t in range(4)],
                  mid_hook=(lambda cp, g=g: issue_gathers(g + 1, (cp,)) if cp < 4 else None) if g + 1 < ng else None)
        P.group_done(g)


def build_fused(L, nlayers=4, dbg=()):
    TS = L // 4
    P = Prog(fused=True)
    S = P.S
    nc = P.nc
    U32 = mybir.dt.uint32
    x_in = P.dram_in('x', [L, D])
    idx_d = P.dram_in('gidx', [128, 96], U32)
    out_d = P.dram_out('out', [TS, D])
    mo_e = nc.dram_tensor('mo_e', [L, WE], F32).ap()
    mo_e_all = nc.dram_tensor('mo_e_all', [4 * L, WE], F32).ap()
    mo_o = nc.dram_tensor('mo_o', [L, WO], F32).ap()
    mo_o_all = nc.dram_tensor('mo_o_all', [4 * L, WO], F32).ap()
    xg = nc.dram_tensor('xg', [TS, D], F32).ap()
    xg_all = nc.dram_tensor('xg_all', [L, D], F32).ap()
    groups = [[0, 1, 2, 3], [4, 5, 6, 7]]
    wscs = [{nm: nc.dram_tensor(f'wsc{k}_' + nm, [n, 128, 8, 512], BF16).ap() for nm, n in (('wo', 4), ('wg', 6), ('wu', 6), ('wd', 6))}
            for k in range(2)]
    cast_lists = []
    for l_ in range(nlayers):
        cl = []
        for nm, n in (('wo', 4 if l_ % 2 == 0 else 2), ('wg', 6), ('wu', 6), ('wd', 6)):
            src = nc.dram_tensor(f'{nm}_t{l_}', [n, 128, 8, 512], F32, kind="ExternalInput").ap()
            cl += [(wscs[l_ % 2][nm][i], src[i]) for i in range(n)]
        cast_lists.append(cl)

    def issue_casts(l_, k):
        if l_ >= nlayers:
            return
        cl = cast_lists[l_]
        for _ in range(k):
            if cl:
                dst, src = cl.pop(0)
                S.op('pool', lambda e, dst=dst, src=src: e.dma_start(out=dst, in_=src), dma=True)
    idx = P.sb('gidx', [128, 96], U32)
    bidx = Buf('gidx')
    P.load('sp', idx[:, :], idx_d, bidx)
    for l in range(nlayers):
        xsrc = x_in if l == 0 else xg_all
        P.xmap = None if l == 0 else (lambda r0: ((r0 % TS) // 128) * 512 + (r0 // TS) * 128)
        xcol = 0 if l == 0 else 80
        wsc = wscs[l % 2]
        P.sfx = f'_m{l}'
        if l % 2 == 0:
            P.io = {'x': xsrc, 'yg': mo_e[:, 0:257], 'ypool': mo_e[:, 264:392]}
            def hook_e(g):
                P.cc_async("AllGather", groups, mo_e[g * 512:(g + 1) * 512, :], mo_e_all[g * 2048:(g + 1) * 2048, :], r=P.outs)
                P.outs = []
                issue_casts(l, 2)
            P.hook = hook_e
            P.begin_phase(2)
            build_even_mixer(L, P)
            P.hook = None
            issue_casts(l, 100)
            P.end_phase()
            mo_all = mo_e_all
        else:
            P.io = {'x': xsrc, 'o': mo_o}
            def hook_o(g):
                P.cc_async("AllGather", groups, mo_o[g * 512:(g + 1) * 512, :], mo_o_all[g * 2048:(g + 1) * 2048, :], r=P.outs)
                P.outs = []
                issue_casts(l, 2)
            P.hook = hook_o
            P.begin_phase(5, ps_banks=(4, 5, 6, 7))
            build_odd_mixer(L, P)
            P.hook = None
            issue_casts(l, 100)
            P.end_phase()
            mo_all = mo_o_all
        P.outs = []
        P.sfx = f'_t{l}'
        P.io = dict(wsc)
        P.w_eng = 'sp'
        def hook_t(g, l=l):
            for c in range(4):
                k = g * 4 + c
                P.cc_async("AllGather", groups, xg[k * 128:(k + 1) * 128, :], xg_all[k * 512:(k + 1) * 512, :], r=P.outs)
            P.outs = []
            issue_casts(l + 1, 6)
        P.hook = hook_t if l < nlayers - 1 else None
        P.begin_phase(8)
        if 'notail' in dbg:
            tt = P.sb('tt', [128, D], F32)
            btt = Buf('tt')
            P.load('sp', tt[:, :], xsrc[0:128, :], btt)
            ob = Buf('o')
            S.op('sp', lambda e: e.dma_start(out=out_d[0:128, :], in_=tt[:, :]), r=[btt], w=[ob], dma=True)
            P.outs.append(ob)
        else:
            emit_tail_fused(P, 'even' if l % 2 == 0 else 'odd', TS, L, xsrc, mo_all, idx, bidx, out_d if l == nlayers - 1 else xg, xcol)
        if l == nlayers - 1:
            S.wait_for('sp', P.outs)
        P.hook = None
        P.w_eng = 'pool'
        P.io = {}
        P.end_phase()
    S.emit()
    return nc


def fused_inputs(inp, c, L, nlayers=4):
    TS = L // 4
    b, j = c // 4, c % 4
    x = inp['x']
    d = {'x': np.ascontiguousarray(x[b], dtype=np.float32)}
    p = np.arange(128)[:, None]
    tile = np.arange(16)[None, :]
    tok = j * TS + tile * 128 + p
    mo = [(tok // 512) * 2048 + i * 512 + (tok % 512) for i in range(4)]
    xg = tile * 512 + j * 128 + p + 0 * tok
    d['gidx'] = np.ascontiguousarray(np.concatenate([tok] + mo + [xg], axis=1).astype(np.uint32))
    for l in range(nlayers):
        i = l // 2
        if l % 2 == 0:
            m = even_mixer_inputs(x[b], j, inp['even_w_in'][i], inp['pool_w'][i], inp['pool_scale'][i], inp['conv_w'][i],
                                  inp['conv_b'][i], inp['dt_bias'][i], inp['a_log'][i], inp['d_skip'][i])
        else:
            m = odd_mixer_inputs(x[b], j, L, inp['odd_w_in'][i], inp['fgate_b'][i], inp['q_norm_w'][i], inp['w_uq'][i],
                                 inp['kv_norm_w'][i], inp['w_ukv'][i])
        m.pop('x')
        for k, v in m.items():
            d[f'{k}_m{l}'] = v
    return d


def fused_shared_inputs(inp, nlayers=4):
    sh = {}
    eye = np.eye(128, dtype=np.float32)
    for l in range(nlayers):
        i = l // 2
        t = dict(wg=panelize(inp['ffn_w_gate'][l]), wu=panelize(inp['ffn_w_up'][l]), wd=panelize(inp['ffn_w_down'][l]),
                 ident=eye, ln1g=bcast_rows(inp['ln_mix_g'][l]), ln1b=bcast_rows(inp['ln_mix_b'][l]),
                 ln2g=bcast_rows(inp['ln_ffn_g'][l]), ln2b=bcast_rows(inp['ln_ffn_b'][l]))
        if l % 2 == 0:
            t['wo'] = panelize(inp['even_w_out'][i])
            t['normw'] = bcast_rows(inp['ssm_norm_w'][i])
        else:
            t['wo'] = panelize(inp['odd_w_out'][i])
        for k, v in t.items():
            sh[f'{k}_t{l}'] = v
    return sh


def kernel(**inp):
    inp = {k: np.asarray(v) for k, v in inp.items()}
    B, L, _ = inp['x'].shape
    TS = L // 4
    nc = build_fused(L)
    sh = fused_shared_inputs(inp)
    ins = [dict(sh, **fused_inputs(inp, c, L)) for c in range(8)]
    res = run_bass_kernel_spmd(nc, ins, core_ids=list(range(8))).results
    out = np.empty((B, L, D), np.float32)
    for c in range(8):
        b, j = c // 4, c % 4
        out[b, j * TS:(j + 1) * TS] = res[c]['out']
    return out
```
